# Optimizing a Trainium2 kernel written in Bass

```python
import math
import jax, jax.numpy as jnp
from jax import lax
import numpy as np

D_MODEL = 2048
BATCH = 4
SEQ = 8192
DEPTH = 1
DEC_BATCH = 32
DEC_SEQ = 64
PAST_LEN = 4096

CHUNK = 64
PLE_DIM = 256
MIX_A = D_MODEL // 2
MIX_B = D_MODEL // 2
ML_HEADS = 4
ML_HEAD_DIM = MIX_A // ML_HEADS
SB_HEADS = 8
SB_HEAD_DIM = MIX_B // SB_HEADS
CONV_W = 4
D_FF = 4 * D_MODEL
SB_BLOCK = 128
EPS = 1e-6

OFF_QK = 2 * MIX_A
OFF_V = OFF_QK + MIX_A
OFF_O = OFF_V + MIX_A
OFF_IF = OFF_O + 2 * ML_HEADS
OFF_SB = OFF_IF + 3 * MIX_B
D_IN = OFF_SB + 2 * D_MODEL
SPLITS = (OFF_QK, OFF_V, OFF_O, OFF_IF, OFF_SB)

kernel_name = 'hybrid_mlstm_stickbreaking_stream_step'


def rmsnorm(x, g):
    xf = x.astype(jnp.float32)
    y = xf * lax.rsqrt(jnp.mean(xf * xf, axis=-1, keepdims=True) + EPS)
    return (y * g.astype(jnp.float32)).astype(x.dtype)


def causal_conv(u, buf, w, b):
    ext = jnp.concatenate([buf.astype(u.dtype), u], axis=1)
    out = lax.conv_general_dilated(ext, w.astype(u.dtype)[:, None, :], (1,), 'VALID',
                                   dimension_numbers=('NWC', 'WIO', 'NWC'),
                                   feature_group_count=u.shape[-1])
    return out + b.astype(u.dtype), ext[:, -(CONV_W - 1):]


def mlstm_chunk(carry, xs):
    C, n, m = carry
    q, k, v, ig, lf = xs
    L = q.shape[1]
    b = jnp.cumsum(lf, axis=1)
    causal = jnp.tril(jnp.ones((L, L), dtype=bool))
    dmat = b[:, :, None, :] - b[:, None, :, :] + ig[:, None, :, :]
    dmat = jnp.where(causal[None, :, :, None], dmat, -jnp.inf)
    inter = b + m[:, None, :]
    m_t = jnp.maximum(inter, jnp.max(dmat, axis=2))
    wts = jnp.exp(dmat - m_t[:, :, None, :])
    s_qk = jnp.einsum('bthd,bshd->btsh', q, k) * wts
    dec = jnp.exp(inter - m_t)
    num = jnp.einsum('btsh,bshd->bthd', s_qk, v) + dec[..., None] * jnp.einsum('bthk,bhkv->bthv', q, C)
    den = jnp.sum(s_qk, axis=2) + dec * jnp.einsum('bthk,bhk->bth', q, n)
    h = num / jnp.maximum(jnp.abs(den), jnp.exp(-m_t))[..., None]
    m_last = m_t[:, -1]
    b_last = b[:, -1]
    wk = jnp.exp(b_last[:, None, :] - b + ig - m_last[:, None, :])
    carry_dec = jnp.exp(b_last + m - m_last)
    C_new = carry_dec[..., None, None] * C + jnp.einsum('bsh,bshk,bshv->bhkv', wk, k, v)
    n_new = carry_dec[..., None] * n + jnp.einsum('bsh,bshk->bhk', wk, k)
    return (C_new, n_new, m_last), h


def mlstm_seq(q, k, v, ig, lf, C0, n0, m0):
    B, L, H, d = q.shape
    cl = min(L, CHUNK)
    nc = L // cl
    to_chunks = lambda t: t.astype(jnp.float32).reshape((B, nc, cl) + t.shape[2:]).swapaxes(0, 1)
    xs = (to_chunks(q), to_chunks(k), to_chunks(v), to_chunks(ig), to_chunks(lf))
    init = (C0.astype(jnp.float32), n0.astype(jnp.float32), m0.astype(jnp.float32))
    (C, n, m), h = lax.scan(mlstm_chunk, init, xs)
    return h.swapaxes(0, 1).reshape(B, L, H, d), C, n, m


def sb_block(q, q_pos, k, v, k_pos):
    d = q.shape[-1]
    z = jnp.einsum('bthd,bshd->bhts', q.astype(jnp.float32), k.astype(jnp.float32)) * (d ** -0.5)
    mask = (k_pos[None, :] < q_pos[:, None])[None, None]
    log_beta = jax.nn.log_sigmoid(z)
    log_keep = jnp.where(mask, log_beta - z, 0.0)
    later = lax.cumsum(log_keep, axis=3, reverse=True) - log_keep
    a = jnp.where(mask, jnp.exp(log_beta + later), 0.0)
    return jnp.einsum('bhts,bshd->bthd', a, v.astype(jnp.float32))


def stick_breaking(q, k, v, past):
    B, L, H, d = q.shape
    qb = min(L, SB_BLOCK)
    nb = L // qb
    k_pos = jnp.arange(k.shape[1])
    q_pos = (past + jnp.arange(L)).reshape(nb, qb)
    q_blocks = q.reshape(B, nb, qb, H, d).swapaxes(0, 1)
    out = lax.map(lambda a: sb_block(a[0], a[1], k, v, k_pos), (q_blocks, q_pos))
    return out.swapaxes(0, 1).reshape(B, L, H, d)


def hybrid_layer(x, p, conv_buf, C0, n0, m0, k_past, v_past,
                 w_in, b_if, conv_w, conv_b, ml_norm, w_br_a, w_br_b, w_out,
                 g_pre_mix, g_post_mix, g_pre_mlp, g_post_mlp, w_up, w_down,
                 g_pre_ple, g_post_ple, w_ple, w_ple_gate):
    B, L, _ = x.shape
    past = k_past.shape[1]
    h = rmsnorm(x, g_pre_mix)
    proj = h @ w_in.astype(x.dtype)
    qk_pre, v_m, o_pre, if_pre, qkv_s, gate_pre = jnp.split(proj, SPLITS, axis=-1)
    qk, conv_new = causal_conv(qk_pre, conv_buf, conv_w, conv_b)
    qk = jax.nn.silu(qk)
    q_m = qk[..., :MIX_A].reshape(B, L, ML_HEADS, ML_HEAD_DIM)
    k_m = qk[..., MIX_A:].reshape(B, L, ML_HEADS, ML_HEAD_DIM) * (ML_HEAD_DIM ** -0.5)
    v_m = v_m.reshape(B, L, ML_HEADS, ML_HEAD_DIM)
    ifp = if_pre.astype(jnp.float32) + b_if.astype(jnp.float32)
    ig = ifp[..., :ML_HEADS]
    lf = jax.nn.log_sigmoid(ifp[..., ML_HEADS:])
    h_t, C, n, m = mlstm_seq(q_m, k_m, v_m, ig, lf, C0, n0, m0)
    h_m = jax.nn.sigmoid(o_pre).reshape(B, L, ML_HEADS, ML_HEAD_DIM) * h_t.astype(x.dtype)
    h_m = rmsnorm(h_m, ml_norm.reshape(ML_HEADS, ML_HEAD_DIM)).reshape(B, L, MIX_A)
    q_s, k_s, v_s = [t.reshape(B, L, SB_HEADS, SB_HEAD_DIM) for t in jnp.split(qkv_s, 3, axis=-1)]
    k_all = jnp.concatenate([k_past.astype(k_s.dtype), k_s], axis=1)
    v_all = jnp.concatenate([v_past.astype(v_s.dtype), v_s], axis=1)
    h_s = stick_breaking(q_s, k_all, v_all, past).reshape(B, L, MIX_B).astype(x.dtype)
    g_a, g_b = jnp.split(jax.nn.sigmoid(gate_pre), 2, axis=-1)
    u = g_a * (h_m @ w_br_a.astype(x.dtype)) + g_b * (h_s @ w_br_b.astype(x.dtype))
    x = x + rmsnorm(u @ w_out.astype(x.dtype), g_post_mix)
    h2 = rmsnorm(x, g_pre_mlp)
    f = jnp.square(jax.nn.relu(h2 @ w_up.astype(x.dtype))) @ w_down.astype(x.dtype)
    x = x + rmsnorm(f, g_post_mlp)
    gate = jax.nn.sigmoid(rmsnorm(x, g_pre_ple) @ w_ple_gate.astype(x.dtype))
    ple = (p.astype(x.dtype) @ w_ple.astype(x.dtype)) * gate
    x = x + rmsnorm(ple, g_post_ple)
    return x, (k_s, v_s, conv_new, C, n, m)


def setup_inputs(seed: int = 0) -> dict:
    key = jax.random.key(seed)
    ks = iter(jax.random.split(key, 40))
    nrm = lambda shape, scale: jax.random.normal(next(ks), shape, jnp.float32) * scale
    gain = lambda width: 1.0 + nrm((DEPTH, width), 0.02)
    b_if = jnp.concatenate([nrm((DEPTH, ML_HEADS), 0.1),
                            jnp.linspace(3.0, 6.0, ML_HEADS)[None, :] + nrm((DEPTH, ML_HEADS), 0.01)], axis=-1)
    return {
        'x_prompt': nrm((BATCH, SEQ, D_MODEL), 1.0),
        'x_sample': nrm((DEC_BATCH, DEC_SEQ, D_MODEL), 1.0),
        'p_prompt': nrm((DEPTH, BATCH, SEQ, PLE_DIM), 1.0),
        'p_sample': nrm((DEPTH, DEC_BATCH, DEC_SEQ, PLE_DIM), 1.0),
        'cache_sb_k': nrm((DEPTH, DEC_BATCH, PAST_LEN, SB_HEADS, SB_HEAD_DIM), 1.0),
        'cache_sb_v': nrm((DEPTH, DEC_BATCH, PAST_LEN, SB_HEADS, SB_HEAD_DIM), 1.0),
        'state_conv': nrm((DEPTH, DEC_BATCH, CONV_W - 1, 2 * MIX_A), 1.0),
        'state_mlstm_C': nrm((DEPTH, DEC_BATCH, ML_HEADS, ML_HEAD_DIM, ML_HEAD_DIM), 0.3),
        'state_mlstm_n': nrm((DEPTH, DEC_BATCH, ML_HEADS, ML_HEAD_DIM), 0.3),
        'state_mlstm_m': nrm((DEPTH, DEC_BATCH, ML_HEADS), 1.0),
        'w_in': nrm((DEPTH, D_MODEL, D_IN), D_MODEL ** -0.5),
        'b_if': b_if,
        'conv_w': nrm((DEPTH, CONV_W, 2 * MIX_A), CONV_W ** -0.5),
        'conv_b': nrm((DEPTH, 2 * MIX_A), 0.01),
        'ml_norm': gain(MIX_A),
        'w_br_a': nrm((DEPTH, MIX_A, D_MODEL), MIX_A ** -0.5),
        'w_br_b': nrm((DEPTH, MIX_B, D_MODEL), MIX_B ** -0.5),
        'w_out': nrm((DEPTH, D_MODEL, D_MODEL), D_MODEL ** -0.5),
        'g_pre_mix': gain(D_MODEL),
        'g_post_mix': gain(D_MODEL),
        'g_pre_mlp': gain(D_MODEL),
        'g_post_mlp': gain(D_MODEL),
        'w_up': nrm((DEPTH, D_MODEL, D_FF), D_MODEL ** -0.5),
        'w_down': nrm((DEPTH, D_FF, D_MODEL), D_FF ** -0.5),
        'g_pre_ple': gain(D_MODEL),
        'g_post_ple': gain(D_MODEL),
        'w_ple': nrm((DEPTH, PLE_DIM, D_MODEL), PLE_DIM ** -0.5),
        'w_ple_gate': nrm((DEPTH, D_MODEL, D_MODEL), D_MODEL ** -0.5),
    }


def reference(x_prompt, x_sample, p_prompt, p_sample, cache_sb_k, cache_sb_v, state_conv,
              state_mlstm_C, state_mlstm_n, state_mlstm_m, w_in, b_if, conv_w, conv_b, ml_norm,
              w_br_a, w_br_b, w_out, g_pre_mix, g_post_mix, g_pre_mlp, g_post_mlp, w_up, w_down,
              g_pre_ple, g_post_ple, w_ple, w_ple_gate):
    bp = x_prompt.shape[0]
    f32 = jnp.float32
    yp, ys = x_prompt, x_sample
    st_p, st_s = [], []
    for i in range(DEPTH):
        lw = (w_in[i], b_if[i], conv_w[i], conv_b[i], ml_norm[i], w_br_a[i], w_br_b[i], w_out[i],
              g_pre_mix[i], g_post_mix[i], g_pre_mlp[i], g_post_mlp[i], w_up[i], w_down[i],
              g_pre_ple[i], g_post_ple[i], w_ple[i], w_ple_gate[i])
        empty_kv = jnp.zeros((bp, 0, SB_HEADS, SB_HEAD_DIM), x_prompt.dtype)
        yp, sp = hybrid_layer(yp, p_prompt[i],
                              jnp.zeros((bp, CONV_W - 1, 2 * MIX_A), x_prompt.dtype),
                              jnp.zeros((bp, ML_HEADS, ML_HEAD_DIM, ML_HEAD_DIM), f32),
                              jnp.zeros((bp, ML_HEADS, ML_HEAD_DIM), f32),
                              jnp.zeros((bp, ML_HEADS), f32),
                              empty_kv, empty_kv, *lw)
        ys, ss = hybrid_layer(ys, p_sample[i], state_conv[i], state_mlstm_C[i], state_mlstm_n[i],
                              state_mlstm_m[i], cache_sb_k[i], cache_sb_v[i], *lw)
        st_p.append(sp)
        st_s.append(ss)
    stk = lambda sts, j: jnp.stack([s[j] for s in sts])
    return (yp, ys,
            stk(st_p, 0), stk(st_p, 1), stk(st_p, 2), stk(st_p, 3), stk(st_p, 4), stk(st_p, 5),
            stk(st_s, 0), stk(st_s, 1), stk(st_s, 2), stk(st_s, 3), stk(st_s, 4), stk(st_s, 5))
```

```python
import contextlib
import numpy as np
import concourse.bass as bass
import concourse.mybir as mybir
from concourse.bass_utils import run_bass_kernel_spmd

F32 = mybir.dt.float32
BF16 = mybir.dt.bfloat16
ALU = mybir.AluOpType
AF = mybir.ActivationFunctionType
AX = mybir.AxisListType

ENGS = ("pe", "act", "dve", "pool", "sp")
NDMA_SEM = 6
SAME_ENGINE_SYNC = True


class Op:
    __slots__ = ("eng", "fn", "deps", "dma", "marked", "idx", "sem", "cnt", "ring_wait", "pos")

    def __init__(self, eng, fn, dma):
        self.eng = eng
        self.fn = fn
        self.dma = dma
        self.deps = []
        self.marked = False
        self.idx = 0
        self.sem = None
        self.cnt = 0
        self.ring_wait = None


def _os_dbg():
    import os
    return bool(os.environ.get("K_DBG"))


class _Rec:
    def __init__(self):
        self.call = None

    def __getattr__(self, name):
        def f(*a, **k):
            assert self.call is None
            self.call = (name, a, k)
            return self
        return f


class Sched:
    def __init__(self, nc):
        self.nc = nc
        self.ops = {e: [] for e in ENGS}
        self.bufs = {}
        self.dmas = {e: [] for e in ENGS}
        self.all_dmas = []
        self.npos = 0

    def _conf(self, key):
        root, rest = key[0], tuple(key[1:])
        tab = self.bufs.setdefault(root, {})
        out = []
        for r2, ent in tab.items():
            n = min(len(rest), len(r2))
            if rest[:n] == r2[:n]:
                out.append(ent)
        return tab, rest, out

    def add(self, eng, fn, reads=(), writes=(), dma=False):
        rec = _Rec()
        fn(rec)
        op = Op(eng, rec.call, dma)
        deps = {}
        rk = [k if isinstance(k, tuple) else (k,) for k in reads]
        wk = [k if isinstance(k, tuple) else (k,) for k in writes]
        pk_ = [("ps", k[1]) for k in rk + wk if k[0] == "ps"]
        if pk_:
            rk = [k for k in rk if k[0] != "ps"]
            wk = [k for k in wk if k[0] != "ps"] + list(dict.fromkeys(pk_))
        for k in rk:
            tab, rest, ents = self._conf(k)
            for ent in ents:
                if ent[0] is not None:
                    deps[id(ent[0])] = ent[0]
        for k in wk:
            tab, rest, ents = self._conf(k)
            for ent in ents:
                if ent[0] is not None:
                    deps[id(ent[0])] = ent[0]
                for r in ent[1]:
                    deps[id(r)] = r
        for k in rk:
            tab, rest, ents = self._conf(k)
            ent = tab.get(rest)
            if ent is None:
                ent = [None, []]
                tab[rest] = ent
            ent[1].append(op)
        for k in wk:
            tab, rest, ents = self._conf(k)
            for r2 in [r2 for r2 in tab if r2 != rest and r2[:len(rest)] == rest]:
                del tab[r2]
            tab[rest] = [op, []]
        deps.pop(id(op), None)
        self._finish_add(op, list(deps.values()))
        return op

    def _finish_add(self, op, deps):
        eng = op.eng
        keep = []
        latest = {}
        rest = []
        for d in deps:
            if d.dma:
                rest.append(d)
            elif d.eng not in latest or latest[d.eng].pos < d.pos:
                latest[d.eng] = d
        deps = rest + list(latest.values())
        op.pos = self.npos
        self.npos += 1
        for d in deps:
            if d.dma:
                keep.append(d)
            else:
                if d.eng == eng and (eng == "pe" or not SAME_ENGINE_SYNC):
                    continue
                d.marked = True
                keep.append(d)
        op.deps = keep
        self.ops[eng].append(op)
        if op.dma:
            lst = self.dmas[eng]
            i = len(lst)
            if i >= NDMA_SEM:
                op.ring_wait = lst[i - NDMA_SEM]
            lst.append(op)
            self.all_dmas.append(op)

    def barrier(self):
        last = []
        for e in ENGS:
            for o in reversed(self.ops[e]):
                if not o.dma:
                    last.append(o)
                    break
        last += self.all_dmas
        self.all_dmas = []
        for e in ENGS:
            op = Op(e, None, False)
            self._finish_add(op, [d for d in last if d.dma or d.eng != e])
        self.bufs = {}

    def emit(self):
        nc = self.nc
        with contextlib.ExitStack() as st:
            esem = {e: st.enter_context(nc.semaphore("s_" + e)) for e in ENGS}
            dsem = {}
            for e in ENGS:
                if self.dmas[e]:
                    dsem[e] = [st.enter_context(nc.semaphore("d_%s%d" % (e, i))) for i in range(NDMA_SEM)]
            for e in ENGS:
                n = 0
                for op in self.ops[e]:
                    if op.dma:
                        continue
                    if op.marked:
                        n += 1
                        op.idx = n
                if _os_dbg():
                    print("SCHED", e, "ops", len(self.ops[e]), "marked", n, "dmas", len(self.dmas[e]), flush=True)
                for i, op in enumerate(self.dmas[e]):
                    op.sem = dsem[e][i % NDMA_SEM]
                    op.cnt = 16 * (i // NDMA_SEM + 1)
            fin = Op("sp", None, False)
            fin.deps = [d for e in ENGS for d in self.dmas[e][-NDMA_SEM:]]
            self.ops["sp"].append(fin)
            st.enter_context(nc.allow_non_contiguous_dma(reason="small strided state/layout DMAs"))
            block = st.enter_context(nc.Block())

            def run(e, eng):
                waited = {}
                for op in self.ops[e]:
                    need = {}
                    ds = list(op.deps)
                    if op.ring_wait is not None:
                        ds.append(op.ring_wait)
                    for d in ds:
                        if d.dma:
                            key = ("d", id(d.sem))
                            sem, val = d.sem, d.cnt
                        else:
                            key = ("e", d.eng)
                            sem, val = esem[d.eng], d.idx
                        if waited.get(key, 0) >= val:
                            continue
                        if key not in need or need[key][1] < val:
                            need[key] = (sem, val)
                    for key, (sem, val) in need.items():
                        eng.wait_ge(sem, val)
                        waited[key] = val
                    if op.fn is None:
                        if op.marked:
                            eng.nop().then_inc(esem[e], 1)
                        continue
                    nm, a_, k_ = op.fn
                    ins = getattr(eng, nm)(*a_, **k_)
                    if op.dma:
                        ins.then_inc(op.sem, 16)
                    elif op.marked:
                        ins.then_inc(esem[e], 1)

            @block.tensor
            def _(eng):
                run("pe", eng)

            @block.scalar
            def _(eng):
                run("act", eng)

            @block.vector
            def _(eng):
                run("dve", eng)

            @block.gpsimd
            def _(eng):
                run("pool", eng)

            @block.sync
            def _(eng):
                run("sp", eng)


D = 2048
KC = 16
MIXA = 1024
NSEQ = 4
LS = 64
EPS = 1e-6
C_QK, C_VM, C_O, C_IF, C_SQ, C_SK, C_SV, C_GA, C_GB = 0, 2048, 3072, 4096, 4104, 5128, 6152, 7176, 9224
D_IN = 11272
SLOT = 8192
NEG = -30000.0

PC_GMIX, PC_GMLP, PC_GPLE, PC_POMIX, PC_POMLP, PC_POPLE = 0, 16, 32, 48, 64, 80
PC_CW, PC_CB, PC_BIF, PC_FLAG, PC_NEGB, PC_ZERO = 96, 160, 176, 177, 178, 179
NPC = 192


def cfg_full():
    return dict(HALF=4096, PAST=4096, DFF=8192)


def build(cfg):
    HALF, PAST, DFF = cfg["HALF"], cfg["PAST"], cfg["DFF"]
    NT = HALF // 512
    TP = 2 * HALF
    T_ALL = TP + NSEQ * LS
    TQ = HALF + NSEQ * LS
    NCc = HALF // 64
    NCp = 2 * NCc
    NCH = NCp + NSEQ
    KCACHE = PAST + LS
    FH = DFF // 2
    NFC = FH // 128

    nc = bass.Bass("TRN2", target_bir_lowering=False)

    def din(name, shape, dt=F32):
        return nc.dram_tensor(name, list(shape), dt, kind="ExternalInput").ap()

    def dout(name, shape, dt=F32):
        return nc.dram_tensor(name, list(shape), dt, kind="ExternalOutput").ap()

    def dscr(name, shape, dt=BF16):
        return nc.dram_tensor(name, list(shape), dt, kind="Internal").ap()

    x_ctx = din("x_ctx", [HALF, D]); x_main = din("x_main", [HALF, D]); x_smp = din("x_smp", [NSEQ * LS, D])
    p_main = din("p_main", [HALF, 256]); p_smp = din("p_smp", [NSEQ * LS, 256])
    cache_k = din("cache_k", [NSEQ, PAST, 1024]); cache_v = din("cache_v", [NSEQ, PAST, 1024])
    st_conv = din("st_conv", [NSEQ * 3, D]); st_C = din("st_C", [NSEQ, 4, 256, 256])
    st_n = din("st_n", [NSEQ, 4, 256]); st_m = din("st_m", [1, NSEQ * 4])
    w_in = din("w_in", [D, D_IN]); w_bra = din("w_bra", [MIXA, D]); w_brb = din("w_brb", [MIXA, D])
    w_out = din("w_out", [D, D]); w_up = din("w_up", [D, DFF]); w_down = din("w_down", [DFF, D])
    w_ple = din("w_ple", [256, D]); w_pg = din("w_pg", [D, D])
    pc_d = din("pc", [128, NPC]); mlg_d = din("mlg", [1, MIXA]); bif_d = din("bifrow", [1, 8])
    ident_d = din("ident", [128, 128]); ucm_d = din("ucm", [128, 128]); ones_d = din("ones", [128, 128])
    sbmask_d = din("sbmask", [128, 4 * 512]); tri_d = din("tri", [64, 64]); sel_d = din("sel63", [64, 128])
    i4_d = din("i4", [64, 256]); negm_d = din("negm", [64, 256]); mt4_d = din("mt4", [64, 256])

    y_main = dout("y_main", [HALF, D]); y_smp = dout("y_smp", [NSEQ * LS, D])
    ok_main = dout("k_main", [HALF, 1024]); ov_main = dout("v_main", [HALF, 1024])
    ok_smp = dout("k_smp", [NSEQ * LS, 1024]); ov_smp = dout("v_smp", [NSEQ * LS, 1024])
    oconv_main = dout("conv_main", [3, D]); oC_main = dout("C_main", [4, 256, 256])
    on_main = dout("n_main", [4, 256]); om_main = dout("m_main", [1, 4])
    oconv_smp = dout("conv_smp", [NSEQ * 3, D]); oC_smp = dout("C_smp", [NSEQ, 4, 256, 256])
    on_smp = dout("n_smp", [NSEQ, 4, 256]); om_smp = dout("m_smp", [1, NSEQ * 4])

    wb_in = dscr("wb_in", [D, D_IN]); wb_bra = dscr("wb_bra", [MIXA, D]); wb_brb = dscr("wb_brb", [MIXA, D])
    wb_out = dscr("wb_out", [D, D]); wb_up = dscr("wb_up", [D, DFF]); wb_down = dscr("wb_down", [DFF, D])
    wb_ple = dscr("wb_ple", [256, D]); wb_pg = dscr("wb_pg", [D, D])
    QKT = dscr("QKT", [16, 128, T_ALL]); VM = dscr("VM", [T_ALL, 1024]); OSG = dscr("OSG", [T_ALL, 1024])
    IFT = dscr("IFT", [T_ALL, 8], F32); HM = dscr("HM", [T_ALL, 1024])
    QTS = dscr("QTS", [8, 128, TQ]); KTP = dscr("KTP", [8, 128, TP]); VP = dscr("VP", [TP, 1024])
    KTC = dscr("KTC", [NSEQ, 8, 128, KCACHE]); VC = dscr("VC", [NSEQ, KCACHE, 1024])
    HST = dscr("HST", [8, 128, TQ])

    S = Sched(nc)
    st = contextlib.ExitStack()
    with st:
        NBIG = 47 * 1024
        big = st.enter_context(nc.sbuf_tensor("big", [128, NBIG], F32))
        pst = [st.enter_context(nc.psum_tensor("ps%d" % i, [128, 512], F32)) for i in range(8)]
        off = [0]
        uid = [0]

        class T:
            pass

        def tile(n, dt=F32, name=None):
            w = n if dt == F32 else (n + 1) // 2
            w = (w + 7) // 8 * 8
            assert off[0] + w <= NBIG, ("SBUF overflow", name, off[0], w)
            ap = big[:, off[0]:off[0] + w]
            off[0] += w
            if dt != F32:
                ap = ap.bitcast(dt)
            ap = ap[:, 0:n]
            t = T()
            t.ap = ap
            uid[0] += 1
            t.key = (name or "t") + str(uid[0])
            return t

        pc = tile(NPC, F32, "pc")
        ident_b = tile(128, BF16, "idb"); ident_f = tile(128, F32, "idf")
        ucm = tile(128, BF16, "ucm"); ones_b = tile(128, BF16, "ones"); ones_f = tile(128, F32, "onesf")
        S.add("sp", lambda e: e.dma_start(out=pc.ap, in_=pc_d), writes=[pc.key], dma=True)
        S.add("sp", lambda e: e.dma_start(out=ident_f.ap, in_=ident_d), writes=[ident_f.key], dma=True)
        S.add("sp", lambda e: e.dma_start(out=ones_f.ap, in_=ones_d), writes=[ones_f.key], dma=True)
        S.add("pool", lambda e: e.dma_start(out=ident_b.ap, in_=ident_d), writes=[ident_b.key], dma=True)
        S.add("pool", lambda e: e.dma_start(out=ucm.ap, in_=ucm_d), writes=[ucm.key], dma=True)
        S.add("pool", lambda e: e.dma_start(out=ones_b.ap, in_=ones_d), writes=[ones_b.key], dma=True)
        base_off = off[0]

        def pcc(c, n=1, p=128):
            return pc.ap[0:p, c:c + n]

        for (src, dst, rows) in ((w_in, wb_in, D), (w_bra, wb_bra, MIXA), (w_brb, wb_brb, MIXA), (w_out, wb_out, D),
                                 (w_up, wb_up, D), (w_down, wb_down, DFF), (w_ple, wb_ple, 256), (w_pg, wb_pg, D)):
            for r0 in range(0, rows, 256):
                S.add("pool", (lambda s_, d_, r_: (lambda e: e.dma_start(out=d_[r_:r_ + 256, :], in_=s_[r_:r_ + 256, :])))(src, dst, r0),
                      writes=[("wscr", dst.name if hasattr(dst, "name") else id(dst), r0)], dma=True)
        for s in range(NSEQ):
            for r0 in range(0, PAST, 512):
                S.add("pool", (lambda s_, r_: (lambda e: e.dma_start(out=VC[s_, r_:r_ + 512, :], in_=cache_v[s_, r_:r_ + 512, :])))(s, r0),
                      writes=[("vc", s, r0)], dma=True)
        S.barrier()
        import os as _os
        _stop = _os.environ.get("K_STOP", "")
        if _stop == "W":
            S.emit()
            return nc

        psi = [0]

        def nps():
            i = psi[0] % 8
            psi[0] += 1
            return pst[i], ("ps", i)

        def psbf(p):
            return p[:, 0:512].bitcast(BF16)

        class Pipe:
            def __init__(self, nbuf=3):
                self.slots = [tile(SLOT, BF16, "wslot") for _ in range(nbuf)]
                self.steps = []

            def step(self, loads, compute):
                self.steps.append((loads, compute))

            def run(self, depth=2):
                _ns = int(_os.environ.get("K_NSTEP", "-1"))
                if _ns >= 0:
                    self.steps = self.steps[:_ns]
                n = len(self.steps)
                nb = len(self.slots)
                views = [None] * n

                def issue(i):
                    sl = self.slots[i % nb]
                    vs = []
                    for (eo, (a, b), src) in self.steps[i][0]:
                        v = sl.ap[:, eo:eo + a * b].rearrange("p (a b) -> p a b", a=a, b=b)
                        vs.append(v)
                        S.add("sp", (lambda v_, s_: (lambda e: e.dma_start(out=v_, in_=s_)))(v, src),
                              writes=[(sl.key,)], dma=True)
                    views[i] = (vs, sl.key)

                for i in range(n + depth):
                    if i < n:
                        issue(i)
                    j = i - depth
                    if 0 <= j < n:
                        self.steps[j][1](views[j][0], views[j][1])
                self.steps = []

        def wsrc(wb, c0, w, kc=KC):
            return wb[0:kc * 128, c0:c0 + w].rearrange("(k p) c -> p k c", p=128)

        def evac_copy(i, out, in_, reads, writes):
            if i % 2 == 0:
                S.add("act", lambda e: e.activation(out, in_, AF.Copy), reads=reads, writes=writes)
            else:
                S.add("dve", lambda e: e.tensor_copy(out, in_), reads=reads, writes=writes)

        def norm_to_hT(xt, nb, hT, gcol, scr, xn, ss):
            n = nb * 128
            _lvl = int(_os.environ.get("K_PRE", "9"))
            if _lvl == 0:
                return
            for b in range(nb):
                xs = xt.ap[:, b * D:(b + 1) * D]
                S.add("act", (lambda xs_, b_: (lambda e: e.activation(scr.ap, xs_, AF.Square, accum_out=ss.ap[:, b_:b_ + 1])))(xs, b),
                      reads=[(xt.key, b)], writes=[scr.key, (ss.key, b)])
                S.add("act", (lambda b_: (lambda e: e.activation(ss.ap[:, b_:b_ + 1], ss.ap[:, b_:b_ + 1], AF.Sqrt, bias=pcc(PC_ZERO + 1), scale=1.0 / D)))(b),
                      reads=[(ss.key, b), pc.key], writes=[(ss.key, b)])
                S.add("dve", (lambda b_: (lambda e: e.reciprocal(ss.ap[:, b_:b_ + 1], ss.ap[:, b_:b_ + 1])))(b),
                      reads=[(ss.key, b)], writes=[(ss.key, b)])
                S.add("act", (lambda xs_, b_: (lambda e: e.activation(xn.ap, xs_, AF.Copy, scale=ss.ap[:, b_:b_ + 1])))(xs, b),
                      reads=[(xt.key, b), (ss.key, b)], writes=[xn.key])
                if _lvl == 1:
                    continue
                for g in range(4):
                    p, pk = nps()
                    pb = psbf(p)
                    for j in range(4):
                        kc = g * 4 + j
                        S.add("pe", (lambda pb_, j_, kc_: (lambda e: e.transpose(pb_[:, j_ * 128:(j_ + 1) * 128], xn.ap[:, kc_ * 128:(kc_ + 1) * 128], ident_b.ap)))(pb, j, kc),
                              reads=[xn.key, ident_b.key], writes=[pk])
                    if _lvl == 2:
                        continue
                    for j in range(4):
                        kc = g * 4 + j
                        o = hT.ap[:, kc * n + b * 128: kc * n + (b + 1) * 128]
                        i_ = pb[:, j * 128:(j + 1) * 128]
                        sc = pcc(gcol + kc)
                        _ev = _os.environ.get("K_EV", "dve")
                        if (j % 2 == 0 and _ev != "dve") or _ev == "act":
                            S.add("act", (lambda o_, i__, sc_: (lambda e: e.activation(o_, i__, AF.Copy, scale=sc_)))(o, i_, sc),
                                  reads=[pk, pc.key], writes=[(hT.key, kc, b)])
                        else:
                            S.add("dve", (lambda o_, i__, sc_: (lambda e: e.tensor_scalar(o_, i__, sc_, None, ALU.mult)))(o, i_, sc),
                                  reads=[pk, pc.key], writes=[(hT.key, kc, b)])

        xt = tile(4 * D, F32, "xt"); hT = tile(16 * 512, BF16, "hT")
        scr = tile(D, F32, "scr"); xn = tile(D, BF16, "xn"); ss = tile(8, F32, "ss")
        ext = [tile(16 + 512, F32, "ext") for _ in range(2)]
        carry = tile(16 * 12 + 128, F32, "carry")
        acc = [tile(512, F32, "acc") for _ in range(2)]
        qko = [tile(512, BF16, "qko") for _ in range(2)]
        stg_f = [tile(4 * 1024, F32, "stgf") for _ in range(2)]
        stg_b = [tile(4 * 1024, BF16, "stgb") for _ in range(2)]
        kts = [tile(512, BF16, "kts") for _ in range(2)]
        ifs = tile(4 * 8, F32, "ifs")
        bifb = tile(8, F32, "bifb"); cvt = tile(16 * 12, F32, "cvt")
        cstg = tile(D, F32, "cstg")
        pipe = Pipe(3)
        S.add("sp", lambda e: e.dma_start(out=bifb.ap, in_=bif_d.partition_broadcast(128)), writes=[bifb.key], dma=True)
        S.add("pool", lambda e: e.memset(carry.ap, 0.0), writes=[carry.key])
        cnt = [0]

        def k_transposes(kbf, nb, dest_fn):
            for h in range(8):
                p, pk = nps()
                pb = psbf(p)
                for b in range(nb):
                    src, rk = kbf(b, h)
                    S.add("pe", (lambda pb_, b_, src_: (lambda e: e.transpose(pb_[:, b_ * 128:(b_ + 1) * 128], src_, ident_b.ap)))(pb, b, src),
                          reads=[rk, ident_b.key], writes=[pk])
                kt = kts[cnt[0] % 2]
                cnt[0] += 1
                evac_copy(cnt[0], kt.ap[:, 0:nb * 128], pb[:, 0:nb * 128], [pk], [kt.key])
                dest_fn(h, kt)

        def phaseA_tile(kind, ti):
            if kind == "ctx":
                xsrc, t0, n, tg0 = x_ctx, ti * 512, 512, ti * 512
            elif kind == "main":
                xsrc, t0, n, tg0 = x_main, ti * 512, 512, HALF + ti * 512
            else:
                xsrc, t0, n, tg0 = x_smp, 0, NSEQ * LS, TP
            nb = n // 128
            full = kind != "ctx" and "f" not in _os.environ.get("K_SKIP", "")
            def pre(vs, key):
                S.add("sp", lambda e: e.dma_start(out=xt.ap[:, 0:nb * D].rearrange("p (b d) -> p b d", b=nb), in_=xsrc[t0:t0 + n, :].rearrange("(b p) d -> p b d", p=128)),
                      writes=[xt.key], dma=True)
                norm_to_hT(xt, nb, hT, PC_GMIX, scr, xn, ss)
            pipe.step([], pre)

            def hTs(kc, c0=0, w=None):
                w = n if w is None else w
                return hT.ap[:, kc * n + c0: kc * n + c0 + w]

            def fm_group(wv, j, p):
                for kc in range(KC):
                    S.add("pe", (lambda kc_: (lambda e: e.matmul(p[0][:, 0:n], wv[:, kc_, j * 128:(j + 1) * 128], hTs(kc_), start=(kc_ == 0), stop=(kc_ == KC - 1))))(kc),
                          reads=[wv_key[0], (hT.key, kc)], writes=[p[1]])

            def tm_group(wv, b, p, w=512):
                for kc in range(KC):
                    S.add("pe", (lambda kc_: (lambda e: e.matmul(p[0][:, 0:w], hTs(kc_, b * 128, 128), wv[:, kc_, 0:w], start=(kc_ == 0), stop=(kc_ == KC - 1))))(kc),
                          reads=[wv_key[0], (hT.key, kc, b)], writes=[p[1]])

            wv_key = [None]

            def qk_step(g):
                def comp(vs, key):
                    wv_key[0] = (key,)
                    for j in range(4):
                        ch = g * 4 + j
                        p = nps()
                        fm_group(vs[0], j, p)
                        ex = ext[ch % 2]; ac = acc[ch % 2]; qo = qko[ch % 2]
                        if kind != "smp":
                            exv = ex.ap[:, 0:3 + n]
                            S.add("pool", lambda e: e.tensor_copy(ex.ap[:, 0:3], carry.ap[:, ch * 12: ch * 12 + 3]), reads=[(carry.key, ch)], writes=[(ex.key, "h")])
                            S.add("act", lambda e: e.activation(ex.ap[:, 3:3 + n], p[0][:, 0:n], AF.Copy), reads=[p[1]], writes=[(ex.key, "b")])
                            S.add("pool", lambda e: e.tensor_copy(carry.ap[:, ch * 12: ch * 12 + 3], ex.ap[:, n:n + 3]), reads=[(ex.key, "b")], writes=[(carry.key, ch)])
                            sl = lambda jj: ex.ap[:, jj:jj + n]
                            av = ac.ap[:, 0:n]
                        else:
                            ex3 = ex.ap[:, 0:NSEQ * 67].rearrange("p (s t) -> p s t", s=NSEQ)
                            S.add("pool", lambda e: e.tensor_copy(ex3[:, :, 0:3], cvt.ap[:, ch * 12:(ch + 1) * 12].rearrange("p (s t) -> p s t", s=NSEQ)), reads=[(cvt.key, ch)], writes=[(ex.key, "h")])
                            S.add("act", lambda e: e.activation(ex3[:, :, 3:67], p[0][:, 0:n].rearrange("p (s t) -> p s t", s=NSEQ), AF.Copy), reads=[p[1]], writes=[(ex.key, "b")])
                            S.add("pool", lambda e: e.tensor_copy(carry.ap[:, ch * 12:(ch + 1) * 12].rearrange("p (s t) -> p s t", s=NSEQ), ex3[:, :, 64:67]), reads=[(ex.key, "b")], writes=[(carry.key, ch)])
                            sl = lambda jj: ex3[:, :, jj:jj + 64]
                            av = ac.ap[:, 0:n].rearrange("p (s t) -> p s t", s=NSEQ)
                        cw = lambda jj: pcc(PC_CW + jj * 16 + ch)
                        S.add("dve", lambda e: e.tensor_scalar(av, sl(3), cw(3), pcc(PC_CB + ch), ALU.mult, ALU.add), reads=[ex.key, pc.key], writes=[ac.key])
                        for jj in (2, 1, 0):
                            S.add("dve", (lambda jj_: (lambda e: e.scalar_tensor_tensor(av, sl(jj_), cw(jj_), av, ALU.mult, ALU.add)))(jj), reads=[ex.key, pc.key, ac.key], writes=[ac.key])
                        S.add("act", lambda e: e.activation(qo.ap[:, 0:n], ac.ap[:, 0:n], AF.Silu), reads=[ac.key], writes=[qo.key])
                        S.add("pool", lambda e: e.dma_start(out=QKT[ch, :, tg0:tg0 + n], in_=qo.ap[:, 0:n]), reads=[qo.key], writes=[("QKT", ch, tg0)], dma=True)
                return comp

            for g in range(4):
                pipe.step([(0, (KC, 512), wsrc(wb_in, C_QK + g * 512, 512))], qk_step(g))

            def tm_step(c0, half, mode):
                def comp(vs, key):
                    wv_key[0] = (key,)
                    sf = stg_f[0 if mode in ("o", "k") else 1]
                    sb_ = stg_b[0 if mode in ("vm", "k") else 1]
                    for b in range(nb):
                        p = nps()
                        tm_group(vs[0], b, p)
                        cs = slice(b * 1024 + half * 512, b * 1024 + half * 512 + 512)
                        if mode == "vm":
                            evac_copy(b, sb_.ap[:, cs], p[0][:, 0:512], [p[1]], [(sb_.key, b, half)])
                        elif mode == "o":
                            S.add("act", (lambda cs_: (lambda e: e.activation(sb_.ap[:, cs_], p[0][:, 0:512], AF.Sigmoid)))(cs), reads=[p[1]], writes=[(sb_.key, b, half)])
                        else:
                            if full:
                                S.add("dve", (lambda cs_: (lambda e: e.tensor_copy(sf.ap[:, cs_], p[0][:, 0:512])))(cs), reads=[p[1]], writes=[(sf.key, b, half)])
                            S.add("dve", (lambda cs_: (lambda e: e.tensor_copy(sb_.ap[:, cs_], p[0][:, 0:512])))(cs), reads=[p[1]], writes=[(sb_.key, b, half)])
                    if half == 1:
                        sfv = sf.ap[:, 0:nb * 1024].rearrange("p (b f) -> p b f", b=nb)
                        sbv = sb_.ap[:, 0:nb * 1024].rearrange("p (b f) -> p b f", b=nb)
                        rows = lambda dst, r0: dst[r0:r0 + n, :].rearrange("(b p) f -> p b f", p=128)
                        if mode == "vm":
                            S.add("pool", lambda e: e.dma_start(out=rows(VM, tg0), in_=sbv), reads=[sb_.key], writes=[("VM", tg0)], dma=True)
                        elif mode == "o":
                            S.add("pool", lambda e: e.dma_start(out=rows(OSG, tg0), in_=sbv), reads=[sb_.key], writes=[("OSG", tg0)], dma=True)
                        elif mode == "v":
                            if kind == "smp":
                                S.add("pool", lambda e: e.dma_start(out=rows(ov_smp, 0), in_=sfv), reads=[sf.key], writes=[("ovs",)], dma=True)
                                for s_ in range(NSEQ):
                                    S.add("pool", (lambda s__: (lambda e: e.dma_start(out=VC[s__, PAST:PAST + LS, :], in_=sb_.ap[(s__ % 2) * 64:(s__ % 2) * 64 + 64, (s__ // 2) * 1024:(s__ // 2 + 1) * 1024])))(s_),
                                          reads=[sb_.key], writes=[("VCn", s_)], dma=True)
                            else:
                                if full:
                                    S.add("pool", lambda e: e.dma_start(out=rows(ov_main, t0), in_=sfv), reads=[sf.key], writes=[("ovm", t0)], dma=True)
                                S.add("pool", lambda e: e.dma_start(out=rows(VP, tg0), in_=sbv), reads=[sb_.key], writes=[("VP", tg0)], dma=True)
                        elif mode == "k":
                            if kind == "smp":
                                S.add("pool", lambda e: e.dma_start(out=rows(ok_smp, 0), in_=sfv), reads=[sf.key], writes=[("oks",)], dma=True)
                            elif full:
                                S.add("pool", lambda e: e.dma_start(out=rows(ok_main, t0), in_=sfv), reads=[sf.key], writes=[("okm", t0)], dma=True)

                            def dest(h, kt):
                                if kind == "smp":
                                    S.add("pool", lambda e: e.dma_start(out=KTC[:, h, :, PAST:PAST + LS].rearrange("s d t -> d s t"),
                                                                      in_=kt.ap[:, 0:n].rearrange("p (s t) -> p s t", s=NSEQ)),
                                          reads=[kt.key], writes=[("KTCn", h)], dma=True)
                                else:
                                    S.add("pool", lambda e: e.dma_start(out=KTP[h, :, tg0:tg0 + n], in_=kt.ap[:, 0:n]), reads=[kt.key], writes=[("KTP", h, tg0)], dma=True)
                            k_transposes(lambda b, h: (sb_.ap[:, b * 1024 + h * 128: b * 1024 + (h + 1) * 128], (sb_.key, b)), nb, dest)
                return comp

            for half in range(2):
                pipe.step([(0, (KC, 512), wsrc(wb_in, C_VM + half * 512, 512))], tm_step(C_VM, half, "vm"))
            _skip = _os.environ.get("K_SKIP", "")
            if full and "o" not in _skip:
                for half in range(2):
                    pipe.step([(0, (KC, 512), wsrc(wb_in, C_O + half * 512, 512))], tm_step(C_O, half, "o"))
            for half in range(2):
                pipe.step([(0, (KC, 512), wsrc(wb_in, C_SK + half * 512, 512))], tm_step(C_SK, half, "k"))
            for half in range(2):
                pipe.step([(0, (KC, 512), wsrc(wb_in, C_SV + half * 512, 512))], tm_step(C_SV, half, "v"))

            def if_step():
                def comp(vs, key):
                    wv_key[0] = (key,)
                    for b in range(nb):
                        p = nps()
                        tm_group(vs[0], b, p, w=8)
                        S.add("dve", (lambda b_, p_: (lambda e: e.tensor_tensor(ifs.ap[:, b_ * 8:(b_ + 1) * 8], p_[0][:, 0:8], bifb.ap, ALU.add)))(b, p),
                              reads=[p[1], bifb.key], writes=[(ifs.key, b)])
                    S.add("pool", lambda e: e.dma_start(out=IFT[tg0:tg0 + n, :].rearrange("(b p) g -> p b g", p=128), in_=ifs.ap[:, 0:nb * 8].rearrange("p (b g) -> p b g", b=nb)),
                          reads=[ifs.key], writes=[("IFT", tg0)], dma=True)
                return comp
            pipe.step([(0, (KC, 8), wsrc(wb_in, C_IF, 8))], if_step())

            if full and "q" not in _skip:
                def q_step(g):
                    def comp(vs, key):
                        wv_key[0] = (key,)
                        for j in range(4):
                            h = g * 4 + j
                            p = nps()
                            fm_group(vs[0], j, p)
                            qo = qko[h % 2]
                            S.add("act", lambda e: e.activation(qo.ap[:, 0:n], p[0][:, 0:n], AF.Copy, scale=float(128 ** -0.5)), reads=[p[1]], writes=[qo.key])
                            tq0 = tg0 - HALF
                            S.add("pool", lambda e: e.dma_start(out=QTS[h, :, tq0:tq0 + n], in_=qo.ap[:, 0:n]), reads=[qo.key], writes=[("QTS", h, tq0)], dma=True)
                    return comp
                for g in range(2):
                    pipe.step([(0, (KC, 512), wsrc(wb_in, C_SQ + g * 512, 512))], q_step(g))

        def conv_state_out(ncol, dst):
            for g in range(4):
                p, pk = nps()
                for j in range(4):
                    ch = g * 4 + j
                    S.add("pe", (lambda j_, ch_: (lambda e: e.transpose(p[:, j_ * 128:(j_ + 1) * 128], carry.ap[:, ch_ * 12: ch_ * 12 + 128], ident_f.ap)))(j, ch),
                          reads=[carry.key, ident_f.key], writes=[pk])
                evac_copy(g, cstg.ap[0:ncol, g * 512:(g + 1) * 512], p[0:ncol, 0:512], [pk], [(cstg.key, g)])
            S.add("pool", lambda e: e.dma_start(out=dst, in_=cstg.ap[0:ncol, :]), reads=[cstg.key], writes=[("cvo", ncol)], dma=True)

        for s in range(NSEQ):
            for r0 in range(0, PAST, 512):
                sb_ = stg_b[(r0 // 512) % 2]
                S.add("pool", (lambda sb__, s_, r_: (lambda e: e.dma_start(out=sb__.ap[:, 0:4096].rearrange("p (b f) -> p b f", b=4), in_=cache_k[s_, r_:r_ + 512, :].rearrange("(b p) f -> p b f", p=128))))(sb_, s, r0),
                      writes=[sb_.key], dma=True)

                def dest(h, kt, s_=s, r_=r0):
                    S.add("pool", lambda e: e.dma_start(out=KTC[s_, h, :, r_:r_ + 512], in_=kt.ap[:, 0:512]), reads=[kt.key], writes=[("KTC", s_, h, r_)], dma=True)
                k_transposes((lambda sb__: (lambda b, h: (sb__.ap[:, b * 1024 + h * 128: b * 1024 + (h + 1) * 128], (sb__.key,))))(sb_), 4, dest)

        if _stop == "A0":
            S.barrier(); S.emit()
            return nc
        for ti in range(NT):
            phaseA_tile("ctx", ti)
        pipe.run()
        if _stop == "A1":
            S.barrier(); S.emit()
            return nc
        S.add("dve", lambda e: e.tensor_scalar(carry.ap, carry.ap, pcc(PC_FLAG), None, ALU.mult), reads=[carry.key, pc.key], writes=[carry.key])
        for ti in range(NT):
            phaseA_tile("main", ti)
        pipe.run()
        if _stop == "A2":
            S.barrier(); S.emit()
            return nc
        conv_state_out(3, oconv_main)
        if _stop == "A3":
            S.barrier(); S.emit()
            return nc
        S.add("sp", lambda e: e.dma_start(out=cstg.ap[0:12, :], in_=st_conv), writes=[cstg.key], dma=True)
        for g in range(4):
            p, pk = nps()
            for j in range(4):
                ch = g * 4 + j
                S.add("pe", (lambda j_, ch_, p_: (lambda e: e.transpose(p_[:, j_ * 12:(j_ + 1) * 12], cstg.ap[0:12, ch_ * 128:(ch_ + 1) * 128], ident_f.ap[0:12, 0:12])))(j, ch, p),
                      reads=[cstg.key, ident_f.key], writes=[pk])
            evac_copy(g, cvt.ap[:, g * 48:(g + 1) * 48], p[:, 0:48], [pk], [(cvt.key, 4 * g), (cvt.key, 4 * g + 1), (cvt.key, 4 * g + 2), (cvt.key, 4 * g + 3)])
        phaseA_tile("smp", 0)
        pipe.run()
        conv_state_out(12, oconv_smp)
        S.barrier()
        if _stop == "A":
            S.emit()
            return nc

        off[0] = base_off
        tri = tile(64, F32, "tri"); sel = tile(128, F32, "sel"); i4 = tile(256, F32, "i4")
        negm = tile(256, F32, "negm"); mt4 = tile(256, F32, "mt4"); mlg = tile(MIXA, F32, "mlg")
        S.add("sp", lambda e: e.dma_start(out=tri.ap[0:64, :], in_=tri_d), writes=[tri.key], dma=True)
        S.add("sp", lambda e: e.dma_start(out=sel.ap[0:64, :], in_=sel_d), writes=[sel.key], dma=True)
        S.add("sp", lambda e: e.dma_start(out=i4.ap[0:64, :], in_=i4_d), writes=[i4.key], dma=True)
        S.add("sp", lambda e: e.dma_start(out=negm.ap[0:64, :], in_=negm_d), writes=[negm.key], dma=True)
        S.add("sp", lambda e: e.dma_start(out=mt4.ap[0:64, :], in_=mt4_d), writes=[mt4.key], dma=True)
        S.add("sp", lambda e: e.dma_start(out=mlg.ap[0:64, :], in_=mlg_d.partition_broadcast(64)), writes=[mlg.key], dma=True)
        NG = NCH * 4
        ift = tile(NCH * 8, F32, "ift")
        tE = tile(NG, F32, "tE"); tL = tile(NG, F32, "tL"); bc = tile(NG, F32, "bc"); ta = tile(NG, F32, "ta")
        cm = tile(NG, F32, "cm"); Mn = tile(NG, F32, "Mn"); M0 = tile(NG, F32, "M0"); M0b = tile(NG, F32, "M0b")
        tg = tile(NG, F32, "tg"); tmt = tile(NG, F32, "tmt"); tfl = tile(NG, F32, "tfl"); tdec = tile(NG, F32, "tdec")
        cdec = tile(NG, F32, "cdec"); mtb = tile(NG, F32, "mtb"); minit = tile(4, F32, "minit")
        dD = [tile(256, F32, "dD") for _ in range(2)]; tmx = [tile(256, F32, "tmx") for _ in range(2)]
        P64 = slice(0, 64)

        def v3(t, p=64):
            return t.ap[0:p, 0:NG].rearrange("p (c h) -> p c h", h=4)

        ift3 = ift.ap[0:64, :].rearrange("p (c g) -> p c g", g=8)
        for c0 in range(0, NCH, 32):
            c1 = min(NCH, c0 + 32)
            S.add("sp", (lambda c0_, c1_: (lambda e: e.dma_start(out=ift3[:, c0_:c1_, :], in_=IFT[c0_ * 64:c1_ * 64, :].rearrange("(c t) g -> t c g", t=64))))(c0, c1),
                  writes=[(ift.key, c0)], dma=True)
        S.add("act", lambda e: e.activation(v3(tE), ift3[:, :, 4:8], AF.Exp, scale=-1.0), reads=[ift.key], writes=[tE.key])
        S.add("act", lambda e: e.activation(tL.ap[P64, 0:NG], tE.ap[P64, 0:NG], AF.Ln, bias=pcc(PC_ZERO + 2, p=64)), reads=[tE.key, pc.key], writes=[tL.key])
        for c0 in range(0, NG, 512):
            c1 = min(NG, c0 + 512)
            p, pk = nps()
            S.add("pe", (lambda c0_, c1_, p_: (lambda e: e.matmul(p_[0:64, 0:c1_ - c0_], tri.ap[0:64, 0:64], tL.ap[0:64, c0_:c1_], start=True, stop=True)))(c0, c1, p),
                  reads=[tri.key, tL.key], writes=[pk])
            S.add("dve", (lambda c0_, c1_, p_: (lambda e: e.tensor_copy(bc.ap[0:64, c0_:c1_], p_[0:64, 0:c1_ - c0_])))(c0, c1, p), reads=[pk], writes=[(bc.key, c0)])
        S.add("dve", lambda e: e.tensor_tensor(v3(ta), ift3[:, :, 0:4], v3(bc), ALU.add), reads=[ift.key, bc.key], writes=[ta.key])
        for c in range(NCH):
            d_ = dD[c % 2]; mx = tmx[c % 2]
            d3 = d_.ap[0:64, :].rearrange("p (h s) -> p h s", h=4)
            S.add("dve", (lambda c_, d3_: (lambda e: e.tensor_tensor(d3_, i4.ap[0:64, :].rearrange("p (h s) -> p h s", h=4), v3(ta)[:, c_, :].unsqueeze(2).to_broadcast([64, 4, 64]), ALU.mult)))(c, d3),
                  reads=[i4.key, ta.key], writes=[d_.key])
            p, pk = nps()
            S.add("pe", (lambda p_, d__: (lambda e: e.matmul(p_[0:64, 0:256], ones_f.ap[0:64, 0:64], d__.ap[0:64, :], start=True, stop=True)))(p, d_), reads=[ones_f.key, d_.key], writes=[pk])
            S.add("dve", (lambda p_, mx_: (lambda e: e.tensor_tensor(mx_.ap[0:64, :], p_[0:64, 0:256], negm.ap[0:64, :], ALU.add)))(p, mx), reads=[pk, negm.key], writes=[mx.key])
            S.add("dve", (lambda c_, mx_: (lambda e: e.tensor_reduce(v3(cm)[:, c_, :], mx_.ap[0:64, :].rearrange("p (h s) -> p h s", h=4), AX.X, ALU.max)))(c, mx), reads=[mx.key], writes=[(cm.key, c)])
        S.add("pool", lambda e: e.memset(M0.ap[0:64, :], 0.0), writes=[M0.key])
        S.add("pool", lambda e: e.memset(Mn.ap[0:64, :], 0.0), writes=[Mn.key])
        R = slice(32, 64)
        negb = tile(NG, F32, "negb")
        S.add("dve", lambda e: e.tensor_scalar(negb.ap[0:64, 0:NG], bc.ap[0:64, 0:NG], -1.0, None, ALU.mult), reads=[bc.key], writes=[negb.key])

        def v3r(t):
            return t.ap[32:64, 0:NG].rearrange("p (c h) -> p c h", h=4)
        for h in range(4):
            S.add("dve", (lambda h_: (lambda e: e.tensor_tensor_scan(v3r(Mn)[:, 0:NCc, h_], v3r(cm)[:, 0:NCc, h_], v3r(negb)[:, 0:NCc, h_], 0.0, ALU.max, ALU.add)))(h),
                  reads=[cm.key, negb.key], writes=[(Mn.key, "c", h)])
        S.add("dve", lambda e: e.tensor_scalar(minit.ap[32:64, 0:4], v3r(Mn)[:, NCc - 1, :], pcc(PC_FLAG)[32:64, :], None, ALU.mult), reads=[Mn.key, pc.key], writes=[minit.key])
        for h in range(4):
            S.add("dve", (lambda h_: (lambda e: e.tensor_tensor_scan(v3r(Mn)[:, NCc:NCp, h_], v3r(cm)[:, NCc:NCp, h_], v3r(negb)[:, NCc:NCp, h_], minit.ap[32:64, h_:h_ + 1], ALU.max, ALU.add)))(h),
                  reads=[cm.key, negb.key, minit.key], writes=[(Mn.key, "m", h)])
        S.add("dve", lambda e: e.tensor_copy(v3r(M0)[:, 1:NCc, :], v3r(Mn)[:, 0:NCc - 1, :]), reads=[Mn.key], writes=[M0.key])
        S.add("dve", lambda e: e.tensor_copy(v3r(M0)[:, NCc, :], minit.ap[32:64, 0:4]), reads=[minit.key], writes=[M0.key])
        S.add("dve", lambda e: e.tensor_copy(v3r(M0)[:, NCc + 1:NCp, :], v3r(Mn)[:, NCc:NCp - 1, :]), reads=[Mn.key], writes=[M0.key])
        S.add("sp", lambda e: e.dma_start(out=M0.ap[0:64, NCp * 4:NG], in_=st_m.partition_broadcast(64)), reads=[M0.key], writes=[M0.key], dma=True)

        def selmm(dst, src, npart):
            for c0 in range(0, NG, 512):
                c1 = min(NG, c0 + 512)
                p, pk = nps()
                S.add("pe", (lambda c0_, c1_, p_: (lambda e: e.matmul(p_[0:npart, 0:c1_ - c0_], sel.ap[0:64, 0:npart], src.ap[0:64, c0_:c1_], start=True, stop=True)))(c0, c1, p),
                      reads=[sel.key, src.key], writes=[pk])
                S.add("dve", (lambda c0_, c1_, p_: (lambda e: e.tensor_copy(dst.ap[0:npart, c0_:c1_], p_[0:npart, 0:c1_ - c0_])))(c0, c1, p), reads=[pk], writes=[(dst.key, c0)])
        selmm(M0b, M0, 64)
        A64 = lambda t: t.ap[0:64, 0:NG]
        S.add("dve", lambda e: e.tensor_tensor(A64(tg), A64(cm), A64(M0b), ALU.max), reads=[cm.key, M0b.key], writes=[tg.key])
        S.add("dve", lambda e: e.tensor_tensor(A64(tmt), A64(tg), A64(bc), ALU.subtract), reads=[tg.key, bc.key], writes=[tmt.key])
        S.add("act", lambda e: e.activation(A64(tfl), A64(tmt), AF.Exp, scale=-1.0), reads=[tmt.key], writes=[tfl.key])
        S.add("dve", lambda e: e.tensor_tensor(A64(tdec), A64(M0b), A64(tg), ALU.subtract), reads=[tg.key, M0b.key], writes=[tdec.key])
        S.add("act", lambda e: e.activation(A64(tdec), A64(tdec), AF.Exp), reads=[tdec.key], writes=[tdec.key])
        selmm(cdec, tdec, 128)
        selmm(mtb, tmt, 64)
        S.add("sp", lambda e: e.dma_start(out=om_main, in_=mtb.ap[0:1, (NCp - 1) * 4:NCp * 4]), reads=[mtb.key], dma=True)
        S.add("sp", lambda e: e.dma_start(out=om_smp, in_=mtb.ap[0:1, NCp * 4:NG]), reads=[mtb.key], dma=True)

        if _stop == "B0":
            S.barrier(); S.emit()
            return nc
        qkt = [tile(16 * 512, BF16, "qkt") for _ in range(2)]
        vt = [tile(8 * 1024, BF16, "vt") for _ in range(2)]
        ost = [tile(8 * 1024, BF16, "ost") for _ in range(2)]
        Cst = tile(4 * 512, F32, "Cst"); Cbf = tile(4 * 512, BF16, "Cbf"); nst = tile(8, F32, "nst"); nbf = tile(8, BF16, "nbf")
        wT = [tile(256, F32, "wT") for _ in range(2)]; SW = [tile(256, BF16, "SW") for _ in range(2)]
        wk16 = [tile(4, F32, "wk16") for _ in range(2)]; kw = [tile(1024, BF16, "kw") for _ in range(2)]
        intra = tile(1024, F32, "intra"); num = tile(1024, F32, "num"); hg = tile(1024, F32, "hg")
        dn = tile(8, F32, "dn"); rr = tile(4, F32, "rr"); ssq = tile(4, F32, "ssq"); sq2 = tile(256, F32, "sq2")
        hmo = [tile(1024, BF16, "hmo") for _ in range(2)]
        S.add("pool", lambda e: e.memset(Cst.ap, 0.0), writes=[Cst.key])
        S.add("pool", lambda e: e.memset(nst.ap, 0.0), writes=[nst.key])
        S.add("pool", lambda e: e.memset(Cbf.ap, 0.0), writes=[Cbf.key])
        S.add("pool", lambda e: e.memset(nbf.ap, 0.0), writes=[nbf.key])
        B_SG, B_KT, B_I0, B_I1, B_N0, B_N1, B_DEN, B_DC = range(8)
        PK = lambda i: ("ps", i)

        def chunk(c, sc, j, out, qk_, v_, o_):
            cols = slice(j * 64, j * 64 + 64)
            nload = 512 if c < NCp else NSEQ * LS
            qv = lambda ch: qk_.ap[:, ch * nload + j * 64: ch * nload + j * 64 + 64]
            vv = lambda h: v_.ap[0:64, j * 1024 + h * 256: j * 1024 + (h + 1) * 256]
            w_ = wT[c % 2]; sw_ = SW[c % 2]; wk_ = wk16[c % 2]; kw_ = kw[c % 2]; d_ = dD[c % 2]
            d3 = d_.ap[0:64, :].rearrange("p (h s) -> p h s", h=4)
            if _os.environ.get("K_GBF"):
                dbf = d_.ap[0:64, 0:128].bitcast(BF16)
                d3b = dbf.rearrange("p (h s) -> p h s", h=4)
                S.add("dve", lambda e: e.tensor_tensor(d3b, i4.ap[0:64, :].rearrange("p (h s) -> p h s", h=4), v3(tg)[:, c, :].unsqueeze(2).to_broadcast([64, 4, 64]), ALU.mult),
                      reads=[i4.key, tg.key], writes=[d_.key])
                S.add("pe", lambda e: e.matmul(pst[B_SG][0:64, 256:512], ones_b.ap[0:64, 0:64], dbf, start=True, stop=True), reads=[ones_b.key, d_.key], writes=[("ps", B_SG, "g")])
            else:
                S.add("dve", lambda e: e.tensor_tensor(d3, i4.ap[0:64, :].rearrange("p (h s) -> p h s", h=4), v3(tg)[:, c, :].unsqueeze(2).to_broadcast([64, 4, 64]), ALU.mult),
                      reads=[i4.key, tg.key], writes=[d_.key])
                S.add("pe", lambda e: e.matmul(pst[B_SG][0:64, 256:512], ones_f.ap[0:64, 0:64], d_.ap[0:64, :], start=True, stop=True), reads=[ones_f.key, d_.key], writes=[("ps", B_SG, "g")])
            for h in range(4):
                S.add("act", (lambda h_: (lambda e: e.activation(w_.ap[0:64, h_ * 64:(h_ + 1) * 64], pst[B_SG][0:64, 256 + h_ * 64:256 + (h_ + 1) * 64], AF.Exp, bias=v3(ta)[:, c, h_:h_ + 1], scale=-1.0)))(h),
                      reads=[("ps", B_SG, "g"), ta.key], writes=[(w_.key, h)])
            S.add("pool", lambda e: e.tensor_tensor(w_.ap[0:64, :], w_.ap[0:64, :], mt4.ap[0:64, :], ALU.mult), reads=[w_.key, mt4.key], writes=[w_.key])
            S.add("dve", lambda e: e.tensor_scalar(wk_.ap[0:64, 0:4], w_.ap[0:64, :].rearrange("p (h t) -> p h t", h=4)[:, :, 63], 1.0 / 16, None, ALU.mult), reads=[w_.key], writes=[wk_.key])
            _bs = _os.environ.get("K_BSKIP", "")
            if "o" in _bs:
                out = False
            if out:
                for h in range(4):
                    for kc in range(2):
                        S.add("pe", (lambda h_, kc_: (lambda e: e.matmul(pst[B_SG][0:64, h_ * 64:(h_ + 1) * 64], qv(8 + 2 * h_ + kc_), qv(2 * h_ + kc_), start=(kc_ == 0), stop=(kc_ == 1))))(h, kc),
                              reads=[qk_.key], writes=[("ps", B_SG, "s")])
                S.add("dve", lambda e: e.scalar_tensor_tensor(sw_.ap[0:64, :], pst[B_SG][0:64, 0:256], 1.0 / 16, w_.ap[0:64, :], ALU.mult, ALU.mult), reads=[("ps", B_SG, "s"), w_.key], writes=[sw_.key])
                for h in range(4):
                    bi = B_I0 + h // 2
                    S.add("pe", (lambda h_, bi_: (lambda e: e.matmul(pst[bi_][0:64, (h_ % 2) * 256:(h_ % 2 + 1) * 256], sw_.ap[0:64, h_ * 64:(h_ + 1) * 64], vv(h_), start=True, stop=True)))(h, bi),
                          reads=[sw_.key, v_.key], writes=[("ps", bi, h % 2)])
                    S.add("pe", (lambda h_: (lambda e: e.matmul(pst[B_DEN][0:64, h_:h_ + 1], sw_.ap[0:64, h_ * 64:(h_ + 1) * 64], ones_b.ap[0:64, 0:1], start=True, stop=True)))(h),
                          reads=[sw_.key, ones_b.key], writes=[("ps", B_DEN, h)])
                for h in range(4):
                    bi = B_N0 + h // 2
                    for kc in range(2):
                        S.add("pe", (lambda h_, kc_, bi_: (lambda e: e.matmul(pst[bi_][0:64, (h_ % 2) * 256:(h_ % 2 + 1) * 256], qv(2 * h_ + kc_), Cbf.ap[:, (h_ * 2 + kc_) * 256:(h_ * 2 + kc_ + 1) * 256], start=(kc_ == 0), stop=(kc_ == 1))))(h, kc, bi),
                              reads=[qk_.key, (Cbf.key, h)], writes=[("ps", bi, h % 2)])
                    for kc in range(2):
                        S.add("pe", (lambda h_, kc_: (lambda e: e.matmul(pst[B_DEN][0:64, 4 + h_:5 + h_], qv(2 * h_ + kc_), nbf.ap[:, h_ * 2 + kc_:h_ * 2 + kc_ + 1], start=(kc_ == 0), stop=(kc_ == 1))))(h, kc),
                              reads=[qk_.key, (nbf.key, h)], writes=[("ps", B_DEN, 4 + h)])
                for bi in (B_I0, B_I1):
                    S.add("act", (lambda bi_: (lambda e: e.activation(intra.ap[0:64, (bi_ - B_I0) * 512:(bi_ - B_I0 + 1) * 512], pst[bi_][0:64, :], AF.Copy)))(bi), reads=[("ps", bi)], writes=[(intra.key, bi)])
                for h in range(4):
                    bi = B_N0 + h // 2
                    S.add("dve", (lambda h_, bi_: (lambda e: e.scalar_tensor_tensor(num.ap[0:64, h_ * 256:(h_ + 1) * 256], pst[bi_][0:64, (h_ % 2) * 256:(h_ % 2 + 1) * 256], v3(tdec)[:, c, h_:h_ + 1], intra.ap[0:64, h_ * 256:(h_ + 1) * 256], ALU.mult, ALU.add)))(h, bi),
                          reads=[("ps", bi, h % 2), tdec.key, intra.key], writes=[(num.key, h)])
                S.add("dve", lambda e: e.tensor_copy(dn.ap[0:64, 0:8], pst[B_DEN][0:64, 0:8]), reads=[("ps", B_DEN)], writes=[dn.key])
                S.add("dve", lambda e: e.tensor_tensor(dn.ap[0:64, 4:8], dn.ap[0:64, 4:8], v3(tdec)[:, c, :], ALU.mult), reads=[dn.key, tdec.key], writes=[dn.key])
                S.add("dve", lambda e: e.tensor_tensor(dn.ap[0:64, 0:4], dn.ap[0:64, 0:4], dn.ap[0:64, 4:8], ALU.add), reads=[dn.key], writes=[dn.key])
                S.add("dve", lambda e: e.tensor_scalar(dn.ap[0:64, 4:8], dn.ap[0:64, 0:4], -1.0, None, ALU.mult), reads=[dn.key], writes=[dn.key])
                S.add("dve", lambda e: e.tensor_tensor(dn.ap[0:64, 0:4], dn.ap[0:64, 0:4], dn.ap[0:64, 4:8], ALU.max), reads=[dn.key], writes=[dn.key])
                S.add("dve", lambda e: e.tensor_tensor(dn.ap[0:64, 0:4], dn.ap[0:64, 0:4], v3(tfl)[:, c, :], ALU.max), reads=[dn.key, tfl.key], writes=[dn.key])
                S.add("dve", lambda e: e.reciprocal(rr.ap[0:64, 0:4], dn.ap[0:64, 0:4]), reads=[dn.key], writes=[rr.key])
                for h in range(4):
                    S.add("dve", (lambda h_: (lambda e: e.scalar_tensor_tensor(hg.ap[0:64, h_ * 256:(h_ + 1) * 256], num.ap[0:64, h_ * 256:(h_ + 1) * 256], rr.ap[0:64, h_:h_ + 1], o_.ap[0:64, j * 1024 + h_ * 256: j * 1024 + (h_ + 1) * 256], ALU.mult, ALU.mult)))(h),
                          reads=[(num.key, h), rr.key, o_.key], writes=[(hg.key, h)])
                    S.add("act", (lambda h_: (lambda e: e.activation(sq2.ap[0:64, :], hg.ap[0:64, h_ * 256:(h_ + 1) * 256], AF.Square, accum_out=ssq.ap[0:64, h_:h_ + 1])))(h),
                          reads=[(hg.key, h)], writes=[sq2.key, (ssq.key, h)])
                S.add("act", lambda e: e.activation(ssq.ap[0:64, 0:4], ssq.ap[0:64, 0:4], AF.Sqrt, bias=pcc(PC_ZERO + 1, p=64), scale=1.0 / 256), reads=[ssq.key, pc.key], writes=[ssq.key])
                S.add("dve", lambda e: e.reciprocal(ssq.ap[0:64, 0:4], ssq.ap[0:64, 0:4]), reads=[ssq.key], writes=[ssq.key])
                ho = hmo[c % 2]
                for h in range(4):
                    S.add("dve", (lambda h_: (lambda e: e.scalar_tensor_tensor(ho.ap[0:64, h_ * 256:(h_ + 1) * 256], hg.ap[0:64, h_ * 256:(h_ + 1) * 256], ssq.ap[0:64, h_:h_ + 1], mlg.ap[0:64, h_ * 256:(h_ + 1) * 256], ALU.mult, ALU.mult)))(h),
                          reads=[(hg.key, h), ssq.key, mlg.key], writes=[(ho.key, h)])
                tg0 = c * 64
                S.add("pool", lambda e: e.dma_start(out=HM[tg0:tg0 + 64, :], in_=ho.ap[0:64, :]), reads=[ho.key], writes=[("HM", c)], dma=True)
            if "s" in _bs:
                return
            pb = psbf(pst[B_KT])
            for h in range(4):
                for kc in range(2):
                    if "y" in _bs:
                        continue
                    S.add("pe", (lambda h_, kc_: (lambda e: e.transpose(pb[0:64, (2 * h_ + kc_) * 128:(2 * h_ + kc_ + 1) * 128], qv(8 + 2 * h_ + kc_), ident_b.ap)))(h, kc),
                          reads=[qk_.key, ident_b.key], writes=[("ps", B_KT, h)])
                if "x" in _bs:
                    continue
                S.add("dve", (lambda h_: (lambda e: e.tensor_scalar(kw_.ap[0:64, h_ * 256:(h_ + 1) * 256], pb[0:64, h_ * 256:(h_ + 1) * 256], wk_.ap[0:64, h_:h_ + 1], None, ALU.mult)))(h),
                      reads=[("ps", B_KT, h), wk_.key], writes=[(kw_.key, h)])
            if "t" in _bs:
                return
            for h in range(4):
                for kc in range(2):
                    S.add("pe", (lambda h_, kc_: (lambda e: e.matmul(pst[B_DC][:, kc_ * 256:(kc_ + 1) * 256], kw_.ap[0:64, h_ * 256 + kc_ * 128: h_ * 256 + (kc_ + 1) * 128], vv(h_), start=True, stop=True)))(h, kc),
                          reads=[(kw_.key, h), v_.key], writes=[("ps", B_DC)])
                    S.add("pe", (lambda h_, kc_: (lambda e: e.matmul(pst[B_DEN][:, 16 + 2 * h_ + kc_: 17 + 2 * h_ + kc_], kw_.ap[0:64, h_ * 256 + kc_ * 128: h_ * 256 + (kc_ + 1) * 128], ones_b.ap[0:64, 0:1], start=True, stop=True)))(h, kc),
                          reads=[(kw_.key, h), ones_b.key], writes=[("ps", B_DEN, "n", h)])
                cd = cdec.ap[:, c * 4 + h: c * 4 + h + 1]
                if "m" in _bs:
                    continue
                S.add("dve", (lambda h_, cd_: (lambda e: e.scalar_tensor_tensor(Cst.ap[:, h_ * 512:(h_ + 1) * 512], Cst.ap[:, h_ * 512:(h_ + 1) * 512], cd_, pst[B_DC][:, 0:512], ALU.mult, ALU.add)))(h, cd),
                      reads=[(Cst.key, h), cdec.key, ("ps", B_DC)], writes=[(Cst.key, h)])
                S.add("dve", (lambda h_, cd_: (lambda e: e.scalar_tensor_tensor(nst.ap[:, 2 * h_:2 * h_ + 2], nst.ap[:, 2 * h_:2 * h_ + 2], cd_, pst[B_DEN][:, 16 + 2 * h_:18 + 2 * h_], ALU.mult, ALU.add)))(h, cd),
                      reads=[(nst.key, h), cdec.key, ("ps", B_DEN, "n", h)], writes=[(nst.key, h)])
                S.add("pool", (lambda h_: (lambda e: e.tensor_copy(Cbf.ap[:, h_ * 512:(h_ + 1) * 512], Cst.ap[:, h_ * 512:(h_ + 1) * 512])))(h), reads=[(Cst.key, h)], writes=[(Cbf.key, h)])
                S.add("pool", (lambda h_: (lambda e: e.tensor_copy(nbf.ap[:, 2 * h_:2 * h_ + 2], nst.ap[:, 2 * h_:2 * h_ + 2])))(h), reads=[(nst.key, h)], writes=[(nbf.key, h)])

        def state_out(dC, dn_):
            S.add("sp", lambda e: e.dma_start(out=dC.rearrange("h (k p) v -> p h k v", p=128), in_=Cst.ap.rearrange("p (h k v) -> p h k v", h=4, k=2)), reads=[Cst.key], dma=True)
            S.add("sp", lambda e: e.dma_start(out=dn_.rearrange("h (k p) -> p h k", p=128), in_=nst.ap.rearrange("p (h k) -> p h k", h=4)), reads=[nst.key], dma=True)

        with nc.allow_non_contiguous_dma(reason="small state layouts"):
            for sc in range(2 * NT):
                qk_ = qkt[sc % 2]; v_ = vt[sc % 2]; o_ = ost[sc % 2]
                t0 = sc * 512
                S.add("sp", (lambda qk__, t0_: (lambda e: e.dma_start(out=qk__.ap.rearrange("p (c t) -> p c t", c=16), in_=QKT[:, :, t0_:t0_ + 512].rearrange("c p t -> p c t"))))(qk_, t0), writes=[qk_.key], dma=True)
                S.add("sp", (lambda v__, t0_: (lambda e: e.dma_start(out=v__.ap[0:64, :].rearrange("p (c f) -> p c f", c=8), in_=VM[t0_:t0_ + 512, :].rearrange("(c s) f -> s c f", s=64))))(v_, t0), writes=[v_.key], dma=True)
                isout = sc >= NT
                if isout:
                    S.add("sp", (lambda o__, t0_: (lambda e: e.dma_start(out=o__.ap[0:64, :].rearrange("p (c f) -> p c f", c=8), in_=OSG[t0_:t0_ + 512, :].rearrange("(c s) f -> s c f", s=64))))(o_, t0), writes=[o_.key], dma=True)
                if sc == NT:
                    S.add("dve", lambda e: e.tensor_scalar(Cst.ap, Cst.ap, pcc(PC_FLAG), None, ALU.mult), reads=[Cst.key, pc.key], writes=[Cst.key])
                    S.add("dve", lambda e: e.tensor_scalar(nst.ap, nst.ap, pcc(PC_FLAG), None, ALU.mult), reads=[nst.key, pc.key], writes=[nst.key])
                    S.add("pool", lambda e: e.tensor_copy(Cbf.ap, Cst.ap), reads=[Cst.key], writes=[Cbf.key])
                    S.add("pool", lambda e: e.tensor_copy(nbf.ap, nst.ap), reads=[nst.key], writes=[nbf.key])
                for j in range(8):
                    chunk(sc * 8 + j, sc, j, isout, qk_, v_, o_)
            state_out(oC_main, on_main)
            qk_ = qkt[0]; v_ = vt[0]; o_ = ost[0]
            nS = NSEQ * LS
            S.add("sp", lambda e: e.dma_start(out=qk_.ap[:, 0:16 * nS].rearrange("p (c t) -> p c t", c=16), in_=QKT[:, :, TP:TP + nS].rearrange("c p t -> p c t")), writes=[qk_.key], dma=True)
            S.add("sp", lambda e: e.dma_start(out=v_.ap[0:64, 0:NSEQ * 1024].rearrange("p (c f) -> p c f", c=NSEQ), in_=VM[TP:TP + nS, :].rearrange("(c s) f -> s c f", s=64)), writes=[v_.key], dma=True)
            S.add("sp", lambda e: e.dma_start(out=o_.ap[0:64, 0:NSEQ * 1024].rearrange("p (c f) -> p c f", c=NSEQ), in_=OSG[TP:TP + nS, :].rearrange("(c s) f -> s c f", s=64)), writes=[o_.key], dma=True)
            for s in range(NSEQ):
                S.add("sp", (lambda s_: (lambda e: e.dma_start(out=Cst.ap.rearrange("p (h k v) -> p h k v", h=4, k=2), in_=st_C[s_].rearrange("h (k p) v -> p h k v", p=128))))(s), writes=[Cst.key], dma=True)
                S.add("sp", (lambda s_: (lambda e: e.dma_start(out=nst.ap.rearrange("p (h k) -> p h k", h=4), in_=st_n[s_].rearrange("h (k p) -> p h k", p=128))))(s), writes=[nst.key], dma=True)
                S.add("pool", lambda e: e.tensor_copy(Cbf.ap, Cst.ap), reads=[Cst.key], writes=[Cbf.key])
                S.add("pool", lambda e: e.tensor_copy(nbf.ap, nst.ap), reads=[nst.key], writes=[nbf.key])
                chunk(NCp + s, 0, s, True, qk_, v_, o_)
                state_out(oC_smp[s], on_smp[s])
        S.barrier()
        if _stop == "B":
            S.emit()
            return nc

        off[0] = base_off
        sbm = tile(4 * 512, BF16, "sbm")
        S.add("pool", lambda e: e.dma_start(out=sbm.ap, in_=sbmask_d), writes=[sbm.key], dma=True)
        NKBp = TP // 128
        KTt = [tile(max(TP, KCACHE + 64), BF16, "KTt") for _ in range(2)]
        Vt = [tile(max(NKBp, PAST // 128 + 1) * 128, BF16, "Vt") for _ in range(2)]
        QTt = [tile(512, BF16, "QTt") for _ in range(2)]
        tEc = [tile(512, F32, "tEc") for _ in range(2)]; tsp = [tile(512, F32, "tsp") for _ in range(2)]
        spb = [tile(512, BF16, "spb") for _ in range(2)]; t1 = [tile(512, F32, "t1") for _ in range(2)]
        t3 = [tile(512, F32, "t3") for _ in range(2)]; ab = [tile(512, BF16, "ab") for _ in range(2)]
        Rb = tile(512, F32, "Rb"); hso = [tile(512, BF16, "hso") for _ in range(2)]
        bi_ = [0]
        jobn = [0]
        ps6 = [0]

        def nps6():
            i = ps6[0] % 6
            ps6[0] += 1
            return pst[i], ("ps", i)

        def sb_job(KT, V, QT, N, blocks, dst):
            S.add("pool", lambda e: e.memset(Rb.ap[:, 0:N], 0.0), writes=[Rb.key])
            po, pok = pst[6 + jobn[0] % 2], ("ps", 6 + jobn[0] % 2)
            nblk = len(blocks)
            for bi, (k0, ns, vb, mj, bcol) in enumerate(blocks):
                i = bi_[0] % 2
                bi_[0] += 1
                E, sp, sb, a1, a3, aa = tEc[i], tsp[i], spb[i], t1[i], t3[i], ab[i]
                pz, pzk = nps6(); pcu, pck = nps6(); pr, prk = nps6()
                PS_ = slice(0, ns)
                bias = pcc(bcol, p=ns)
                S.add("pe", lambda e: e.matmul(pz[PS_, 0:N], KT.ap[:, k0:k0 + ns], QT.ap[:, 0:N], start=True, stop=True), reads=[KT.key, QT.key], writes=[pzk])
                S.add("act", lambda e: e.activation(E.ap[PS_, 0:N], pz[PS_, 0:N], AF.Exp, bias=bias), reads=[pzk, pc.key], writes=[E.key])
                S.add("act", lambda e: e.activation(sp.ap[PS_, 0:N], E.ap[PS_, 0:N], AF.Ln, bias=pcc(PC_ZERO + 2, p=ns)), reads=[E.key, pc.key], writes=[sp.key])
                if mj is None:
                    S.add("pool", lambda e: e.tensor_copy(sb.ap[PS_, 0:N], sp.ap[PS_, 0:N]), reads=[sp.key], writes=[sb.key])
                else:
                    S.add("pool", lambda e: e.tensor_tensor(sb.ap[PS_, 0:N], sp.ap[PS_, 0:N], sbm.ap[PS_, mj * 512: mj * 512 + N], ALU.mult), reads=[sp.key, sbm.key], writes=[sb.key])
                S.add("pe", lambda e: e.matmul(pcu[PS_, 0:N], ucm.ap[PS_, 0:ns], sb.ap[PS_, 0:N], start=True, stop=True), reads=[ucm.key, sb.key], writes=[pck])
                S.add("pe", lambda e: e.matmul(pr[:, 0:N], ones_b.ap[PS_, 0:128], sb.ap[PS_, 0:N], start=True, stop=True), reads=[ones_b.key, sb.key], writes=[prk])
                S.add("dve", lambda e: e.scalar_tensor_tensor(a1.ap[PS_, 0:N], pz[PS_, 0:N], bias, sp.ap[PS_, 0:N], ALU.add, ALU.subtract), reads=[pzk, pc.key, sp.key], writes=[a1.key])
                S.add("dve", lambda e: e.tensor_tensor(a1.ap[PS_, 0:N], a1.ap[PS_, 0:N], pcu[PS_, 0:N], ALU.subtract), reads=[a1.key, pck], writes=[a1.key])
                S.add("pool", lambda e: e.tensor_tensor(a3.ap[PS_, 0:N], a1.ap[PS_, 0:N], Rb.ap[PS_, 0:N], ALU.subtract), reads=[a1.key, Rb.key], writes=[a3.key])
                S.add("act", lambda e: e.activation(aa.ap[PS_, 0:N], a3.ap[PS_, 0:N], AF.Exp), reads=[a3.key], writes=[aa.key])
                if mj is not None:
                    S.add("pool", lambda e: e.tensor_tensor(aa.ap[PS_, 0:N], aa.ap[PS_, 0:N], sbm.ap[PS_, mj * 512: mj * 512 + N], ALU.mult), reads=[aa.key, sbm.key], writes=[aa.key])
                S.add("dve", lambda e: e.tensor_tensor(Rb.ap[:, 0:N], Rb.ap[:, 0:N], pr[:, 0:N], ALU.add), reads=[Rb.key, prk], writes=[Rb.key])
                S.add("pe", lambda e: e.matmul(po[:, 0:N], V.ap[PS_, vb * 128:(vb + 1) * 128], aa.ap[PS_, 0:N], start=(bi == 0), stop=(bi == nblk - 1)), reads=[V.key, aa.key], writes=[pok])
            ho = hso[jobn[0] % 2]
            jobn[0] += 1
            evac_copy(jobn[0], ho.ap[:, 0:N], po[:, 0:N], [pok], [ho.key])
            S.add("pool", lambda e: e.dma_start(out=dst, in_=ho.ap[:, 0:N]), reads=[ho.key], dma=True)

        hj = 0
        for h in range(8):
            KT = KTt[hj % 2]; V = Vt[hj % 2]
            hj += 1
            for c0 in range(0, TP, 2048):
                c1 = min(TP, c0 + 2048)
                S.add("sp", (lambda KT_, c0_, c1_: (lambda e: e.dma_start(out=KT_.ap[:, c0_:c1_], in_=KTP[h, :, c0_:c1_])))(KT, c0, c1), writes=[KT.key], dma=True)
            for c0 in range(0, NKBp, 16):
                c1 = min(NKBp, c0 + 16)
                S.add("sp", (lambda V_, c0_, c1_: (lambda e: e.dma_start(out=V_.ap[:, c0_ * 128:c1_ * 128].rearrange("p (b d) -> p b d", d=128), in_=VP[c0_ * 128:c1_ * 128, h * 128:(h + 1) * 128].rearrange("(b p) d -> p b d", p=128))))(V, c0, c1),
                      writes=[V.key], dma=True)
            for qt in range(NT):
                QT = QTt[qt % 2]
                S.add("sp", (lambda QT_: (lambda e: e.dma_start(out=QT_.ap, in_=QTS[h, :, qt * 512:(qt + 1) * 512])))(QT), writes=[QT.key], dma=True)
                kb_diag0 = HALF // 128 + 4 * qt
                blocks = []
                for kb in range(kb_diag0 + 3, -1, -1):
                    mj = kb - kb_diag0 if kb >= kb_diag0 else None
                    blocks.append((kb * 128, 128, kb, mj, PC_NEGB if kb < HALF // 128 else PC_ZERO))
                sb_job(KT, V, QT, 512, blocks, HST[h, :, qt * 512:(qt + 1) * 512])
        NKBc = PAST // 128
        for s in range(NSEQ):
            for h in range(8):
                KT = KTt[hj % 2]; V = Vt[hj % 2]
                hj += 1
                S.add("sp", (lambda KT_: (lambda e: e.dma_start(out=KT_.ap[:, 0:KCACHE], in_=KTC[s, h, :, :])))(KT), writes=[KT.key], dma=True)
                for c0 in range(0, NKBc, 16):
                    c1 = min(NKBc, c0 + 16)
                    S.add("sp", (lambda V_, c0_, c1_: (lambda e: e.dma_start(out=V_.ap[:, c0_ * 128:c1_ * 128].rearrange("p (b d) -> p b d", d=128), in_=VC[s, c0_ * 128:c1_ * 128, h * 128:(h + 1) * 128].rearrange("(b p) d -> p b d", p=128))))(V, c0, c1),
                          writes=[V.key], dma=True)
                S.add("sp", (lambda V_: (lambda e: e.dma_start(out=V_.ap[0:64, NKBc * 128:(NKBc + 1) * 128], in_=VC[s, PAST:PAST + LS, h * 128:(h + 1) * 128])))(V), writes=[V.key], dma=True)
                QT = QTt[hj % 2]
                S.add("sp", (lambda QT_: (lambda e: e.dma_start(out=QT_.ap[:, 0:LS], in_=QTS[h, :, HALF + s * LS: HALF + (s + 1) * LS])))(QT), writes=[QT.key], dma=True)
                blocks = [(PAST, 64, NKBc, 0, PC_ZERO)] + [(kb * 128, 128, kb, None, PC_ZERO) for kb in range(NKBc - 1, -1, -1)]
                sb_job(KT, V, QT, LS, blocks, HST[h, :, HALF + s * LS: HALF + (s + 1) * LS])
        S.barrier()
        if _stop == "C":
            S.emit()
            return nc

        off[0] = base_off
        xt = tile(4 * D, F32, "xt"); hT = tile(16 * 512, BF16, "hT")
        xn = tile(D, BF16, "xn"); ss = tile(8, F32, "ss")
        mixT = tile(16 * 512, F32, "mixT")
        scr = T(); scr.ap = mixT.ap[:, 0:D]; scr.key = mixT.key
        hm_tm = T(); hm_tm.ap = mixT.ap[:, D:2 * D].bitcast(BF16); hm_tm.key = mixT.key
        hid = tile(max(NFC * 512, 32 * 512), BF16, "hid")
        _dstop = _os.environ.get("K_DSTOP", "")
        tmpf = [tile(512, F32, "tmpf") for _ in range(4)]
        sqt = [tile(512, F32, "sqt") for _ in range(2)]
        rstd = tile(512, F32, "rstd")
        pipe = Pipe(3)
        tcount = [0]

        def postnorm_residual(n, nb, gcol):
            pss, pssk = nps()
            for kc in range(KC):
                sq = sqt[kc % 2]
                S.add("act", (lambda kc_, sq_: (lambda e: e.activation(sq_.ap[:, 0:n], mixT.ap[:, kc_ * n:(kc_ + 1) * n], AF.Square)))(kc, sq), reads=[(mixT.key, kc)], writes=[sq.key])
                S.add("pe", (lambda kc_, sq_: (lambda e: e.matmul(pss[:, 0:n], ones_f.ap, sq_.ap[:, 0:n], start=(kc_ == 0), stop=(kc_ == KC - 1))))(kc, sq), reads=[ones_f.key, sq.key], writes=[pssk])
            S.add("act", lambda e: e.activation(rstd.ap[:, 0:n], pss[:, 0:n], AF.Sqrt, bias=pcc(PC_ZERO + 1), scale=1.0 / D), reads=[pssk, pc.key], writes=[rstd.key])
            S.add("dve", lambda e: e.reciprocal(rstd.ap[:, 0:n], rstd.ap[:, 0:n]), reads=[rstd.key], writes=[rstd.key])
            for kc in range(KC):
                S.add("dve", (lambda kc_: (lambda e: e.scalar_tensor_tensor(mixT.ap[:, kc_ * n:(kc_ + 1) * n], mixT.ap[:, kc_ * n:(kc_ + 1) * n], pcc(gcol + kc_), rstd.ap[:, 0:n], ALU.mult, ALU.mult)))(kc),
                      reads=[(mixT.key, kc), pc.key, rstd.key], writes=[(mixT.key, kc)])
            for b in range(nb):
                for g in range(4):
                    p, pk = nps()
                    for j in range(4):
                        kc = g * 4 + j
                        S.add("pe", (lambda j_, kc_, p_: (lambda e: e.transpose(p_[:, j_ * 128:(j_ + 1) * 128], mixT.ap[:, kc_ * n + b * 128: kc_ * n + (b + 1) * 128], ident_f.ap)))(j, kc, p),
                              reads=[(mixT.key, kc), ident_f.key], writes=[pk])
                    xs = xt.ap[:, b * D + g * 512: b * D + (g + 1) * 512]
                    S.add("dve", (lambda xs_, p_: (lambda e: e.tensor_tensor(xs_, xs_, p_[:, 0:512], ALU.add)))(xs, p), reads=[pk, (xt.key, b, g)], writes=[(xt.key, b, g)])

        def phaseD_tile(kind, ti):
            if kind == "main":
                xsrc, psrc, t0, n, tg0, ydst = x_main, p_main, ti * 512, 512, HALF + ti * 512, y_main
            else:
                xsrc, psrc, t0, n, tg0, ydst = x_smp, p_smp, 0, NSEQ * LS, TP, y_smp
            nb = n // 128
            tq0 = tg0 - HALF
            hmT = lambda kc: hid.ap[:, kc * n:(kc + 1) * n]
            hsT = lambda kc: hid.ap[:, 8 * n + kc * n: 8 * n + (kc + 1) * n]
            uT = lambda kc: hid.ap[:, 16 * n + kc * n: 16 * n + (kc + 1) * n]
            S.add("sp", lambda e: e.dma_start(out=xt.ap[:, 0:nb * D].rearrange("p (b d) -> p b d", b=nb), in_=xsrc[t0:t0 + n, :].rearrange("(b p) d -> p b d", p=128)), writes=[xt.key], dma=True)
            S.add("sp", lambda e: e.dma_start(out=hm_tm.ap[:, 0:nb * 1024].rearrange("p (b f) -> p b f", b=nb), in_=HM[tg0:tg0 + n, :].rearrange("(b p) f -> p b f", p=128)), writes=[hm_tm.key], dma=True)
            S.add("sp", lambda e: e.dma_start(out=hid.ap[:, 8 * n:16 * n].rearrange("p (h t) -> p h t", h=8), in_=HST[:, :, tq0:tq0 + n].rearrange("h p t -> p h t")), writes=[hid.key], dma=True)
            norm_to_hT(xt, nb, hT, PC_GMIX, scr, xn, ss)
            for kc in range(8):
                p, pk = nps()
                pb = psbf(p)
                for b in range(nb):
                    S.add("pe", (lambda b_, pb_: (lambda e: e.transpose(pb_[:, b_ * 128:(b_ + 1) * 128], hm_tm.ap[:, b_ * 1024 + kc * 128: b_ * 1024 + (kc + 1) * 128], ident_b.ap)))(b, pb),
                          reads=[hm_tm.key, ident_b.key], writes=[pk])
                evac_copy(kc, hmT(kc), pb[:, 0:n], [pk], [(hid.key, "hm", kc)])

            hk = lambda kc: hT.ap[:, kc * n:(kc + 1) * n]

            def merge_step(cg):
                def comp(vs, key):
                    wA, wB, wa, wb_ = vs
                    pA = nps(); pB = nps(); pa = nps(); pb2 = nps()
                    for kc in range(KC):
                        S.add("pe", (lambda kc_: (lambda e: e.matmul(pA[0][:, 0:n], wA[:, kc_, :], hk(kc_), start=(kc_ == 0), stop=(kc_ == KC - 1))))(kc), reads=[(key,), (hT.key, kc)], writes=[pA[1]])
                    for kc in range(KC):
                        S.add("pe", (lambda kc_: (lambda e: e.matmul(pB[0][:, 0:n], wB[:, kc_, :], hk(kc_), start=(kc_ == 0), stop=(kc_ == KC - 1))))(kc), reads=[(key,), (hT.key, kc)], writes=[pB[1]])
                    for kc in range(8):
                        S.add("pe", (lambda kc_: (lambda e: e.matmul(pa[0][:, 0:n], wa[:, kc_, :], hmT(kc_), start=(kc_ == 0), stop=(kc_ == 7))))(kc), reads=[(key,), (hid.key, "hm", kc)], writes=[pa[1]])
                    for kc in range(8):
                        S.add("pe", (lambda kc_: (lambda e: e.matmul(pb2[0][:, 0:n], wb_[:, kc_, :], hsT(kc_), start=(kc_ == 0), stop=(kc_ == 7))))(kc), reads=[(key,), (hid.key, "hs")], writes=[pb2[1]])
                    i = (cg % 2) * 2
                    sA = tmpf[i]; sB = tmpf[i + 1]
                    S.add("act", lambda e: e.activation(sA.ap[:, 0:n], pA[0][:, 0:n], AF.Sigmoid), reads=[pA[1]], writes=[sA.key])
                    S.add("act", lambda e: e.activation(sB.ap[:, 0:n], pB[0][:, 0:n], AF.Sigmoid), reads=[pB[1]], writes=[sB.key])
                    S.add("dve", lambda e: e.tensor_tensor(sA.ap[:, 0:n], sA.ap[:, 0:n], pa[0][:, 0:n], ALU.mult), reads=[sA.key, pa[1]], writes=[sA.key])
                    S.add("dve", lambda e: e.tensor_tensor(sB.ap[:, 0:n], sB.ap[:, 0:n], pb2[0][:, 0:n], ALU.mult), reads=[sB.key, pb2[1]], writes=[sB.key])
                    S.add("pool", lambda e: e.tensor_tensor(uT(cg), sA.ap[:, 0:n], sB.ap[:, 0:n], ALU.add), reads=[sA.key, sB.key], writes=[(hid.key, "u", cg)])
                return comp
            for cg in range(16):
                pipe.step([(0, (KC, 128), wsrc(wb_in, C_GA + cg * 128, 128)), (2048, (KC, 128), wsrc(wb_in, C_GB + cg * 128, 128)),
                           (4096, (8, 128), wsrc(wb_bra, cg * 128, 128, 8)), (5120, (8, 128), wsrc(wb_brb, cg * 128, 128, 8))], merge_step(cg))

            def proj_step(g, src_fn, src_key_fn, nk, evac_fn):
                def comp(vs, key):
                    for j in range(4):
                        cg = g * 4 + j
                        p = nps()
                        for kc in range(nk):
                            S.add("pe", (lambda kc_: (lambda e: e.matmul(p[0][:, 0:n], vs[0][:, kc_, j * 128:(j + 1) * 128], src_fn(kc_), start=(kc_ == 0), stop=(kc_ == nk - 1))))(kc),
                                  reads=[(key,), src_key_fn(kc)], writes=[p[1]])
                        evac_fn(cg, p)
                return comp

            def ev_mix(cg, p):
                evac_copy(cg, mixT.ap[:, cg * n:(cg + 1) * n], p[0][:, 0:n], [p[1]], [(mixT.key, cg)])
            for g in range(4):
                pipe.step([(0, (KC, 512), wsrc(wb_out, g * 512, 512))], proj_step(g, uT, lambda kc: (hid.key, "u", kc), KC, ev_mix))
            pipe.run()
            if _dstop == "1":
                return
            postnorm_residual(n, nb, PC_POMIX)
            if _dstop == "2":
                return

            norm_to_hT(xt, nb, hT, PC_GMLP, scr, xn, ss)
            for half in range(2):
                def ev_up(fc, p):
                    r = tmpf[fc % 4]
                    S.add("act", lambda e: e.activation(r.ap[:, 0:n], p[0][:, 0:n], AF.Relu), reads=[p[1]], writes=[r.key])
                    S.add("pool", lambda e: e.tensor_tensor(hid.ap[:, fc * n:(fc + 1) * n], r.ap[:, 0:n], r.ap[:, 0:n], ALU.mult), reads=[r.key], writes=[(hid.key, "f", fc)])
                for g in range(NFC // 4):
                    pipe.step([(0, (KC, 512), wsrc(wb_up, half * FH + g * 512, 512))], proj_step(g, hk, lambda kc: (hT.key, kc), KC, ev_up))

                def down_step(cgp, half_):
                    def comp(vs, key):
                        for j in range(2):
                            cg = cgp * 2 + j
                            p = nps()
                            for fc in range(NFC):
                                S.add("pe", (lambda fc_: (lambda e: e.matmul(p[0][:, 0:n], vs[0][:, fc_, j * 128:(j + 1) * 128], hid.ap[:, fc_ * n:(fc_ + 1) * n], start=(fc_ == 0), stop=(fc_ == NFC - 1))))(fc),
                                      reads=[(key,), (hid.key, "f", fc)], writes=[p[1]])
                            mv = mixT.ap[:, cg * n:(cg + 1) * n]
                            if half_ == 0:
                                evac_copy(cg, mv, p[0][:, 0:n], [p[1]], [(mixT.key, cg)])
                            else:
                                S.add("dve", (lambda mv_, p_: (lambda e: e.tensor_tensor(mv_, mv_, p_[0][:, 0:n], ALU.add)))(mv, p), reads=[p[1], (mixT.key, cg)], writes=[(mixT.key, cg)])
                    return comp
                nfl = NFC
                wcols = min(256, SLOT // nfl)
                assert wcols == 256
                for cgp in range(8):
                    src = wb_down[half * FH: half * FH + FH, cgp * 256:(cgp + 1) * 256].rearrange("(k p) c -> p k c", p=128)
                    pipe.step([(0, (NFC, 256), src)], down_step(cgp, half))
                pipe.run()
            postnorm_residual(n, nb, PC_POMLP)
            if _dstop == "3":
                return

            norm_to_hT(xt, nb, hT, PC_GPLE, scr, xn, ss)
            pf = scr
            S.add("sp", lambda e: e.dma_start(out=pf.ap[:, 0:nb * 256].rearrange("p (b f) -> p b f", b=nb), in_=psrc[t0:t0 + n, :].rearrange("(b p) f -> p b f", p=128)), writes=[pf.key], dma=True)
            S.add("dve", lambda e: e.tensor_copy(xn.ap[:, 0:nb * 256], pf.ap[:, 0:nb * 256]), reads=[pf.key], writes=[xn.key])
            pT = lambda kc: hid.ap[:, kc * n:(kc + 1) * n]
            for kc in range(2):
                p, pk = nps()
                pb = psbf(p)
                for b in range(nb):
                    S.add("pe", (lambda b_, pb_: (lambda e: e.transpose(pb_[:, b_ * 128:(b_ + 1) * 128], xn.ap[:, b_ * 256 + kc * 128: b_ * 256 + (kc + 1) * 128], ident_b.ap)))(b, pb),
                          reads=[xn.key, ident_b.key], writes=[pk])
                evac_copy(kc, pT(kc), pb[:, 0:n], [pk], [(hid.key, "f", kc)])

            if _dstop == "4":
                return

            def ple_step(cgp):
                def comp(vs, key):
                    wg, wp = vs
                    for j in range(2):
                        cg = cgp * 2 + j
                        pg = nps(); pp = nps()
                        for kc in range(KC):
                            S.add("pe", (lambda kc_: (lambda e: e.matmul(pg[0][:, 0:n], wg[:, kc_, j * 128:(j + 1) * 128], hk(kc_), start=(kc_ == 0), stop=(kc_ == KC - 1))))(kc), reads=[(key,), (hT.key, kc)], writes=[pg[1]])
                        for kc in range(2):
                            S.add("pe", (lambda kc_: (lambda e: e.matmul(pp[0][:, 0:n], wp[:, kc_, j * 128:(j + 1) * 128], pT(kc_), start=(kc_ == 0), stop=(kc_ == 1))))(kc), reads=[(key,), (hid.key, "f", kc)], writes=[pp[1]])
                        sg = tmpf[cg % 4]
                        S.add("act", lambda e: e.activation(sg.ap[:, 0:n], pg[0][:, 0:n], AF.Sigmoid), reads=[pg[1]], writes=[sg.key])
                        S.add("dve", lambda e: e.tensor_tensor(mixT.ap[:, cg * n:(cg + 1) * n], sg.ap[:, 0:n], pp[0][:, 0:n], ALU.mult), reads=[sg.key, pp[1]], writes=[(mixT.key, cg)])
                return comp
            for cgp in range(8):
                pipe.step([(0, (KC, 256), wsrc(wb_pg, cgp * 256, 256)), (4096, (2, 256), wsrc(wb_ple, cgp * 256, 256, 2))], ple_step(cgp))
            pipe.run()
            if _dstop == "5":
                return
            postnorm_residual(n, nb, PC_POPLE)
            if _dstop == "6":
                return
            S.add("pool", lambda e: e.dma_start(out=ydst[t0:t0 + n, :].rearrange("(b p) d -> p b d", p=128), in_=xt.ap[:, 0:nb * D].rearrange("p (b d) -> p b d", b=nb)), reads=[xt.key], writes=[("y", kind, ti)], dma=True)

        for ti in range(NT):
            phaseD_tile("main", ti)
        phaseD_tile("smp", 0)
        S.emit()
    return nc


def host_consts(inputs):
    pcs = np.zeros((128, NPC), np.float32)

    def fm(v):
        return np.ascontiguousarray(v.reshape(16, 128).T)
    pcs[:, PC_GMIX:PC_GMIX + 16] = fm(inputs["g_pre_mix"][0])
    pcs[:, PC_GMLP:PC_GMLP + 16] = fm(inputs["g_pre_mlp"][0])
    pcs[:, PC_GPLE:PC_GPLE + 16] = fm(inputs["g_pre_ple"][0])
    pcs[:, PC_POMIX:PC_POMIX + 16] = fm(inputs["g_post_mix"][0])
    pcs[:, PC_POMLP:PC_POMLP + 16] = fm(inputs["g_post_mlp"][0])
    pcs[:, PC_POPLE:PC_POPLE + 16] = fm(inputs["g_post_ple"][0])
    for j in range(4):
        pcs[:, PC_CW + j * 16: PC_CW + (j + 1) * 16] = fm(inputs["conv_w"][0, j])
    pcs[:, PC_CB:PC_CB + 16] = fm(inputs["conv_b"][0])
    pcs[:, PC_ZERO] = 0.0
    pcs[:, PC_ZERO + 1] = EPS
    pcs[:, PC_ZERO + 2] = 1.0
    c = dict(
        mlg=np.ascontiguousarray(inputs["ml_norm"][0:1]).astype(np.float32),
        bifrow=np.ascontiguousarray(inputs["b_if"][0:1]).astype(np.float32),
        ident=np.eye(128, dtype=np.float32),
        ones=np.ones((128, 128), np.float32),
    )
    jj, ss_ = np.meshgrid(np.arange(128), np.arange(128), indexing="ij")
    c["ucm"] = (jj > ss_).astype(np.float32)
    sbm = np.zeros((128, 4, 512), np.float32)
    s_idx = np.arange(128)[:, None]
    t_idx = np.arange(512)[None, :]
    for j in range(4):
        sbm[:, j, :] = ((128 * j + s_idx) < t_idx).astype(np.float32)
    c["sbmask"] = sbm.reshape(128, 2048)
    s64, t64 = np.meshgrid(np.arange(64), np.arange(64), indexing="ij")
    c["tri"] = (s64 <= t64).astype(np.float32)
    sel = np.zeros((64, 128), np.float32)
    sel[63, :] = 1.0
    c["sel63"] = sel
    c["i4"] = np.tile(np.eye(64, dtype=np.float32), (1, 4))
    c["negm"] = np.tile(np.where(t64.T >= s64.T, 0.0, 0.0), (1, 4)).astype(np.float32)
    tt, sss = np.meshgrid(np.arange(64), np.arange(64), indexing="ij")
    c["negm"] = np.tile(np.where(sss <= tt, 0.0, -1e30).astype(np.float32), (1, 4))
    c["mt4"] = np.tile((tt <= sss).astype(np.float32), (1, 4))
    return pcs, c


_NC_CACHE = {}


def kernel(**inputs):
    inputs = {k: np.asarray(v) for k, v in inputs.items()}
    SEQ = inputs["x_prompt"].shape[1]
    PAST = inputs["cache_sb_k"].shape[2]
    DFF = inputs["w_up"].shape[2]
    HALF = SEQ // 2
    cfg = dict(HALF=HALF, PAST=PAST, DFF=DFF)
    key = (HALF, PAST, DFF)
    if key not in _NC_CACHE:
        _NC_CACHE[key] = build(cfg)
    nc = _NC_CACHE[key]
    pcs, consts = host_consts(inputs)
    shared = dict(
        w_in=inputs["w_in"][0], w_bra=inputs["w_br_a"][0], w_brb=inputs["w_br_b"][0], w_out=inputs["w_out"][0],
        w_up=inputs["w_up"][0], w_down=inputs["w_down"][0], w_ple=inputs["w_ple"][0], w_pg=inputs["w_ple_gate"][0],
    )
    shared.update(consts)
    in_maps = []
    for c in range(8):
        seq, half = c // 2, c % 2
        s0 = 4 * c
        p = pcs.copy()
        p[:, PC_FLAG] = float(half)
        p[:, PC_NEGB] = 0.0 if half else NEG
        m = dict(shared)
        m.update(
            x_ctx=inputs["x_prompt"][seq, 0:HALF], x_main=inputs["x_prompt"][seq, half * HALF:(half + 1) * HALF],
            x_smp=inputs["x_sample"][s0:s0 + 4].reshape(NSEQ * LS, D),
            p_main=inputs["p_prompt"][0, seq, half * HALF:(half + 1) * HALF], p_smp=inputs["p_sample"][0, s0:s0 + 4].reshape(NSEQ * LS, 256),
            cache_k=inputs["cache_sb_k"][0, s0:s0 + 4].reshape(NSEQ, PAST, 1024), cache_v=inputs["cache_sb_v"][0, s0:s0 + 4].reshape(NSEQ, PAST, 1024),
            st_conv=inputs["state_conv"][0, s0:s0 + 4].reshape(NSEQ * 3, D), st_C=inputs["state_mlstm_C"][0, s0:s0 + 4],
            st_n=inputs["state_mlstm_n"][0, s0:s0 + 4], st_m=inputs["state_mlstm_m"][0, s0:s0 + 4].reshape(1, NSEQ * 4), pc=p,
        )
        in_maps.append({k: np.ascontiguousarray(v, dtype=np.float32) for k, v in m.items()})
    import os as _os
    ncores = int(_os.environ.get("K_CORES", "8"))
    res = run_bass_kernel_spmd(nc, in_maps[:ncores], core_ids=list(range(ncores)))
    R = list(res.results)
    while len(R) < 8:
        R.append({k: np.zeros_like(v) for k, v in R[0].items()})
    B = inputs["x_prompt"].shape[0]
    f = np.float32
    yp = np.stack([np.concatenate([R[2 * b]["y_main"], R[2 * b + 1]["y_main"]], 0) for b in range(B)]).astype(f)
    ys = np.concatenate([R[c]["y_smp"].reshape(4, LS, D) for c in range(8)], 0).astype(f)
    pk = np.stack([np.concatenate([R[2 * b]["k_main"], R[2 * b + 1]["k_main"]], 0) for b in range(B)]).reshape(1, B, SEQ, 8, 128).astype(f)
    pv = np.stack([np.concatenate([R[2 * b]["v_main"], R[2 * b + 1]["v_main"]], 0) for b in range(B)]).reshape(1, B, SEQ, 8, 128).astype(f)
    pconv = np.stack([R[2 * b + 1]["conv_main"] for b in range(B)])[None].astype(f)
    pC = np.stack([R[2 * b + 1]["C_main"] for b in range(B)])[None].astype(f)
    pn = np.stack([R[2 * b + 1]["n_main"] for b in range(B)])[None].astype(f)
    pm = np.stack([R[2 * b + 1]["m_main"].reshape(4) for b in range(B)])[None].astype(f)
    sk = np.concatenate([R[c]["k_smp"].reshape(4, LS, 8, 128) for c in range(8)], 0)[None].astype(f)
    sv = np.concatenate([R[c]["v_smp"].reshape(4, LS, 8, 128) for c in range(8)], 0)[None].astype(f)
    sconv = np.concatenate([R[c]["conv_smp"].reshape(4, 3, D) for c in range(8)], 0)[None].astype(f)
    sC = np.concatenate([R[c]["C_smp"] for c in range(8)], 0)[None].astype(f)
    sn = np.concatenate([R[c]["n_smp"] for c in range(8)], 0)[None].astype(f)
    sm = np.concatenate([R[c]["m_smp"].reshape(4, 4) for c in range(8)], 0)[None].astype(f)
    return (yp, ys, pk, pv, pconv, pC, pn, pm, sk, sv, sconv, sC, sn, sm)
```

```python
import contextlib
import numpy as np
import concourse.bass as bass
import concourse.mybir as mybir
from concourse.bass_utils import run_bass_kernel_spmd

F32 = mybir.dt.float32
BF16 = mybir.dt.bfloat16
ALU = mybir.AluOpType
AF = mybir.ActivationFunctionType
AX = mybir.AxisListType

ENGS = ("pe", "act", "dve", "pool", "sp")
NDMA_SEM = 6
SAME_ENGINE_SYNC = True
BIG_FREE = 256


class Op:
    __slots__ = ("eng", "fn", "deps", "dma", "marked", "idx", "sem", "cnt", "ring_wait", "pos", "big")

    def __init__(self, eng, fn, dma):
        self.eng = eng
        self.fn = fn
        self.dma = dma
        self.deps = []
        self.marked = False
        self.idx = 0
        self.sem = None
        self.cnt = 0
        self.ring_wait = None
        self.big = False


def _os_dbg():
    import os
    return bool(os.environ.get("K_DBG"))


class _Rec:
    def __init__(self):
        self.call = None

    def __getattr__(self, name):
        def f(*a, **k):
            assert self.call is None
            self.call = (name, a, k)
            return self
        return f


class Sched:
    def __init__(self, nc):
        self.nc = nc
        self.ops = {e: [] for e in ENGS}
        self.bufs = {}
        self.dmas = {e: [] for e in ENGS}
        self.all_dmas = []
        self.npos = 0

    def _conf(self, key):
        root, rest = key[0], tuple(key[1:])
        tab = self.bufs.setdefault(root, {})
        out = []
        for r2, ent in tab.items():
            n = min(len(rest), len(r2))
            if rest[:n] == r2[:n]:
                out.append(ent)
        return tab, rest, out

    def add(self, eng, fn, reads=(), writes=(), dma=False):
        rec = _Rec()
        fn(rec)
        op = Op(eng, rec.call, dma)
        if not dma and eng in ("act", "dve"):
            _n, _a, _k = rec.call
            _o = _k.get("out", _a[0] if _a else None)
            try:
                op.big = _o.free_size() >= BIG_FREE
            except Exception:
                op.big = False
        deps = {}
        rk = [k if isinstance(k, tuple) else (k,) for k in reads]
        wk = [k if isinstance(k, tuple) else (k,) for k in writes]
        pk_ = [("ps", k[1]) for k in rk + wk if k[0] == "ps"]
        if pk_:
            rk = [k for k in rk if k[0] != "ps"]
            wk = [k for k in wk if k[0] != "ps"] + list(dict.fromkeys(pk_))
        for k in rk:
            tab, rest, ents = self._conf(k)
            for ent in ents:
                if ent[0] is not None:
                    deps[id(ent[0])] = ent[0]
        for k in wk:
            tab, rest, ents = self._conf(k)
            for ent in ents:
                if ent[0] is not None:
                    deps[id(ent[0])] = ent[0]
                for r in ent[1]:
                    deps[id(r)] = r
        for k in rk:
            tab, rest, ents = self._conf(k)
            ent = tab.get(rest)
            if ent is None:
                ent = [None, []]
                tab[rest] = ent
            ent[1].append(op)
        for k in wk:
            tab, rest, ents = self._conf(k)
            for r2 in [r2 for r2 in tab if r2 != rest and r2[:len(rest)] == rest]:
                del tab[r2]
            tab[rest] = [op, []]
        deps.pop(id(op), None)
        self._finish_add(op, list(deps.values()))
        return op

    def _finish_add(self, op, deps):
        eng = op.eng
        keep = []
        latest = {}
        rest = []
        for d in deps:
            if d.dma:
                rest.append(d)
            elif d.eng not in latest or latest[d.eng].pos < d.pos:
                latest[d.eng] = d
        deps = rest + list(latest.values())
        op.pos = self.npos
        self.npos += 1
        for d in deps:
            if d.dma:
                keep.append(d)
            else:
                if d.eng == eng and (eng == "pe" or not SAME_ENGINE_SYNC):
                    continue
                if d.eng == eng and eng in ("act", "dve") and d.big and op.big:
                    continue
                d.marked = True
                keep.append(d)
        op.deps = keep
        self.ops[eng].append(op)
        if op.dma:
            lst = self.dmas[eng]
            i = len(lst)
            if i >= NDMA_SEM:
                op.ring_wait = lst[i - NDMA_SEM]
            lst.append(op)
            self.all_dmas.append(op)

    def barrier(self):
        last = []
        for e in ENGS:
            for o in reversed(self.ops[e]):
                if not o.dma:
                    last.append(o)
                    break
        last += self.all_dmas
        self.all_dmas = []
        for e in ENGS:
            op = Op(e, None, False)
            self._finish_add(op, [d for d in last if d.dma or d.eng != e])
        self.bufs = {}

    def emit(self):
        nc = self.nc
        with contextlib.ExitStack() as st:
            esem = {e: st.enter_context(nc.semaphore("s_" + e)) for e in ENGS}
            dsem = {}
            for e in ENGS:
                if self.dmas[e]:
                    dsem[e] = [st.enter_context(nc.semaphore("d_%s%d" % (e, i))) for i in range(NDMA_SEM)]
            for e in ENGS:
                n = 0
                for op in self.ops[e]:
                    if op.dma:
                        continue
                    if op.marked:
                        n += 1
                        op.idx = n
                if _os_dbg():
                    print("SCHED", e, "ops", len(self.ops[e]), "marked", n, "dmas", len(self.dmas[e]), flush=True)
                for i, op in enumerate(self.dmas[e]):
                    op.sem = dsem[e][i % NDMA_SEM]
                    op.cnt = 16 * (i // NDMA_SEM + 1)
            fin = Op("sp", None, False)
            fin.deps = [d for e in ENGS for d in self.dmas[e][-NDMA_SEM:]]
            self.ops["sp"].append(fin)
            st.enter_context(nc.allow_non_contiguous_dma(reason="small strided state/layout DMAs"))
            block = st.enter_context(nc.Block())

            def run(e, eng):
                waited = {}
                for op in self.ops[e]:
                    need = {}
                    ds = list(op.deps)
                    if op.ring_wait is not None:
                        ds.append(op.ring_wait)
                    for d in ds:
                        if d.dma:
                            key = ("d", id(d.sem))
                            sem, val = d.sem, d.cnt
                        else:
                            key = ("e", d.eng)
                            sem, val = esem[d.eng], d.idx
                        if waited.get(key, 0) >= val:
                            continue
                        if key not in need or need[key][1] < val:
                            need[key] = (sem, val)
                    for key, (sem, val) in need.items():
                        eng.wait_ge(sem, val)
                        waited[key] = val
                    if op.fn is None:
                        if op.marked:
                            eng.nop().then_inc(esem[e], 1)
                        continue
                    nm, a_, k_ = op.fn
                    ins = getattr(eng, nm)(*a_, **k_)
                    if op.dma:
                        ins.then_inc(op.sem, 16)
                    elif op.marked:
                        ins.then_inc(esem[e], 1)

            @block.tensor
            def _(eng):
                run("pe", eng)

            @block.scalar
            def _(eng):
                run("act", eng)

            @block.vector
            def _(eng):
                run("dve", eng)

            @block.gpsimd
            def _(eng):
                run("pool", eng)

            @block.sync
            def _(eng):
                run("sp", eng)


D = 2048
KC = 16
MIXA = 1024
NSEQ = 4
LS = 64
EPS = 1e-6
C_QK, C_VM, C_O, C_IF, C_SQ, C_SK, C_SV, C_GA, C_GB = 0, 2048, 3072, 4096, 4104, 5128, 6152, 7176, 9224
D_IN = 11272
SLOT = 8192
NEG = -30000.0

PC_GMIX, PC_GMLP, PC_GPLE, PC_POMIX, PC_POMLP, PC_POPLE = 0, 16, 32, 48, 64, 80
PC_CW, PC_CB, PC_BIF, PC_FLAG, PC_NEGB, PC_ZERO = 96, 160, 176, 177, 178, 179
NPC = 192


def cfg_full():
    return dict(HALF=4096, PAST=4096, DFF=8192)


def build(cfg):
    HALF, PAST, DFF = cfg["HALF"], cfg["PAST"], cfg["DFF"]
    NT = HALF // 512
    TP = 2 * HALF
    T_ALL = TP + NSEQ * LS
    TQ = HALF + NSEQ * LS
    NCc = HALF // 64
    NCp = 2 * NCc
    NCH = NCp + NSEQ
    KCACHE = PAST + LS
    FH = DFF // 2
    NFC = FH // 128

    nc = bass.Bass("TRN2", target_bir_lowering=False)

    def din(name, shape, dt=F32):
        return nc.dram_tensor(name, list(shape), dt, kind="ExternalInput").ap()

    def dout(name, shape, dt=F32):
        return nc.dram_tensor(name, list(shape), dt, kind="ExternalOutput").ap()

    def dscr(name, shape, dt=BF16):
        return nc.dram_tensor(name, list(shape), dt, kind="Internal").ap()

    x_ctx = din("x_ctx", [HALF, D]); x_main = din("x_main", [HALF, D]); x_smp = din("x_smp", [NSEQ * LS, D])
    p_main = din("p_main", [HALF, 256]); p_smp = din("p_smp", [NSEQ * LS, 256])
    cache_k = din("cache_k", [NSEQ, PAST, 1024]); cache_v = din("cache_v", [NSEQ, PAST, 1024])
    st_conv = din("st_conv", [NSEQ * 3, D]); st_C = din("st_C", [NSEQ, 4, 256, 256])
    st_n = din("st_n", [NSEQ, 4, 256]); st_m = din("st_m", [1, NSEQ * 4])
    w_in = din("w_in", [D, D_IN]); w_bra = din("w_bra", [MIXA, D]); w_brb = din("w_brb", [MIXA, D])
    w_out = din("w_out", [D, D]); w_up = din("w_up", [D, DFF]); w_down = din("w_down", [DFF, D])
    w_ple = din("w_ple", [256, D]); w_pg = din("w_pg", [D, D])
    pc_d = din("pc", [128, NPC]); mlg_d = din("mlg", [1, MIXA]); bif_d = din("bifrow", [1, 8])
    ident_d = din("ident", [128, 128]); ucm_d = din("ucm", [128, 128]); ones_d = din("ones", [128, 128])
    sbmask_d = din("sbmask", [128, 4 * 512]); tri_d = din("tri", [64, 64]); sel_d = din("sel63", [64, 128])
    i4_d = din("i4", [64, 256]); negm_d = din("negm", [64, 256]); mt4_d = din("mt4", [64, 256])

    y_main = dout("y_main", [HALF, D]); y_smp = dout("y_smp", [NSEQ * LS, D])
    ok_main = dout("k_main", [HALF, 1024]); ov_main = dout("v_main", [HALF, 1024])
    ok_smp = dout("k_smp", [NSEQ * LS, 1024]); ov_smp = dout("v_smp", [NSEQ * LS, 1024])
    oconv_main = dout("conv_main", [3, D]); oC_main = dout("C_main", [4, 256, 256])
    on_main = dout("n_main", [4, 256]); om_main = dout("m_main", [1, 4])
    oconv_smp = dout("conv_smp", [NSEQ * 3, D]); oC_smp = dout("C_smp", [NSEQ, 4, 256, 256])
    on_smp = dout("n_smp", [NSEQ, 4, 256]); om_smp = dout("m_smp", [1, NSEQ * 4])

    wb_in = dscr("wb_in", [D, D_IN]); wb_bra = dscr("wb_bra", [MIXA, D]); wb_brb = dscr("wb_brb", [MIXA, D])
    wb_out = dscr("wb_out", [D, D]); wb_up = dscr("wb_up", [D, DFF]); wb_down = dscr("wb_down", [DFF, D])
    wb_ple = dscr("wb_ple", [256, D]); wb_pg = dscr("wb_pg", [D, D])
    QKT = dscr("QKT", [16, 128, T_ALL]); VM = dscr("VM", [T_ALL, 1024]); OSG = dscr("OSG", [T_ALL, 1024])
    IFT = dscr("IFT", [T_ALL, 8], F32); HM = dscr("HM", [T_ALL, 1024])
    QTS = dscr("QTS", [8, 128, TQ]); KTP = dscr("KTP", [8, 128, TP]); VP = dscr("VP", [TP, 1024])
    KTC = dscr("KTC", [NSEQ, 8, 128, KCACHE]); VC = dscr("VC", [NSEQ, KCACHE, 1024])
    HST = dscr("HST", [8, 128, TQ])

    S = Sched(nc)
    st = contextlib.ExitStack()
    with st:
        NBIG = 47 * 1024
        big = st.enter_context(nc.sbuf_tensor("big", [128, NBIG], F32))
        pst = [st.enter_context(nc.psum_tensor("ps%d" % i, [128, 512], F32)) for i in range(8)]
        off = [0]
        uid = [0]

        class T:
            pass

        def tile(n, dt=F32, name=None):
            w = n if dt == F32 else (n + 1) // 2
            w = (w + 7) // 8 * 8
            assert off[0] + w <= NBIG, ("SBUF overflow", name, off[0], w)
            ap = big[:, off[0]:off[0] + w]
            off[0] += w
            if dt != F32:
                ap = ap.bitcast(dt)
            ap = ap[:, 0:n]
            t = T()
            t.ap = ap
            uid[0] += 1
            t.key = (name or "t") + str(uid[0])
            return t

        pc = tile(NPC, F32, "pc")
        ident_b = tile(128, BF16, "idb"); ident_f = tile(128, F32, "idf")
        ucm = tile(128, BF16, "ucm"); ones_b = tile(128, BF16, "ones"); ones_f = tile(128, F32, "onesf")
        S.add("sp", lambda e: e.dma_start(out=pc.ap, in_=pc_d), writes=[pc.key], dma=True)
        S.add("sp", lambda e: e.dma_start(out=ident_f.ap, in_=ident_d), writes=[ident_f.key], dma=True)
        S.add("sp", lambda e: e.dma_start(out=ones_f.ap, in_=ones_d), writes=[ones_f.key], dma=True)
        S.add("pool", lambda e: e.dma_start(out=ident_b.ap, in_=ident_d), writes=[ident_b.key], dma=True)
        S.add("pool", lambda e: e.dma_start(out=ucm.ap, in_=ucm_d), writes=[ucm.key], dma=True)
        S.add("pool", lambda e: e.dma_start(out=ones_b.ap, in_=ones_d), writes=[ones_b.key], dma=True)
        base_off = off[0]

        def pcc(c, n=1, p=128):
            return pc.ap[0:p, c:c + n]

        for (src, dst, rows) in ((w_in, wb_in, D), (w_bra, wb_bra, MIXA), (w_brb, wb_brb, MIXA), (w_out, wb_out, D),
                                 (w_up, wb_up, D), (w_down, wb_down, DFF), (w_ple, wb_ple, 256), (w_pg, wb_pg, D)):
            for r0 in range(0, rows, 256):
                S.add("pool", (lambda s_, d_, r_: (lambda e: e.dma_start(out=d_[r_:r_ + 256, :], in_=s_[r_:r_ + 256, :])))(src, dst, r0),
                      writes=[("wscr", dst.name if hasattr(dst, "name") else id(dst), r0)], dma=True)
        for s in range(NSEQ):
            for r0 in range(0, PAST, 512):
                S.add("pool", (lambda s_, r_: (lambda e: e.dma_start(out=VC[s_, r_:r_ + 512, :], in_=cache_v[s_, r_:r_ + 512, :])))(s, r0),
                      writes=[("vc", s, r0)], dma=True)
        S.barrier()
        import os as _os
        _stop = _os.environ.get("K_STOP", "")
        if _stop == "W":
            S.emit()
            return nc

        psi = [0]

        def nps():
            i = psi[0] % 8
            psi[0] += 1
            return pst[i], ("ps", i)

        def psbf(p):
            return p[:, 0:512].bitcast(BF16)

        class Pipe:
            def __init__(self, nbuf=3):
                self.slots = [tile(SLOT, BF16, "wslot") for _ in range(nbuf)]
                self.steps = []

            def step(self, loads, compute):
                self.steps.append((loads, compute))

            def run(self, depth=2):
                _ns = int(_os.environ.get("K_NSTEP", "-1"))
                if _ns >= 0:
                    self.steps = self.steps[:_ns]
                n = len(self.steps)
                nb = len(self.slots)
                views = [None] * n

                def issue(i):
                    sl = self.slots[i % nb]
                    vs = []
                    for (eo, (a, b), src) in self.steps[i][0]:
                        v = sl.ap[:, eo:eo + a * b].rearrange("p (a b) -> p a b", a=a, b=b)
                        vs.append(v)
                        S.add("sp", (lambda v_, s_: (lambda e: e.dma_start(out=v_, in_=s_)))(v, src),
                              writes=[(sl.key,)], dma=True)
                    views[i] = (vs, sl.key)

                for i in range(n + depth):
                    if i < n:
                        issue(i)
                    j = i - depth
                    if 0 <= j < n:
                        self.steps[j][1](views[j][0], views[j][1])
                self.steps = []

        def wsrc(wb, c0, w, kc=KC):
            return wb[0:kc * 128, c0:c0 + w].rearrange("(k p) c -> p k c", p=128)

        def evac_copy(i, out, in_, reads, writes):
            if i % 2 == 0:
                S.add("act", lambda e: e.activation(out, in_, AF.Copy), reads=reads, writes=writes)
            else:
                S.add("dve", lambda e: e.tensor_copy(out, in_), reads=reads, writes=writes)

        def norm_to_hT(xt, nb, hT, gcol, scr, xn, ss):
            n = nb * 128
            _lvl = int(_os.environ.get("K_PRE", "9"))
            if _lvl == 0:
                return
            for b in range(nb):
                xs = xt.ap[:, b * D:(b + 1) * D]
                S.add("act", (lambda xs_, b_: (lambda e: e.activation(scr.ap, xs_, AF.Square, accum_out=ss.ap[:, b_:b_ + 1])))(xs, b),
                      reads=[(xt.key, b)], writes=[scr.key, (ss.key, b)])
                S.add("act", (lambda b_: (lambda e: e.activation(ss.ap[:, b_:b_ + 1], ss.ap[:, b_:b_ + 1], AF.Sqrt, bias=pcc(PC_ZERO + 1), scale=1.0 / D)))(b),
                      reads=[(ss.key, b), pc.key], writes=[(ss.key, b)])
                S.add("dve", (lambda b_: (lambda e: e.reciprocal(ss.ap[:, b_:b_ + 1], ss.ap[:, b_:b_ + 1])))(b),
                      reads=[(ss.key, b)], writes=[(ss.key, b)])
                S.add("act", (lambda xs_, b_: (lambda e: e.activation(xn.ap, xs_, AF.Copy, scale=ss.ap[:, b_:b_ + 1])))(xs, b),
                      reads=[(xt.key, b), (ss.key, b)], writes=[xn.key])
                if _lvl == 1:
                    continue
                for g in range(4):
                    p, pk = nps()
                    pb = psbf(p)
                    for j in range(4):
                        kc = g * 4 + j
                        S.add("pe", (lambda pb_, j_, kc_: (lambda e: e.transpose(pb_[:, j_ * 128:(j_ + 1) * 128], xn.ap[:, kc_ * 128:(kc_ + 1) * 128], ident_b.ap)))(pb, j, kc),
                              reads=[xn.key, ident_b.key], writes=[pk])
                    if _lvl == 2:
                        continue
                    for j in range(4):
                        kc = g * 4 + j
                        o = hT.ap[:, kc * n + b * 128: kc * n + (b + 1) * 128]
                        i_ = pb[:, j * 128:(j + 1) * 128]
                        sc = pcc(gcol + kc)
                        _ev = _os.environ.get("K_EV", "dve")
                        if (j % 2 == 0 and _ev != "dve") or _ev == "act":
                            S.add("act", (lambda o_, i__, sc_: (lambda e: e.activation(o_, i__, AF.Copy, scale=sc_)))(o, i_, sc),
                                  reads=[pk, pc.key], writes=[(hT.key, kc, b)])
                        else:
                            S.add("dve", (lambda o_, i__, sc_: (lambda e: e.tensor_scalar(o_, i__, sc_, None, ALU.mult)))(o, i_, sc),
                                  reads=[pk, pc.key], writes=[(hT.key, kc, b)])

        xt = tile(4 * D, F32, "xt"); hT = tile(16 * 512, BF16, "hT")
        scr = tile(D, F32, "scr"); xn = tile(D, BF16, "xn"); ss = tile(8, F32, "ss")
        ext = [tile(16 + 512, F32, "ext") for _ in range(2)]
        carry = tile(16 * 12 + 128, F32, "carry")
        acc = [tile(512, F32, "acc") for _ in range(2)]
        qko = [tile(512, BF16, "qko") for _ in range(2)]
        stg_f = [tile(4 * 1024, F32, "stgf") for _ in range(2)]
        stg_b = [tile(4 * 1024, BF16, "stgb") for _ in range(2)]
        kts = [tile(512, BF16, "kts") for _ in range(2)]
        ifs = tile(4 * 8, F32, "ifs")
        bifb = tile(8, F32, "bifb"); cvt = tile(16 * 12, F32, "cvt")
        cstg = tile(D, F32, "cstg")
        pipe = Pipe(3)
        S.add("sp", lambda e: e.dma_start(out=bifb.ap, in_=bif_d.partition_broadcast(128)), writes=[bifb.key], dma=True)
        S.add("pool", lambda e: e.memset(carry.ap, 0.0), writes=[carry.key])
        cnt = [0]

        def k_transposes(kbf, nb, dest_fn):
            for h in range(8):
                p, pk = nps()
                pb = psbf(p)
                for b in range(nb):
                    src, rk = kbf(b, h)
                    S.add("pe", (lambda pb_, b_, src_: (lambda e: e.transpose(pb_[:, b_ * 128:(b_ + 1) * 128], src_, ident_b.ap)))(pb, b, src),
                          reads=[rk, ident_b.key], writes=[pk])
                kt = kts[cnt[0] % 2]
                cnt[0] += 1
                evac_copy(cnt[0], kt.ap[:, 0:nb * 128], pb[:, 0:nb * 128], [pk], [kt.key])
                dest_fn(h, kt)

        def phaseA_tile(kind, ti):
            if kind == "ctx":
                xsrc, t0, n, tg0 = x_ctx, ti * 512, 512, ti * 512
            elif kind == "main":
                xsrc, t0, n, tg0 = x_main, ti * 512, 512, HALF + ti * 512
            else:
                xsrc, t0, n, tg0 = x_smp, 0, NSEQ * LS, TP
            nb = n // 128
            full = kind != "ctx" and "f" not in _os.environ.get("K_SKIP", "")
            def pre(vs, key):
                S.add("sp", lambda e: e.dma_start(out=xt.ap[:, 0:nb * D].rearrange("p (b d) -> p b d", b=nb), in_=xsrc[t0:t0 + n, :].rearrange("(b p) d -> p b d", p=128)),
                      writes=[xt.key], dma=True)
                norm_to_hT(xt, nb, hT, PC_GMIX, scr, xn, ss)
            pipe.step([], pre)

            def hTs(kc, c0=0, w=None):
                w = n if w is None else w
                return hT.ap[:, kc * n + c0: kc * n + c0 + w]

            def fm_group(wv, j, p):
                for kc in range(KC):
                    S.add("pe", (lambda kc_: (lambda e: e.matmul(p[0][:, 0:n], wv[:, kc_, j * 128:(j + 1) * 128], hTs(kc_), start=(kc_ == 0), stop=(kc_ == KC - 1))))(kc),
                          reads=[wv_key[0], (hT.key, kc)], writes=[p[1]])

            def tm_group(wv, b, p, w=512):
                for kc in range(KC):
                    S.add("pe", (lambda kc_: (lambda e: e.matmul(p[0][:, 0:w], hTs(kc_, b * 128, 128), wv[:, kc_, 0:w], start=(kc_ == 0), stop=(kc_ == KC - 1))))(kc),
                          reads=[wv_key[0], (hT.key, kc, b)], writes=[p[1]])

            wv_key = [None]

            def qk_step(g):
                def comp(vs, key):
                    wv_key[0] = (key,)
                    for j in range(4):
                        ch = g * 4 + j
                        p = nps()
                        fm_group(vs[0], j, p)
                        ex = ext[ch % 2]; ac = acc[ch % 2]; qo = qko[ch % 2]
                        if kind != "smp":
                            exv = ex.ap[:, 0:3 + n]
                            S.add("pool", lambda e: e.tensor_copy(ex.ap[:, 0:3], carry.ap[:, ch * 12: ch * 12 + 3]), reads=[(carry.key, ch)], writes=[(ex.key, "h")])
                            S.add("act", lambda e: e.activation(ex.ap[:, 3:3 + n], p[0][:, 0:n], AF.Copy), reads=[p[1]], writes=[(ex.key, "b")])
                            S.add("pool", lambda e: e.tensor_copy(carry.ap[:, ch * 12: ch * 12 + 3], ex.ap[:, n:n + 3]), reads=[(ex.key, "b")], writes=[(carry.key, ch)])
                            sl = lambda jj: ex.ap[:, jj:jj + n]
                            av = ac.ap[:, 0:n]
                        else:
                            ex3 = ex.ap[:, 0:NSEQ * 67].rearrange("p (s t) -> p s t", s=NSEQ)
                            S.add("pool", lambda e: e.tensor_copy(ex3[:, :, 0:3], cvt.ap[:, ch * 12:(ch + 1) * 12].rearrange("p (s t) -> p s t", s=NSEQ)), reads=[(cvt.key, ch)], writes=[(ex.key, "h")])
                            S.add("act", lambda e: e.activation(ex3[:, :, 3:67], p[0][:, 0:n].rearrange("p (s t) -> p s t", s=NSEQ), AF.Copy), reads=[p[1]], writes=[(ex.key, "b")])
                            S.add("pool", lambda e: e.tensor_copy(carry.ap[:, ch * 12:(ch + 1) * 12].rearrange("p (s t) -> p s t", s=NSEQ), ex3[:, :, 64:67]), reads=[(ex.key, "b")], writes=[(carry.key, ch)])
                            sl = lambda jj: ex3[:, :, jj:jj + 64]
                            av = ac.ap[:, 0:n].rearrange("p (s t) -> p s t", s=NSEQ)
                        cw = lambda jj: pcc(PC_CW + jj * 16 + ch)
                        S.add("dve", lambda e: e.tensor_scalar(av, sl(3), cw(3), pcc(PC_CB + ch), ALU.mult, ALU.add), reads=[ex.key, pc.key], writes=[ac.key])
                        for jj in (2, 1, 0):
                            S.add("dve", (lambda jj_: (lambda e: e.scalar_tensor_tensor(av, sl(jj_), cw(jj_), av, ALU.mult, ALU.add)))(jj), reads=[ex.key, pc.key, ac.key], writes=[ac.key])
                        S.add("act", lambda e: e.activation(qo.ap[:, 0:n], ac.ap[:, 0:n], AF.Silu), reads=[ac.key], writes=[qo.key])
                        S.add("pool", lambda e: e.dma_start(out=QKT[ch, :, tg0:tg0 + n], in_=qo.ap[:, 0:n]), reads=[qo.key], writes=[("QKT", ch, tg0)], dma=True)
                return comp

            for g in range(4):
                pipe.step([(0, (KC, 512), wsrc(wb_in, C_QK + g * 512, 512))], qk_step(g))

            def tm_step(c0, half, mode):
                def comp(vs, key):
                    wv_key[0] = (key,)
                    sf = stg_f[0 if mode in ("o", "k") else 1]
                    sb_ = stg_b[0 if mode in ("vm", "k") else 1]
                    for b in range(nb):
                        p = nps()
                        tm_group(vs[0], b, p)
                        cs = slice(b * 1024 + half * 512, b * 1024 + half * 512 + 512)
                        if mode == "vm":
                            evac_copy(b, sb_.ap[:, cs], p[0][:, 0:512], [p[1]], [(sb_.key, b, half)])
                        elif mode == "o":
                            S.add("act", (lambda cs_: (lambda e: e.activation(sb_.ap[:, cs_], p[0][:, 0:512], AF.Sigmoid)))(cs), reads=[p[1]], writes=[(sb_.key, b, half)])
                        else:
                            if full:
                                S.add("dve", (lambda cs_: (lambda e: e.tensor_copy(sf.ap[:, cs_], p[0][:, 0:512])))(cs), reads=[p[1]], writes=[(sf.key, b, half)])
                            S.add("dve", (lambda cs_: (lambda e: e.tensor_copy(sb_.ap[:, cs_], p[0][:, 0:512])))(cs), reads=[p[1]], writes=[(sb_.key, b, half)])
                    if half == 1:
                        sfv = sf.ap[:, 0:nb * 1024].rearrange("p (b f) -> p b f", b=nb)
                        sbv = sb_.ap[:, 0:nb * 1024].rearrange("p (b f) -> p b f", b=nb)
                        rows = lambda dst, r0: dst[r0:r0 + n, :].rearrange("(b p) f -> p b f", p=128)
                        if mode == "vm":
                            S.add("pool", lambda e: e.dma_start(out=rows(VM, tg0), in_=sbv), reads=[sb_.key], writes=[("VM", tg0)], dma=True)
                        elif mode == "o":
                            S.add("pool", lambda e: e.dma_start(out=rows(OSG, tg0), in_=sbv), reads=[sb_.key], writes=[("OSG", tg0)], dma=True)
                        elif mode == "v":
                            if kind == "smp":
                                S.add("pool", lambda e: e.dma_start(out=rows(ov_smp, 0), in_=sfv), reads=[sf.key], writes=[("ovs",)], dma=True)
                                for s_ in range(NSEQ):
                                    S.add("pool", (lambda s__: (lambda e: e.dma_start(out=VC[s__, PAST:PAST + LS, :], in_=sb_.ap[(s__ % 2) * 64:(s__ % 2) * 64 + 64, (s__ // 2) * 1024:(s__ // 2 + 1) * 1024])))(s_),
                                          reads=[sb_.key], writes=[("VCn", s_)], dma=True)
                            else:
                                if full:
                                    S.add("pool", lambda e: e.dma_start(out=rows(ov_main, t0), in_=sfv), reads=[sf.key], writes=[("ovm", t0)], dma=True)
                                S.add("pool", lambda e: e.dma_start(out=rows(VP, tg0), in_=sbv), reads=[sb_.key], writes=[("VP", tg0)], dma=True)
                        elif mode == "k":
                            if kind == "smp":
                                S.add("pool", lambda e: e.dma_start(out=rows(ok_smp, 0), in_=sfv), reads=[sf.key], writes=[("oks",)], dma=True)
                            elif full:
                                S.add("pool", lambda e: e.dma_start(out=rows(ok_main, t0), in_=sfv), reads=[sf.key], writes=[("okm", t0)], dma=True)

                            def dest(h, kt):
                                if kind == "smp":
                                    S.add("pool", lambda e: e.dma_start(out=KTC[:, h, :, PAST:PAST + LS].rearrange("s d t -> d s t"),
                                                                      in_=kt.ap[:, 0:n].rearrange("p (s t) -> p s t", s=NSEQ)),
                                          reads=[kt.key], writes=[("KTCn", h)], dma=True)
                                else:
                                    S.add("pool", lambda e: e.dma_start(out=KTP[h, :, tg0:tg0 + n], in_=kt.ap[:, 0:n]), reads=[kt.key], writes=[("KTP", h, tg0)], dma=True)
                            k_transposes(lambda b, h: (sb_.ap[:, b * 1024 + h * 128: b * 1024 + (h + 1) * 128], (sb_.key, b)), nb, dest)
                return comp

            for half in range(2):
                pipe.step([(0, (KC, 512), wsrc(wb_in, C_VM + half * 512, 512))], tm_step(C_VM, half, "vm"))
            _skip = _os.environ.get("K_SKIP", "")
            if full and "o" not in _skip:
                for half in range(2):
                    pipe.step([(0, (KC, 512), wsrc(wb_in, C_O + half * 512, 512))], tm_step(C_O, half, "o"))
            for half in range(2):
                pipe.step([(0, (KC, 512), wsrc(wb_in, C_SK + half * 512, 512))], tm_step(C_SK, half, "k"))
            for half in range(2):
                pipe.step([(0, (KC, 512), wsrc(wb_in, C_SV + half * 512, 512))], tm_step(C_SV, half, "v"))

            def if_step():
                def comp(vs, key):
                    wv_key[0] = (key,)
                    for b in range(nb):
                        p = nps()
                        tm_group(vs[0], b, p, w=8)
                        S.add("dve", (lambda b_, p_: (lambda e: e.tensor_tensor(ifs.ap[:, b_ * 8:(b_ + 1) * 8], p_[0][:, 0:8], bifb.ap, ALU.add)))(b, p),
                              reads=[p[1], bifb.key], writes=[(ifs.key, b)])
                    S.add("pool", lambda e: e.dma_start(out=IFT[tg0:tg0 + n, :].rearrange("(b p) g -> p b g", p=128), in_=ifs.ap[:, 0:nb * 8].rearrange("p (b g) -> p b g", b=nb)),
                          reads=[ifs.key], writes=[("IFT", tg0)], dma=True)
                return comp
            pipe.step([(0, (KC, 8), wsrc(wb_in, C_IF, 8))], if_step())

            if full and "q" not in _skip:
                def q_step(g):
                    def comp(vs, key):
                        wv_key[0] = (key,)
                        for j in range(4):
                            h = g * 4 + j
                            p = nps()
                            fm_group(vs[0], j, p)
                            qo = qko[h % 2]
                            S.add("act", lambda e: e.activation(qo.ap[:, 0:n], p[0][:, 0:n], AF.Copy, scale=float(128 ** -0.5)), reads=[p[1]], writes=[qo.key])
                            tq0 = tg0 - HALF
                            S.add("pool", lambda e: e.dma_start(out=QTS[h, :, tq0:tq0 + n], in_=qo.ap[:, 0:n]), reads=[qo.key], writes=[("QTS", h, tq0)], dma=True)
                    return comp
                for g in range(2):
                    pipe.step([(0, (KC, 512), wsrc(wb_in, C_SQ + g * 512, 512))], q_step(g))

        def conv_state_out(ncol, dst):
            for g in range(4):
                p, pk = nps()
                for j in range(4):
                    ch = g * 4 + j
                    S.add("pe", (lambda j_, ch_: (lambda e: e.transpose(p[:, j_ * 128:(j_ + 1) * 128], carry.ap[:, ch_ * 12: ch_ * 12 + 128], ident_f.ap)))(j, ch),
                          reads=[carry.key, ident_f.key], writes=[pk])
                evac_copy(g, cstg.ap[0:ncol, g * 512:(g + 1) * 512], p[0:ncol, 0:512], [pk], [(cstg.key, g)])
            S.add("pool", lambda e: e.dma_start(out=dst, in_=cstg.ap[0:ncol, :]), reads=[cstg.key], writes=[("cvo", ncol)], dma=True)

        for s in range(NSEQ):
            for r0 in range(0, PAST, 512):
                sb_ = stg_b[(r0 // 512) % 2]
                S.add("pool", (lambda sb__, s_, r_: (lambda e: e.dma_start(out=sb__.ap[:, 0:4096].rearrange("p (b f) -> p b f", b=4), in_=cache_k[s_, r_:r_ + 512, :].rearrange("(b p) f -> p b f", p=128))))(sb_, s, r0),
                      writes=[sb_.key], dma=True)

                def dest(h, kt, s_=s, r_=r0):
                    S.add("pool", lambda e: e.dma_start(out=KTC[s_, h, :, r_:r_ + 512], in_=kt.ap[:, 0:512]), reads=[kt.key], writes=[("KTC", s_, h, r_)], dma=True)
                k_transposes((lambda sb__: (lambda b, h: (sb__.ap[:, b * 1024 + h * 128: b * 1024 + (h + 1) * 128], (sb__.key,))))(sb_), 4, dest)

        if _stop == "A0":
            S.barrier(); S.emit()
            return nc
        for ti in range(NT):
            phaseA_tile("ctx", ti)
        pipe.run()
        if _stop == "A1":
            S.barrier(); S.emit()
            return nc
        S.add("dve", lambda e: e.tensor_scalar(carry.ap, carry.ap, pcc(PC_FLAG), None, ALU.mult), reads=[carry.key, pc.key], writes=[carry.key])
        for ti in range(NT):
            phaseA_tile("main", ti)
        pipe.run()
        if _stop == "A2":
            S.barrier(); S.emit()
            return nc
        conv_state_out(3, oconv_main)
        if _stop == "A3":
            S.barrier(); S.emit()
            return nc
        S.add("sp", lambda e: e.dma_start(out=cstg.ap[0:12, :], in_=st_conv), writes=[cstg.key], dma=True)
        for g in range(4):
            p, pk = nps()
            for j in range(4):
                ch = g * 4 + j
                S.add("pe", (lambda j_, ch_, p_: (lambda e: e.transpose(p_[:, j_ * 12:(j_ + 1) * 12], cstg.ap[0:12, ch_ * 128:(ch_ + 1) * 128], ident_f.ap[0:12, 0:12])))(j, ch, p),
                      reads=[cstg.key, ident_f.key], writes=[pk])
            evac_copy(g, cvt.ap[:, g * 48:(g + 1) * 48], p[:, 0:48], [pk], [(cvt.key, 4 * g), (cvt.key, 4 * g + 1), (cvt.key, 4 * g + 2), (cvt.key, 4 * g + 3)])
        phaseA_tile("smp", 0)
        pipe.run()
        conv_state_out(12, oconv_smp)
        S.barrier()
        if _stop == "A":
            S.emit()
            return nc

        off[0] = base_off
        tri = tile(64, F32, "tri"); sel = tile(128, F32, "sel"); i4 = tile(256, F32, "i4")
        negm = tile(256, F32, "negm"); mt4 = tile(256, F32, "mt4"); mlg = tile(MIXA, F32, "mlg")
        S.add("sp", lambda e: e.dma_start(out=tri.ap[0:64, :], in_=tri_d), writes=[tri.key], dma=True)
        S.add("sp", lambda e: e.dma_start(out=sel.ap[0:64, :], in_=sel_d), writes=[sel.key], dma=True)
        S.add("sp", lambda e: e.dma_start(out=i4.ap[0:64, :], in_=i4_d), writes=[i4.key], dma=True)
        S.add("sp", lambda e: e.dma_start(out=negm.ap[0:64, :], in_=negm_d), writes=[negm.key], dma=True)
        S.add("sp", lambda e: e.dma_start(out=mt4.ap[0:64, :], in_=mt4_d), writes=[mt4.key], dma=True)
        S.add("sp", lambda e: e.dma_start(out=mlg.ap[0:64, :], in_=mlg_d.partition_broadcast(64)), writes=[mlg.key], dma=True)
        NG = NCH * 4
        ift = tile(NCH * 8, F32, "ift")
        tE = tile(NG, F32, "tE"); tL = tile(NG, F32, "tL"); bc = tile(NG, F32, "bc"); ta = tile(NG, F32, "ta")
        cm = tile(NG, F32, "cm"); Mn = tile(NG, F32, "Mn"); M0 = tile(NG, F32, "M0"); M0b = tile(NG, F32, "M0b")
        tg = tile(NG, F32, "tg"); tmt = tile(NG, F32, "tmt"); tfl = tile(NG, F32, "tfl"); tdec = tile(NG, F32, "tdec")
        cdec = tile(NG, F32, "cdec"); mtb = tile(NG, F32, "mtb"); minit = tile(4, F32, "minit")
        dD = [tile(256, F32, "dD") for _ in range(2)]; tmx = [tile(256, F32, "tmx") for _ in range(2)]
        P64 = slice(0, 64)

        def v3(t, p=64):
            return t.ap[0:p, 0:NG].rearrange("p (c h) -> p c h", h=4)

        ift3 = ift.ap[0:64, :].rearrange("p (c g) -> p c g", g=8)
        for c0 in range(0, NCH, 32):
            c1 = min(NCH, c0 + 32)
            S.add("sp", (lambda c0_, c1_: (lambda e: e.dma_start(out=ift3[:, c0_:c1_, :], in_=IFT[c0_ * 64:c1_ * 64, :].rearrange("(c t) g -> t c g", t=64))))(c0, c1),
                  writes=[(ift.key, c0)], dma=True)
        S.add("act", lambda e: e.activation(v3(tE), ift3[:, :, 4:8], AF.Exp, scale=-1.0), reads=[ift.key], writes=[tE.key])
        S.add("act", lambda e: e.activation(tL.ap[P64, 0:NG], tE.ap[P64, 0:NG], AF.Ln, bias=pcc(PC_ZERO + 2, p=64)), reads=[tE.key, pc.key], writes=[tL.key])
        for c0 in range(0, NG, 512):
            c1 = min(NG, c0 + 512)
            p, pk = nps()
            S.add("pe", (lambda c0_, c1_, p_: (lambda e: e.matmul(p_[0:64, 0:c1_ - c0_], tri.ap[0:64, 0:64], tL.ap[0:64, c0_:c1_], start=True, stop=True)))(c0, c1, p),
                  reads=[tri.key, tL.key], writes=[pk])
            S.add("dve", (lambda c0_, c1_, p_: (lambda e: e.tensor_copy(bc.ap[0:64, c0_:c1_], p_[0:64, 0:c1_ - c0_])))(c0, c1, p), reads=[pk], writes=[(bc.key, c0)])
        S.add("dve", lambda e: e.tensor_tensor(v3(ta), ift3[:, :, 0:4], v3(bc), ALU.add), reads=[ift.key, bc.key], writes=[ta.key])
        for c in range(NCH):
            d_ = dD[c % 2]; mx = tmx[c % 2]
            d3 = d_.ap[0:64, :].rearrange("p (h s) -> p h s", h=4)
            S.add("dve", (lambda c_, d3_: (lambda e: e.tensor_tensor(d3_, i4.ap[0:64, :].rearrange("p (h s) -> p h s", h=4), v3(ta)[:, c_, :].unsqueeze(2).to_broadcast([64, 4, 64]), ALU.mult)))(c, d3),
                  reads=[i4.key, ta.key], writes=[d_.key])
            p, pk = nps()
            S.add("pe", (lambda p_, d__: (lambda e: e.matmul(p_[0:64, 0:256], ones_f.ap[0:64, 0:64], d__.ap[0:64, :], start=True, stop=True)))(p, d_), reads=[ones_f.key, d_.key], writes=[pk])
            S.add("dve", (lambda p_, mx_: (lambda e: e.tensor_tensor(mx_.ap[0:64, :], p_[0:64, 0:256], negm.ap[0:64, :], ALU.add)))(p, mx), reads=[pk, negm.key], writes=[mx.key])
            S.add("dve", (lambda c_, mx_: (lambda e: e.tensor_reduce(v3(cm)[:, c_, :], mx_.ap[0:64, :].rearrange("p (h s) -> p h s", h=4), AX.X, ALU.max)))(c, mx), reads=[mx.key], writes=[(cm.key, c)])
        S.add("pool", lambda e: e.memset(M0.ap[0:64, :], 0.0), writes=[M0.key])
        S.add("pool", lambda e: e.memset(Mn.ap[0:64, :], 0.0), writes=[Mn.key])
        R = slice(32, 64)
        negb = tile(NG, F32, "negb")
        S.add("dve", lambda e: e.tensor_scalar(negb.ap[0:64, 0:NG], bc.ap[0:64, 0:NG], -1.0, None, ALU.mult), reads=[bc.key], writes=[negb.key])

        def v3r(t):
            return t.ap[32:64, 0:NG].rearrange("p (c h) -> p c h", h=4)
        for h in range(4):
            S.add("dve", (lambda h_: (lambda e: e.tensor_tensor_scan(v3r(Mn)[:, 0:NCc, h_], v3r(cm)[:, 0:NCc, h_], v3r(negb)[:, 0:NCc, h_], 0.0, ALU.max, ALU.add)))(h),
                  reads=[cm.key, negb.key], writes=[(Mn.key, "c", h)])
        S.add("dve", lambda e: e.tensor_scalar(minit.ap[32:64, 0:4], v3r(Mn)[:, NCc - 1, :], pcc(PC_FLAG)[32:64, :], None, ALU.mult), reads=[Mn.key, pc.key], writes=[minit.key])
        for h in range(4):
            S.add("dve", (lambda h_: (lambda e: e.tensor_tensor_scan(v3r(Mn)[:, NCc:NCp, h_], v3r(cm)[:, NCc:NCp, h_], v3r(negb)[:, NCc:NCp, h_], minit.ap[32:64, h_:h_ + 1], ALU.max, ALU.add)))(h),
                  reads=[cm.key, negb.key, minit.key], writes=[(Mn.key, "m", h)])
        S.add("dve", lambda e: e.tensor_copy(v3r(M0)[:, 1:NCc, :], v3r(Mn)[:, 0:NCc - 1, :]), reads=[Mn.key], writes=[M0.key])
        S.add("dve", lambda e: e.tensor_copy(v3r(M0)[:, NCc, :], minit.ap[32:64, 0:4]), reads=[minit.key], writes=[M0.key])
        S.add("dve", lambda e: e.tensor_copy(v3r(M0)[:, NCc + 1:NCp, :], v3r(Mn)[:, NCc:NCp - 1, :]), reads=[Mn.key], writes=[M0.key])
        S.add("sp", lambda e: e.dma_start(out=M0.ap[0:64, NCp * 4:NG], in_=st_m.partition_broadcast(64)), reads=[M0.key], writes=[M0.key], dma=True)

        def selmm(dst, src, npart):
            for c0 in range(0, NG, 512):
                c1 = min(NG, c0 + 512)
                p, pk = nps()
                S.add("pe", (lambda c0_, c1_, p_: (lambda e: e.matmul(p_[0:npart, 0:c1_ - c0_], sel.ap[0:64, 0:npart], src.ap[0:64, c0_:c1_], start=True, stop=True)))(c0, c1, p),
                      reads=[sel.key, src.key], writes=[pk])
                S.add("dve", (lambda c0_, c1_, p_: (lambda e: e.tensor_copy(dst.ap[0:npart, c0_:c1_], p_[0:npart, 0:c1_ - c0_])))(c0, c1, p), reads=[pk], writes=[(dst.key, c0)])
        selmm(M0b, M0, 64)
        A64 = lambda t: t.ap[0:64, 0:NG]
        S.add("dve", lambda e: e.tensor_tensor(A64(tg), A64(cm), A64(M0b), ALU.max), reads=[cm.key, M0b.key], writes=[tg.key])
        S.add("dve", lambda e: e.tensor_tensor(A64(tmt), A64(tg), A64(bc), ALU.subtract), reads=[tg.key, bc.key], writes=[tmt.key])
        S.add("act", lambda e: e.activation(A64(tfl), A64(tmt), AF.Exp, scale=-1.0), reads=[tmt.key], writes=[tfl.key])
        S.add("dve", lambda e: e.tensor_tensor(A64(tdec), A64(M0b), A64(tg), ALU.subtract), reads=[tg.key, M0b.key], writes=[tdec.key])
        S.add("act", lambda e: e.activation(A64(tdec), A64(tdec), AF.Exp), reads=[tdec.key], writes=[tdec.key])
        selmm(cdec, tdec, 128)
        selmm(mtb, tmt, 64)
        S.add("sp", lambda e: e.dma_start(out=om_main, in_=mtb.ap[0:1, (NCp - 1) * 4:NCp * 4]), reads=[mtb.key], dma=True)
        S.add("sp", lambda e: e.dma_start(out=om_smp, in_=mtb.ap[0:1, NCp * 4:NG]), reads=[mtb.key], dma=True)

        if _stop == "B0":
            S.barrier(); S.emit()
            return nc
        qkt = [tile(16 * 512, BF16, "qkt") for _ in range(2)]
        vt = [tile(8 * 1024, BF16, "vt") for _ in range(2)]
        ost = [tile(8 * 1024, BF16, "ost") for _ in range(2)]
        Cst = tile(4 * 512, F32, "Cst"); Cbf = tile(4 * 512, BF16, "Cbf"); nst = tile(8, F32, "nst"); nbf = tile(8, BF16, "nbf")
        wT = [tile(256, F32, "wT") for _ in range(2)]; SW = [tile(256, BF16, "SW") for _ in range(2)]
        wk16 = [tile(4, F32, "wk16") for _ in range(2)]; kw = [tile(1024, BF16, "kw") for _ in range(2)]
        intra = tile(1024, F32, "intra"); num = tile(1024, F32, "num"); hg = tile(1024, F32, "hg")
        dn = tile(8, F32, "dn"); rr = tile(4, F32, "rr"); ssq = tile(4, F32, "ssq"); sq2 = tile(256, F32, "sq2")
        hmo = [tile(1024, BF16, "hmo") for _ in range(2)]
        S.add("pool", lambda e: e.memset(Cst.ap, 0.0), writes=[Cst.key])
        S.add("pool", lambda e: e.memset(nst.ap, 0.0), writes=[nst.key])
        S.add("pool", lambda e: e.memset(Cbf.ap, 0.0), writes=[Cbf.key])
        S.add("pool", lambda e: e.memset(nbf.ap, 0.0), writes=[nbf.key])
        B_SG, B_KT, B_I0, B_I1, B_N0, B_N1, B_DEN, B_DC = range(8)
        PK = lambda i: ("ps", i)

        def chunk(c, sc, j, out, qk_, v_, o_):
            cols = slice(j * 64, j * 64 + 64)
            nload = 512 if c < NCp else NSEQ * LS
            qv = lambda ch: qk_.ap[:, ch * nload + j * 64: ch * nload + j * 64 + 64]
            vv = lambda h: v_.ap[0:64, j * 1024 + h * 256: j * 1024 + (h + 1) * 256]
            w_ = wT[c % 2]; sw_ = SW[c % 2]; wk_ = wk16[c % 2]; kw_ = kw[c % 2]; d_ = dD[c % 2]
            d3 = d_.ap[0:64, :].rearrange("p (h s) -> p h s", h=4)
            if _os.environ.get("K_GBF"):
                dbf = d_.ap[0:64, 0:128].bitcast(BF16)
                d3b = dbf.rearrange("p (h s) -> p h s", h=4)
                S.add("dve", lambda e: e.tensor_tensor(d3b, i4.ap[0:64, :].rearrange("p (h s) -> p h s", h=4), v3(tg)[:, c, :].unsqueeze(2).to_broadcast([64, 4, 64]), ALU.mult),
                      reads=[i4.key, tg.key], writes=[d_.key])
                S.add("pe", lambda e: e.matmul(pst[B_SG][0:64, 256:512], ones_b.ap[0:64, 0:64], dbf, start=True, stop=True), reads=[ones_b.key, d_.key], writes=[("ps", B_SG, "g")])
            else:
                S.add("dve", lambda e: e.tensor_tensor(d3, i4.ap[0:64, :].rearrange("p (h s) -> p h s", h=4), v3(tg)[:, c, :].unsqueeze(2).to_broadcast([64, 4, 64]), ALU.mult),
                      reads=[i4.key, tg.key], writes=[d_.key])
                S.add("pe", lambda e: e.matmul(pst[B_SG][0:64, 256:512], ones_f.ap[0:64, 0:64], d_.ap[0:64, :], start=True, stop=True), reads=[ones_f.key, d_.key], writes=[("ps", B_SG, "g")])
            for h in range(4):
                S.add("act", (lambda h_: (lambda e: e.activation(w_.ap[0:64, h_ * 64:(h_ + 1) * 64], pst[B_SG][0:64, 256 + h_ * 64:256 + (h_ + 1) * 64], AF.Exp, bias=v3(ta)[:, c, h_:h_ + 1], scale=-1.0)))(h),
                      reads=[("ps", B_SG, "g"), ta.key], writes=[(w_.key, h)])
            S.add("pool", lambda e: e.tensor_tensor(w_.ap[0:64, :], w_.ap[0:64, :], mt4.ap[0:64, :], ALU.mult), reads=[w_.key, mt4.key], writes=[w_.key])
            S.add("dve", lambda e: e.tensor_scalar(wk_.ap[0:64, 0:4], w_.ap[0:64, :].rearrange("p (h t) -> p h t", h=4)[:, :, 63], 1.0 / 16, None, ALU.mult), reads=[w_.key], writes=[wk_.key])
            _bs = _os.environ.get("K_BSKIP", "")
            if "o" in _bs:
                out = False
            if out:
                for h in range(4):
                    for kc in range(2):
                        S.add("pe", (lambda h_, kc_: (lambda e: e.matmul(pst[B_SG][0:64, h_ * 64:(h_ + 1) * 64], qv(8 + 2 * h_ + kc_), qv(2 * h_ + kc_), start=(kc_ == 0), stop=(kc_ == 1))))(h, kc),
                              reads=[qk_.key], writes=[("ps", B_SG, "s")])
                S.add("dve", lambda e: e.scalar_tensor_tensor(sw_.ap[0:64, :], pst[B_SG][0:64, 0:256], 1.0 / 16, w_.ap[0:64, :], ALU.mult, ALU.mult), reads=[("ps", B_SG, "s"), w_.key], writes=[sw_.key])
                for h in range(4):
                    bi = B_I0 + h // 2
                    S.add("pe", (lambda h_, bi_: (lambda e: e.matmul(pst[bi_][0:64, (h_ % 2) * 256:(h_ % 2 + 1) * 256], sw_.ap[0:64, h_ * 64:(h_ + 1) * 64], vv(h_), start=True, stop=True)))(h, bi),
                          reads=[sw_.key, v_.key], writes=[("ps", bi, h % 2)])
                    S.add("pe", (lambda h_: (lambda e: e.matmul(pst[B_DEN][0:64, h_:h_ + 1], sw_.ap[0:64, h_ * 64:(h_ + 1) * 64], ones_b.ap[0:64, 0:1], start=True, stop=True)))(h),
                          reads=[sw_.key, ones_b.key], writes=[("ps", B_DEN, h)])
                for h in range(4):
                    bi = B_N0 + h // 2
                    for kc in range(2):
                        S.add("pe", (lambda h_, kc_, bi_: (lambda e: e.matmul(pst[bi_][0:64, (h_ % 2) * 256:(h_ % 2 + 1) * 256], qv(2 * h_ + kc_), Cbf.ap[:, (h_ * 2 + kc_) * 256:(h_ * 2 + kc_ + 1) * 256], start=(kc_ == 0), stop=(kc_ == 1))))(h, kc, bi),
                              reads=[qk_.key, (Cbf.key, h)], writes=[("ps", bi, h % 2)])
                    for kc in range(2):
                        S.add("pe", (lambda h_, kc_: (lambda e: e.matmul(pst[B_DEN][0:64, 4 + h_:5 + h_], qv(2 * h_ + kc_), nbf.ap[:, h_ * 2 + kc_:h_ * 2 + kc_ + 1], start=(kc_ == 0), stop=(kc_ == 1))))(h, kc),
                              reads=[qk_.key, (nbf.key, h)], writes=[("ps", B_DEN, 4 + h)])
                for bi in (B_I0, B_I1):
                    S.add("act", (lambda bi_: (lambda e: e.activation(intra.ap[0:64, (bi_ - B_I0) * 512:(bi_ - B_I0 + 1) * 512], pst[bi_][0:64, :], AF.Copy)))(bi), reads=[("ps", bi)], writes=[(intra.key, bi)])
                for h in range(4):
                    bi = B_N0 + h // 2
                    S.add("dve", (lambda h_, bi_: (lambda e: e.scalar_tensor_tensor(num.ap[0:64, h_ * 256:(h_ + 1) * 256], pst[bi_][0:64, (h_ % 2) * 256:(h_ % 2 + 1) * 256], v3(tdec)[:, c, h_:h_ + 1], intra.ap[0:64, h_ * 256:(h_ + 1) * 256], ALU.mult, ALU.add)))(h, bi),
                          reads=[("ps", bi, h % 2), tdec.key, intra.key], writes=[(num.key, h)])
                S.add("dve", lambda e: e.tensor_copy(dn.ap[0:64, 0:8], pst[B_DEN][0:64, 0:8]), reads=[("ps", B_DEN)], writes=[dn.key])
                S.add("dve", lambda e: e.tensor_tensor(dn.ap[0:64, 4:8], dn.ap[0:64, 4:8], v3(tdec)[:, c, :], ALU.mult), reads=[dn.key, tdec.key], writes=[dn.key])
                S.add("dve", lambda e: e.tensor_tensor(dn.ap[0:64, 0:4], dn.ap[0:64, 0:4], dn.ap[0:64, 4:8], ALU.add), reads=[dn.key], writes=[dn.key])
                S.add("dve", lambda e: e.tensor_scalar(dn.ap[0:64, 4:8], dn.ap[0:64, 0:4], -1.0, None, ALU.mult), reads=[dn.key], writes=[dn.key])
                S.add("dve", lambda e: e.tensor_tensor(dn.ap[0:64, 0:4], dn.ap[0:64, 0:4], dn.ap[0:64, 4:8], ALU.max), reads=[dn.key], writes=[dn.key])
                S.add("dve", lambda e: e.tensor_tensor(dn.ap[0:64, 0:4], dn.ap[0:64, 0:4], v3(tfl)[:, c, :], ALU.max), reads=[dn.key, tfl.key], writes=[dn.key])
                S.add("dve", lambda e: e.reciprocal(rr.ap[0:64, 0:4], dn.ap[0:64, 0:4]), reads=[dn.key], writes=[rr.key])
                for h in range(4):
                    S.add("dve", (lambda h_: (lambda e: e.scalar_tensor_tensor(hg.ap[0:64, h_ * 256:(h_ + 1) * 256], num.ap[0:64, h_ * 256:(h_ + 1) * 256], rr.ap[0:64, h_:h_ + 1], o_.ap[0:64, j * 1024 + h_ * 256: j * 1024 + (h_ + 1) * 256], ALU.mult, ALU.mult)))(h),
                          reads=[(num.key, h), rr.key, o_.key], writes=[(hg.key, h)])
                    S.add("act", (lambda h_: (lambda e: e.activation(sq2.ap[0:64, :], hg.ap[0:64, h_ * 256:(h_ + 1) * 256], AF.Square, accum_out=ssq.ap[0:64, h_:h_ + 1])))(h),
                          reads=[(hg.key, h)], writes=[sq2.key, (ssq.key, h)])
                S.add("act", lambda e: e.activation(ssq.ap[0:64, 0:4], ssq.ap[0:64, 0:4], AF.Sqrt, bias=pcc(PC_ZERO + 1, p=64), scale=1.0 / 256), reads=[ssq.key, pc.key], writes=[ssq.key])
                S.add("dve", lambda e: e.reciprocal(ssq.ap[0:64, 0:4], ssq.ap[0:64, 0:4]), reads=[ssq.key], writes=[ssq.key])
                ho = hmo[c % 2]
                for h in range(4):
                    S.add("dve", (lambda h_: (lambda e: e.scalar_tensor_tensor(ho.ap[0:64, h_ * 256:(h_ + 1) * 256], hg.ap[0:64, h_ * 256:(h_ + 1) * 256], ssq.ap[0:64, h_:h_ + 1], mlg.ap[0:64, h_ * 256:(h_ + 1) * 256], ALU.mult, ALU.mult)))(h),
                          reads=[(hg.key, h), ssq.key, mlg.key], writes=[(ho.key, h)])
                tg0 = c * 64
                S.add("pool", lambda e: e.dma_start(out=HM[tg0:tg0 + 64, :], in_=ho.ap[0:64, :]), reads=[ho.key], writes=[("HM", c)], dma=True)
            if "s" in _bs:
                return
            pb = psbf(pst[B_KT])
            for h in range(4):
                for kc in range(2):
                    if "y" in _bs:
                        continue
                    S.add("pe", (lambda h_, kc_: (lambda e: e.transpose(pb[0:64, (2 * h_ + kc_) * 128:(2 * h_ + kc_ + 1) * 128], qv(8 + 2 * h_ + kc_), ident_b.ap)))(h, kc),
                          reads=[qk_.key, ident_b.key], writes=[("ps", B_KT, h)])
                if "x" in _bs:
                    continue
                S.add("dve", (lambda h_: (lambda e: e.tensor_scalar(kw_.ap[0:64, h_ * 256:(h_ + 1) * 256], pb[0:64, h_ * 256:(h_ + 1) * 256], wk_.ap[0:64, h_:h_ + 1], None, ALU.mult)))(h),
                      reads=[("ps", B_KT, h), wk_.key], writes=[(kw_.key, h)])
            if "t" in _bs:
                return
            for h in range(4):
                for kc in range(2):
                    S.add("pe", (lambda h_, kc_: (lambda e: e.matmul(pst[B_DC][:, kc_ * 256:(kc_ + 1) * 256], kw_.ap[0:64, h_ * 256 + kc_ * 128: h_ * 256 + (kc_ + 1) * 128], vv(h_), start=True, stop=True)))(h, kc),
                          reads=[(kw_.key, h), v_.key], writes=[("ps", B_DC)])
                    S.add("pe", (lambda h_, kc_: (lambda e: e.matmul(pst[B_DEN][:, 16 + 2 * h_ + kc_: 17 + 2 * h_ + kc_], kw_.ap[0:64, h_ * 256 + kc_ * 128: h_ * 256 + (kc_ + 1) * 128], ones_b.ap[0:64, 0:1], start=True, stop=True)))(h, kc),
                          reads=[(kw_.key, h), ones_b.key], writes=[("ps", B_DEN, "n", h)])
                cd = cdec.ap[:, c * 4 + h: c * 4 + h + 1]
                if "m" in _bs:
                    continue
                S.add("dve", (lambda h_, cd_: (lambda e: e.scalar_tensor_tensor(Cst.ap[:, h_ * 512:(h_ + 1) * 512], Cst.ap[:, h_ * 512:(h_ + 1) * 512], cd_, pst[B_DC][:, 0:512], ALU.mult, ALU.add)))(h, cd),
                      reads=[(Cst.key, h), cdec.key, ("ps", B_DC)], writes=[(Cst.key, h)])
                S.add("dve", (lambda h_, cd_: (lambda e: e.scalar_tensor_tensor(nst.ap[:, 2 * h_:2 * h_ + 2], nst.ap[:, 2 * h_:2 * h_ + 2], cd_, pst[B_DEN][:, 16 + 2 * h_:18 + 2 * h_], ALU.mult, ALU.add)))(h, cd),
                      reads=[(nst.key, h), cdec.key, ("ps", B_DEN, "n", h)], writes=[(nst.key, h)])
                S.add("pool", (lambda h_: (lambda e: e.tensor_copy(Cbf.ap[:, h_ * 512:(h_ + 1) * 512], Cst.ap[:, h_ * 512:(h_ + 1) * 512])))(h), reads=[(Cst.key, h)], writes=[(Cbf.key, h)])
                S.add("pool", (lambda h_: (lambda e: e.tensor_copy(nbf.ap[:, 2 * h_:2 * h_ + 2], nst.ap[:, 2 * h_:2 * h_ + 2])))(h), reads=[(nst.key, h)], writes=[(nbf.key, h)])

        def state_out(dC, dn_):
            S.add("sp", lambda e: e.dma_start(out=dC.rearrange("h (k p) v -> p h k v", p=128), in_=Cst.ap.rearrange("p (h k v) -> p h k v", h=4, k=2)), reads=[Cst.key], dma=True)
            S.add("sp", lambda e: e.dma_start(out=dn_.rearrange("h (k p) -> p h k", p=128), in_=nst.ap.rearrange("p (h k) -> p h k", h=4)), reads=[nst.key], dma=True)

        with nc.allow_non_contiguous_dma(reason="small state layouts"):
            for sc in range(2 * NT):
                qk_ = qkt[sc % 2]; v_ = vt[sc % 2]; o_ = ost[sc % 2]
                t0 = sc * 512
                S.add("sp", (lambda qk__, t0_: (lambda e: e.dma_start(out=qk__.ap.rearrange("p (c t) -> p c t", c=16), in_=QKT[:, :, t0_:t0_ + 512].rearrange("c p t -> p c t"))))(qk_, t0), writes=[qk_.key], dma=True)
                S.add("sp", (lambda v__, t0_: (lambda e: e.dma_start(out=v__.ap[0:64, :].rearrange("p (c f) -> p c f", c=8), in_=VM[t0_:t0_ + 512, :].rearrange("(c s) f -> s c f", s=64))))(v_, t0), writes=[v_.key], dma=True)
                isout = sc >= NT
                if isout:
                    S.add("sp", (lambda o__, t0_: (lambda e: e.dma_start(out=o__.ap[0:64, :].rearrange("p (c f) -> p c f", c=8), in_=OSG[t0_:t0_ + 512, :].rearrange("(c s) f -> s c f", s=64))))(o_, t0), writes=[o_.key], dma=True)
                if sc == NT:
                    S.add("dve", lambda e: e.tensor_scalar(Cst.ap, Cst.ap, pcc(PC_FLAG), None, ALU.mult), reads=[Cst.key, pc.key], writes=[Cst.key])
                    S.add("dve", lambda e: e.tensor_scalar(nst.ap, nst.ap, pcc(PC_FLAG), None, ALU.mult), reads=[nst.key, pc.key], writes=[nst.key])
                    S.add("pool", lambda e: e.tensor_copy(Cbf.ap, Cst.ap), reads=[Cst.key], writes=[Cbf.key])
                    S.add("pool", lambda e: e.tensor_copy(nbf.ap, nst.ap), reads=[nst.key], writes=[nbf.key])
                for j in range(8):
                    chunk(sc * 8 + j, sc, j, isout, qk_, v_, o_)
            state_out(oC_main, on_main)
            qk_ = qkt[0]; v_ = vt[0]; o_ = ost[0]
            nS = NSEQ * LS
            S.add("sp", lambda e: e.dma_start(out=qk_.ap[:, 0:16 * nS].rearrange("p (c t) -> p c t", c=16), in_=QKT[:, :, TP:TP + nS].rearrange("c p t -> p c t")), writes=[qk_.key], dma=True)
            S.add("sp", lambda e: e.dma_start(out=v_.ap[0:64, 0:NSEQ * 1024].rearrange("p (c f) -> p c f", c=NSEQ), in_=VM[TP:TP + nS, :].rearrange("(c s) f -> s c f", s=64)), writes=[v_.key], dma=True)
            S.add("sp", lambda e: e.dma_start(out=o_.ap[0:64, 0:NSEQ * 1024].rearrange("p (c f) -> p c f", c=NSEQ), in_=OSG[TP:TP + nS, :].rearrange("(c s) f -> s c f", s=64)), writes=[o_.key], dma=True)
            for s in range(NSEQ):
                S.add("sp", (lambda s_: (lambda e: e.dma_start(out=Cst.ap.rearrange("p (h k v) -> p h k v", h=4, k=2), in_=st_C[s_].rearrange("h (k p) v -> p h k v", p=128))))(s), writes=[Cst.key], dma=True)
                S.add("sp", (lambda s_: (lambda e: e.dma_start(out=nst.ap.rearrange("p (h k) -> p h k", h=4), in_=st_n[s_].rearrange("h (k p) -> p h k", p=128))))(s), writes=[nst.key], dma=True)
                S.add("pool", lambda e: e.tensor_copy(Cbf.ap, Cst.ap), reads=[Cst.key], writes=[Cbf.key])
                S.add("pool", lambda e: e.tensor_copy(nbf.ap, nst.ap), reads=[nst.key], writes=[nbf.key])
                chunk(NCp + s, 0, s, True, qk_, v_, o_)
                state_out(oC_smp[s], on_smp[s])
        S.barrier()
        if _stop == "B":
            S.emit()
            return nc

        off[0] = base_off
        sbm = tile(4 * 512, BF16, "sbm")
        S.add("pool", lambda e: e.dma_start(out=sbm.ap, in_=sbmask_d), writes=[sbm.key], dma=True)
        NKBp = TP // 128
        KTt = [tile(max(TP, KCACHE + 64), BF16, "KTt") for _ in range(2)]
        Vt = [tile(max(NKBp, PAST // 128 + 1) * 128, BF16, "Vt") for _ in range(2)]
        QTt = [tile(512, BF16, "QTt") for _ in range(2)]
        tEc = [tile(512, F32, "tEc") for _ in range(2)]; tsp = [tile(512, F32, "tsp") for _ in range(2)]
        spb = [tile(512, BF16, "spb") for _ in range(2)]; t1 = [tile(512, F32, "t1") for _ in range(2)]
        t3 = [tile(512, F32, "t3") for _ in range(2)]; ab = [tile(512, BF16, "ab") for _ in range(2)]
        Rb = tile(512, F32, "Rb"); hso = [tile(512, BF16, "hso") for _ in range(2)]
        bi_ = [0]
        jobn = [0]
        ps6 = [0]

        def nps6():
            i = ps6[0] % 6
            ps6[0] += 1
            return pst[i], ("ps", i)

        def sb_job(KT, V, QT, N, blocks, dst):
            S.add("pool", lambda e: e.memset(Rb.ap[:, 0:N], 0.0), writes=[Rb.key])
            po, pok = pst[6 + jobn[0] % 2], ("ps", 6 + jobn[0] % 2)
            nblk = len(blocks)
            for bi, (k0, ns, vb, mj, bcol) in enumerate(blocks):
                i = bi_[0] % 2
                bi_[0] += 1
                E, sp, sb, a1, a3, aa = tEc[i], tsp[i], spb[i], t1[i], t3[i], ab[i]
                pz, pzk = nps6(); pcu, pck = nps6(); pr, prk = nps6()
                PS_ = slice(0, ns)
                bias = pcc(bcol, p=ns)
                S.add("pe", lambda e: e.matmul(pz[PS_, 0:N], KT.ap[:, k0:k0 + ns], QT.ap[:, 0:N], start=True, stop=True), reads=[KT.key, QT.key], writes=[pzk])
                S.add("act", lambda e: e.activation(E.ap[PS_, 0:N], pz[PS_, 0:N], AF.Exp, bias=bias), reads=[pzk, pc.key], writes=[E.key])
                S.add("act", lambda e: e.activation(sp.ap[PS_, 0:N], E.ap[PS_, 0:N], AF.Ln, bias=pcc(PC_ZERO + 2, p=ns)), reads=[E.key, pc.key], writes=[sp.key])
                if mj is None:
                    S.add("pool", lambda e: e.tensor_copy(sb.ap[PS_, 0:N], sp.ap[PS_, 0:N]), reads=[sp.key], writes=[sb.key])
                else:
                    S.add("pool", lambda e: e.tensor_tensor(sb.ap[PS_, 0:N], sp.ap[PS_, 0:N], sbm.ap[PS_, mj * 512: mj * 512 + N], ALU.mult), reads=[sp.key, sbm.key], writes=[sb.key])
                S.add("pe", lambda e: e.matmul(pcu[PS_, 0:N], ucm.ap[PS_, 0:ns], sb.ap[PS_, 0:N], start=True, stop=True), reads=[ucm.key, sb.key], writes=[pck])
                S.add("pe", lambda e: e.matmul(pr[:, 0:N], ones_b.ap[PS_, 0:128], sb.ap[PS_, 0:N], start=True, stop=True), reads=[ones_b.key, sb.key], writes=[prk])
                S.add("dve", lambda e: e.scalar_tensor_tensor(a1.ap[PS_, 0:N], pz[PS_, 0:N], bias, sp.ap[PS_, 0:N], ALU.add, ALU.subtract), reads=[pzk, pc.key, sp.key], writes=[a1.key])
                S.add("dve", lambda e: e.tensor_tensor(a1.ap[PS_, 0:N], a1.ap[PS_, 0:N], pcu[PS_, 0:N], ALU.subtract), reads=[a1.key, pck], writes=[a1.key])
                S.add("pool", lambda e: e.tensor_tensor(a3.ap[PS_, 0:N], a1.ap[PS_, 0:N], Rb.ap[PS_, 0:N], ALU.subtract), reads=[a1.key, Rb.key], writes=[a3.key])
                S.add("act", lambda e: e.activation(aa.ap[PS_, 0:N], a3.ap[PS_, 0:N], AF.Exp), reads=[a3.key], writes=[aa.key])
                if mj is not None:
                    S.add("pool", lambda e: e.tensor_tensor(aa.ap[PS_, 0:N], aa.ap[PS_, 0:N], sbm.ap[PS_, mj * 512: mj * 512 + N], ALU.mult), reads=[aa.key, sbm.key], writes=[aa.key])
                S.add("dve", lambda e: e.tensor_tensor(Rb.ap[:, 0:N], Rb.ap[:, 0:N], pr[:, 0:N], ALU.add), reads=[Rb.key, prk], writes=[Rb.key])
                S.add("pe", lambda e: e.matmul(po[:, 0:N], V.ap[PS_, vb * 128:(vb + 1) * 128], aa.ap[PS_, 0:N], start=(bi == 0), stop=(bi == nblk - 1)), reads=[V.key, aa.key], writes=[pok])
            ho = hso[jobn[0] % 2]
            jobn[0] += 1
            evac_copy(jobn[0], ho.ap[:, 0:N], po[:, 0:N], [pok], [ho.key])
            S.add("pool", lambda e: e.dma_start(out=dst, in_=ho.ap[:, 0:N]), reads=[ho.key], dma=True)

        hj = 0
        for h in range(8):
            KT = KTt[hj % 2]; V = Vt[hj % 2]
            hj += 1
            for c0 in range(0, TP, 2048):
                c1 = min(TP, c0 + 2048)
                S.add("sp", (lambda KT_, c0_, c1_: (lambda e: e.dma_start(out=KT_.ap[:, c0_:c1_], in_=KTP[h, :, c0_:c1_])))(KT, c0, c1), writes=[KT.key], dma=True)
            for c0 in range(0, NKBp, 16):
                c1 = min(NKBp, c0 + 16)
                S.add("sp", (lambda V_, c0_, c1_: (lambda e: e.dma_start(out=V_.ap[:, c0_ * 128:c1_ * 128].rearrange("p (b d) -> p b d", d=128), in_=VP[c0_ * 128:c1_ * 128, h * 128:(h + 1) * 128].rearrange("(b p) d -> p b d", p=128))))(V, c0, c1),
                      writes=[V.key], dma=True)
            for qt in range(NT):
                QT = QTt[qt % 2]
                S.add("sp", (lambda QT_: (lambda e: e.dma_start(out=QT_.ap, in_=QTS[h, :, qt * 512:(qt + 1) * 512])))(QT), writes=[QT.key], dma=True)
                kb_diag0 = HALF // 128 + 4 * qt
                blocks = []
                for kb in range(kb_diag0 + 3, -1, -1):
                    mj = kb - kb_diag0 if kb >= kb_diag0 else None
                    blocks.append((kb * 128, 128, kb, mj, PC_NEGB if kb < HALF // 128 else PC_ZERO))
                sb_job(KT, V, QT, 512, blocks, HST[h, :, qt * 512:(qt + 1) * 512])
        NKBc = PAST // 128
        for s in range(NSEQ):
            for h in range(8):
                KT = KTt[hj % 2]; V = Vt[hj % 2]
                hj += 1
                S.add("sp", (lambda KT_: (lambda e: e.dma_start(out=KT_.ap[:, 0:KCACHE], in_=KTC[s, h, :, :])))(KT), writes=[KT.key], dma=True)
                for c0 in range(0, NKBc, 16):
                    c1 = min(NKBc, c0 + 16)
                    S.add("sp", (lambda V_, c0_, c1_: (lambda e: e.dma_start(out=V_.ap[:, c0_ * 128:c1_ * 128].rearrange("p (b d) -> p b d", d=128), in_=VC[s, c0_ * 128:c1_ * 128, h * 128:(h + 1) * 128].rearrange("(b p) d -> p b d", p=128))))(V, c0, c1),
                          writes=[V.key], dma=True)
                S.add("sp", (lambda V_: (lambda e: e.dma_start(out=V_.ap[0:64, NKBc * 128:(NKBc + 1) * 128], in_=VC[s, PAST:PAST + LS, h * 128:(h + 1) * 128])))(V), writes=[V.key], dma=True)
                QT = QTt[hj % 2]
                S.add("sp", (lambda QT_: (lambda e: e.dma_start(out=QT_.ap[:, 0:LS], in_=QTS[h, :, HALF + s * LS: HALF + (s + 1) * LS])))(QT), writes=[QT.key], dma=True)
                blocks = [(PAST, 64, NKBc, 0, PC_ZERO)] + [(kb * 128, 128, kb, None, PC_ZERO) for kb in range(NKBc - 1, -1, -1)]
                sb_job(KT, V, QT, LS, blocks, HST[h, :, HALF + s * LS: HALF + (s + 1) * LS])
        S.barrier()
        if _stop == "C":
            S.emit()
            return nc

        off[0] = base_off
        xt = tile(4 * D, F32, "xt"); hT = tile(16 * 512, BF16, "hT")
        xn = tile(D, BF16, "xn"); ss = tile(8, F32, "ss")
        mixT = tile(16 * 512, F32, "mixT")
        scr = T(); scr.ap = mixT.ap[:, 0:D]; scr.key = mixT.key
        hm_tm = T(); hm_tm.ap = mixT.ap[:, D:2 * D].bitcast(BF16); hm_tm.key = mixT.key
        hid = tile(max(NFC * 512, 32 * 512), BF16, "hid")
        _dstop = _os.environ.get("K_DSTOP", "")
        tmpf = [tile(512, F32, "tmpf") for _ in range(4)]
        sqt = [tile(512, F32, "sqt") for _ in range(2)]
        rstd = tile(512, F32, "rstd")
        pipe = Pipe(3)
        tcount = [0]

        def postnorm_residual(n, nb, gcol):
            pss, pssk = nps()
            for kc in range(KC):
                sq = sqt[kc % 2]
                S.add("act", (lambda kc_, sq_: (lambda e: e.activation(sq_.ap[:, 0:n], mixT.ap[:, kc_ * n:(kc_ + 1) * n], AF.Square)))(kc, sq), reads=[(mixT.key, kc)], writes=[sq.key])
                S.add("pe", (lambda kc_, sq_: (lambda e: e.matmul(pss[:, 0:n], ones_f.ap, sq_.ap[:, 0:n], start=(kc_ == 0), stop=(kc_ == KC - 1))))(kc, sq), reads=[ones_f.key, sq.key], writes=[pssk])
            S.add("act", lambda e: e.activation(rstd.ap[:, 0:n], pss[:, 0:n], AF.Sqrt, bias=pcc(PC_ZERO + 1), scale=1.0 / D), reads=[pssk, pc.key], writes=[rstd.key])
            S.add("dve", lambda e: e.reciprocal(rstd.ap[:, 0:n], rstd.ap[:, 0:n]), reads=[rstd.key], writes=[rstd.key])
            for kc in range(KC):
                S.add("dve", (lambda kc_: (lambda e: e.scalar_tensor_tensor(mixT.ap[:, kc_ * n:(kc_ + 1) * n], mixT.ap[:, kc_ * n:(kc_ + 1) * n], pcc(gcol + kc_), rstd.ap[:, 0:n], ALU.mult, ALU.mult)))(kc),
                      reads=[(mixT.key, kc), pc.key, rstd.key], writes=[(mixT.key, kc)])
            for b in range(nb):
                for g in range(4):
                    p, pk = nps()
                    for j in range(4):
                        kc = g * 4 + j
                        S.add("pe", (lambda j_, kc_, p_: (lambda e: e.transpose(p_[:, j_ * 128:(j_ + 1) * 128], mixT.ap[:, kc_ * n + b * 128: kc_ * n + (b + 1) * 128], ident_f.ap)))(j, kc, p),
                              reads=[(mixT.key, kc), ident_f.key], writes=[pk])
                    xs = xt.ap[:, b * D + g * 512: b * D + (g + 1) * 512]
                    S.add("dve", (lambda xs_, p_: (lambda e: e.tensor_tensor(xs_, xs_, p_[:, 0:512], ALU.add)))(xs, p), reads=[pk, (xt.key, b, g)], writes=[(xt.key, b, g)])

        def phaseD_tile(kind, ti):
            if kind == "main":
                xsrc, psrc, t0, n, tg0, ydst = x_main, p_main, ti * 512, 512, HALF + ti * 512, y_main
            else:
                xsrc, psrc, t0, n, tg0, ydst = x_smp, p_smp, 0, NSEQ * LS, TP, y_smp
            nb = n // 128
            tq0 = tg0 - HALF
            hmT = lambda kc: hid.ap[:, kc * n:(kc + 1) * n]
            hsT = lambda kc: hid.ap[:, 8 * n + kc * n: 8 * n + (kc + 1) * n]
            uT = lambda kc: hid.ap[:, 16 * n + kc * n: 16 * n + (kc + 1) * n]
            S.add("sp", lambda e: e.dma_start(out=xt.ap[:, 0:nb * D].rearrange("p (b d) -> p b d", b=nb), in_=xsrc[t0:t0 + n, :].rearrange("(b p) d -> p b d", p=128)), writes=[xt.key], dma=True)
            S.add("sp", lambda e: e.dma_start(out=hm_tm.ap[:, 0:nb * 1024].rearrange("p (b f) -> p b f", b=nb), in_=HM[tg0:tg0 + n, :].rearrange("(b p) f -> p b f", p=128)), writes=[hm_tm.key], dma=True)
            S.add("sp", lambda e: e.dma_start(out=hid.ap[:, 8 * n:16 * n].rearrange("p (h t) -> p h t", h=8), in_=HST[:, :, tq0:tq0 + n].rearrange("h p t -> p h t")), writes=[hid.key], dma=True)
            norm_to_hT(xt, nb, hT, PC_GMIX, scr, xn, ss)
            for kc in range(8):
                p, pk = nps()
                pb = psbf(p)
                for b in range(nb):
                    S.add("pe", (lambda b_, pb_: (lambda e: e.transpose(pb_[:, b_ * 128:(b_ + 1) * 128], hm_tm.ap[:, b_ * 1024 + kc * 128: b_ * 1024 + (kc + 1) * 128], ident_b.ap)))(b, pb),
                          reads=[hm_tm.key, ident_b.key], writes=[pk])
                evac_copy(kc, hmT(kc), pb[:, 0:n], [pk], [(hid.key, "hm", kc)])

            hk = lambda kc: hT.ap[:, kc * n:(kc + 1) * n]

            def merge_step(cg):
                def comp(vs, key):
                    wA, wB, wa, wb_ = vs
                    pA = nps(); pB = nps(); pa = nps(); pb2 = nps()
                    for kc in range(KC):
                        S.add("pe", (lambda kc_: (lambda e: e.matmul(pA[0][:, 0:n], wA[:, kc_, :], hk(kc_), start=(kc_ == 0), stop=(kc_ == KC - 1))))(kc), reads=[(key,), (hT.key, kc)], writes=[pA[1]])
                    for kc in range(KC):
                        S.add("pe", (lambda kc_: (lambda e: e.matmul(pB[0][:, 0:n], wB[:, kc_, :], hk(kc_), start=(kc_ == 0), stop=(kc_ == KC - 1))))(kc), reads=[(key,), (hT.key, kc)], writes=[pB[1]])
                    for kc in range(8):
                        S.add("pe", (lambda kc_: (lambda e: e.matmul(pa[0][:, 0:n], wa[:, kc_, :], hmT(kc_), start=(kc_ == 0), stop=(kc_ == 7))))(kc), reads=[(key,), (hid.key, "hm", kc)], writes=[pa[1]])
                    for kc in range(8):
                        S.add("pe", (lambda kc_: (lambda e: e.matmul(pb2[0][:, 0:n], wb_[:, kc_, :], hsT(kc_), start=(kc_ == 0), stop=(kc_ == 7))))(kc), reads=[(key,), (hid.key, "hs")], writes=[pb2[1]])
                    i = (cg % 2) * 2
                    sA = tmpf[i]; sB = tmpf[i + 1]
                    S.add("act", lambda e: e.activation(sA.ap[:, 0:n], pA[0][:, 0:n], AF.Sigmoid), reads=[pA[1]], writes=[sA.key])
                    S.add("act", lambda e: e.activation(sB.ap[:, 0:n], pB[0][:, 0:n], AF.Sigmoid), reads=[pB[1]], writes=[sB.key])
                    S.add("dve", lambda e: e.tensor_tensor(sA.ap[:, 0:n], sA.ap[:, 0:n], pa[0][:, 0:n], ALU.mult), reads=[sA.key, pa[1]], writes=[sA.key])
                    S.add("dve", lambda e: e.tensor_tensor(sB.ap[:, 0:n], sB.ap[:, 0:n], pb2[0][:, 0:n], ALU.mult), reads=[sB.key, pb2[1]], writes=[sB.key])
                    S.add("pool", lambda e: e.tensor_tensor(uT(cg), sA.ap[:, 0:n], sB.ap[:, 0:n], ALU.add), reads=[sA.key, sB.key], writes=[(hid.key, "u", cg)])
                return comp
            for cg in range(16):
                pipe.step([(0, (KC, 128), wsrc(wb_in, C_GA + cg * 128, 128)), (2048, (KC, 128), wsrc(wb_in, C_GB + cg * 128, 128)),
                           (4096, (8, 128), wsrc(wb_bra, cg * 128, 128, 8)), (5120, (8, 128), wsrc(wb_brb, cg * 128, 128, 8))], merge_step(cg))

            def proj_step(g, src_fn, src_key_fn, nk, evac_fn):
                def comp(vs, key):
                    for j in range(4):
                        cg = g * 4 + j
                        p = nps()
                        for kc in range(nk):
                            S.add("pe", (lambda kc_: (lambda e: e.matmul(p[0][:, 0:n], vs[0][:, kc_, j * 128:(j + 1) * 128], src_fn(kc_), start=(kc_ == 0), stop=(kc_ == nk - 1))))(kc),
                                  reads=[(key,), src_key_fn(kc)], writes=[p[1]])
                        evac_fn(cg, p)
                return comp

            def ev_mix(cg, p):
                evac_copy(cg, mixT.ap[:, cg * n:(cg + 1) * n], p[0][:, 0:n], [p[1]], [(mixT.key, cg)])
            for g in range(4):
                pipe.step([(0, (KC, 512), wsrc(wb_out, g * 512, 512))], proj_step(g, uT, lambda kc: (hid.key, "u", kc), KC, ev_mix))
            pipe.run()
            if _dstop == "1":
                return
            postnorm_residual(n, nb, PC_POMIX)
            if _dstop == "2":
                return

            norm_to_hT(xt, nb, hT, PC_GMLP, scr, xn, ss)
            for half in range(2):
                def ev_up(fc, p):
                    r = tmpf[fc % 4]
                    S.add("act", lambda e: e.activation(r.ap[:, 0:n], p[0][:, 0:n], AF.Relu), reads=[p[1]], writes=[r.key])
                    S.add("pool", lambda e: e.tensor_tensor(hid.ap[:, fc * n:(fc + 1) * n], r.ap[:, 0:n], r.ap[:, 0:n], ALU.mult), reads=[r.key], writes=[(hid.key, "f", fc)])
                for g in range(NFC // 4):
                    pipe.step([(0, (KC, 512), wsrc(wb_up, half * FH + g * 512, 512))], proj_step(g, hk, lambda kc: (hT.key, kc), KC, ev_up))

                def down_step(cgp, half_):
                    def comp(vs, key):
                        for j in range(2):
                            cg = cgp * 2 + j
                            p = nps()
                            for fc in range(NFC):
                                S.add("pe", (lambda fc_: (lambda e: e.matmul(p[0][:, 0:n], vs[0][:, fc_, j * 128:(j + 1) * 128], hid.ap[:, fc_ * n:(fc_ + 1) * n], start=(fc_ == 0), stop=(fc_ == NFC - 1))))(fc),
                                      reads=[(key,), (hid.key, "f", fc)], writes=[p[1]])
                            mv = mixT.ap[:, cg * n:(cg + 1) * n]
                            if half_ == 0:
                                evac_copy(cg, mv, p[0][:, 0:n], [p[1]], [(mixT.key, cg)])
                            else:
                                S.add("dve", (lambda mv_, p_: (lambda e: e.tensor_tensor(mv_, mv_, p_[0][:, 0:n], ALU.add)))(mv, p), reads=[p[1], (mixT.key, cg)], writes=[(mixT.key, cg)])
                    return comp
                nfl = NFC
                wcols = min(256, SLOT // nfl)
                assert wcols == 256
                for cgp in range(8):
                    src = wb_down[half * FH: half * FH + FH, cgp * 256:(cgp + 1) * 256].rearrange("(k p) c -> p k c", p=128)
                    pipe.step([(0, (NFC, 256), src)], down_step(cgp, half))
                pipe.run()
            postnorm_residual(n, nb, PC_POMLP)
            if _dstop == "3":
                return

            norm_to_hT(xt, nb, hT, PC_GPLE, scr, xn, ss)
            pf = scr
            S.add("sp", lambda e: e.dma_start(out=pf.ap[:, 0:nb * 256].rearrange("p (b f) -> p b f", b=nb), in_=psrc[t0:t0 + n, :].rearrange("(b p) f -> p b f", p=128)), writes=[pf.key], dma=True)
            S.add("dve", lambda e: e.tensor_copy(xn.ap[:, 0:nb * 256], pf.ap[:, 0:nb * 256]), reads=[pf.key], writes=[xn.key])
            pT = lambda kc: hid.ap[:, kc * n:(kc + 1) * n]
            for kc in range(2):
                p, pk = nps()
                pb = psbf(p)
                for b in range(nb):
                    S.add("pe", (lambda b_, pb_: (lambda e: e.transpose(pb_[:, b_ * 128:(b_ + 1) * 128], xn.ap[:, b_ * 256 + kc * 128: b_ * 256 + (kc + 1) * 128], ident_b.ap)))(b, pb),
                          reads=[xn.key, ident_b.key], writes=[pk])
                evac_copy(kc, pT(kc), pb[:, 0:n], [pk], [(hid.key, "f", kc)])

            if _dstop == "4":
                return

            def ple_step(cgp):
                def comp(vs, key):
                    wg, wp = vs
                    for j in range(2):
                        cg = cgp * 2 + j
                        pg = nps(); pp = nps()
                        for kc in range(KC):
                            S.add("pe", (lambda kc_: (lambda e: e.matmul(pg[0][:, 0:n], wg[:, kc_, j * 128:(j + 1) * 128], hk(kc_), start=(kc_ == 0), stop=(kc_ == KC - 1))))(kc), reads=[(key,), (hT.key, kc)], writes=[pg[1]])
                        for kc in range(2):
                            S.add("pe", (lambda kc_: (lambda e: e.matmul(pp[0][:, 0:n], wp[:, kc_, j * 128:(j + 1) * 128], pT(kc_), start=(kc_ == 0), stop=(kc_ == 1))))(kc), reads=[(key,), (hid.key, "f", kc)], writes=[pp[1]])
                        sg = tmpf[cg % 4]
                        S.add("act", lambda e: e.activation(sg.ap[:, 0:n], pg[0][:, 0:n], AF.Sigmoid), reads=[pg[1]], writes=[sg.key])
                        S.add("dve", lambda e: e.tensor_tensor(mixT.ap[:, cg * n:(cg + 1) * n], sg.ap[:, 0:n], pp[0][:, 0:n], ALU.mult), reads=[sg.key, pp[1]], writes=[(mixT.key, cg)])
                return comp
            for cgp in range(8):
                pipe.step([(0, (KC, 256), wsrc(wb_pg, cgp * 256, 256)), (4096, (2, 256), wsrc(wb_ple, cgp * 256, 256, 2))], ple_step(cgp))
            pipe.run()
            if _dstop == "5":
                return
            postnorm_residual(n, nb, PC_POPLE)
            if _dstop == "6":
                return
            S.add("pool", lambda e: e.dma_start(out=ydst[t0:t0 + n, :].rearrange("(b p) d -> p b d", p=128), in_=xt.ap[:, 0:nb * D].rearrange("p (b d) -> p b d", b=nb)), reads=[xt.key], writes=[("y", kind, ti)], dma=True)

        for ti in range(NT):
            phaseD_tile("main", ti)
        phaseD_tile("smp", 0)
        S.emit()
    return nc


def host_consts(inputs):
    pcs = np.zeros((128, NPC), np.float32)

    def fm(v):
        return np.ascontiguousarray(v.reshape(16, 128).T)
    pcs[:, PC_GMIX:PC_GMIX + 16] = fm(inputs["g_pre_mix"][0])
    pcs[:, PC_GMLP:PC_GMLP + 16] = fm(inputs["g_pre_mlp"][0])
    pcs[:, PC_GPLE:PC_GPLE + 16] = fm(inputs["g_pre_ple"][0])
    pcs[:, PC_POMIX:PC_POMIX + 16] = fm(inputs["g_post_mix"][0])
    pcs[:, PC_POMLP:PC_POMLP + 16] = fm(inputs["g_post_mlp"][0])
    pcs[:, PC_POPLE:PC_POPLE + 16] = fm(inputs["g_post_ple"][0])
    for j in range(4):
        pcs[:, PC_CW + j * 16: PC_CW + (j + 1) * 16] = fm(inputs["conv_w"][0, j])
    pcs[:, PC_CB:PC_CB + 16] = fm(inputs["conv_b"][0])
    pcs[:, PC_ZERO] = 0.0
    pcs[:, PC_ZERO + 1] = EPS
    pcs[:, PC_ZERO + 2] = 1.0
    c = dict(
        mlg=np.ascontiguousarray(inputs["ml_norm"][0:1]).astype(np.float32),
        bifrow=np.ascontiguousarray(inputs["b_if"][0:1]).astype(np.float32),
        ident=np.eye(128, dtype=np.float32),
        ones=np.ones((128, 128), np.float32),
    )
    jj, ss_ = np.meshgrid(np.arange(128), np.arange(128), indexing="ij")
    c["ucm"] = (jj > ss_).astype(np.float32)
    sbm = np.zeros((128, 4, 512), np.float32)
    s_idx = np.arange(128)[:, None]
    t_idx = np.arange(512)[None, :]
    for j in range(4):
        sbm[:, j, :] = ((128 * j + s_idx) < t_idx).astype(np.float32)
    c["sbmask"] = sbm.reshape(128, 2048)
    s64, t64 = np.meshgrid(np.arange(64), np.arange(64), indexing="ij")
    c["tri"] = (s64 <= t64).astype(np.float32)
    sel = np.zeros((64, 128), np.float32)
    sel[63, :] = 1.0
    c["sel63"] = sel
    c["i4"] = np.tile(np.eye(64, dtype=np.float32), (1, 4))
    c["negm"] = np.tile(np.where(t64.T >= s64.T, 0.0, 0.0), (1, 4)).astype(np.float32)
    tt, sss = np.meshgrid(np.arange(64), np.arange(64), indexing="ij")
    c["negm"] = np.tile(np.where(sss <= tt, 0.0, -1e30).astype(np.float32), (1, 4))
    c["mt4"] = np.tile((tt <= sss).astype(np.float32), (1, 4))
    return pcs, c


_NC_CACHE = {}


def kernel(**inputs):
    inputs = {k: np.asarray(v) for k, v in inputs.items()}
    SEQ = inputs["x_prompt"].shape[1]
    PAST = inputs["cache_sb_k"].shape[2]
    DFF = inputs["w_up"].shape[2]
    HALF = SEQ // 2
    cfg = dict(HALF=HALF, PAST=PAST, DFF=DFF)
    key = (HALF, PAST, DFF)
    if key not in _NC_CACHE:
        _NC_CACHE[key] = build(cfg)
    nc = _NC_CACHE[key]
    pcs, consts = host_consts(inputs)
    shared = dict(
        w_in=inputs["w_in"][0], w_bra=inputs["w_br_a"][0], w_brb=inputs["w_br_b"][0], w_out=inputs["w_out"][0],
        w_up=inputs["w_up"][0], w_down=inputs["w_down"][0], w_ple=inputs["w_ple"][0], w_pg=inputs["w_ple_gate"][0],
    )
    shared.update(consts)
    in_maps = []
    for c in range(8):
        seq, half = c // 2, c % 2
        s0 = 4 * c
        p = pcs.copy()
        p[:, PC_FLAG] = float(half)
        p[:, PC_NEGB] = 0.0 if half else NEG
        m = dict(shared)
        m.update(
            x_ctx=inputs["x_prompt"][seq, 0:HALF], x_main=inputs["x_prompt"][seq, half * HALF:(half + 1) * HALF],
            x_smp=inputs["x_sample"][s0:s0 + 4].reshape(NSEQ * LS, D),
            p_main=inputs["p_prompt"][0, seq, half * HALF:(half + 1) * HALF], p_smp=inputs["p_sample"][0, s0:s0 + 4].reshape(NSEQ * LS, 256),
            cache_k=inputs["cache_sb_k"][0, s0:s0 + 4].reshape(NSEQ, PAST, 1024), cache_v=inputs["cache_sb_v"][0, s0:s0 + 4].reshape(NSEQ, PAST, 1024),
            st_conv=inputs["state_conv"][0, s0:s0 + 4].reshape(NSEQ * 3, D), st_C=inputs["state_mlstm_C"][0, s0:s0 + 4],
            st_n=inputs["state_mlstm_n"][0, s0:s0 + 4], st_m=inputs["state_mlstm_m"][0, s0:s0 + 4].reshape(1, NSEQ * 4), pc=p,
        )
        in_maps.append({k: np.ascontiguousarray(v, dtype=np.float32) for k, v in m.items()})
    import os as _os
    ncores = int(_os.environ.get("K_CORES", "8"))
    res = run_bass_kernel_spmd(nc, in_maps[:ncores], core_ids=list(range(ncores)))
    R = list(res.results)
    while len(R) < 8:
        R.append({k: np.zeros_like(v) for k, v in R[0].items()})
    B = inputs["x_prompt"].shape[0]
    f = np.float32
    yp = np.stack([np.concatenate([R[2 * b]["y_main"], R[2 * b + 1]["y_main"]], 0) for b in range(B)]).astype(f)
    ys = np.concatenate([R[c]["y_smp"].reshape(4, LS, D) for c in range(8)], 0).astype(f)
    pk = np.stack([np.concatenate([R[2 * b]["k_main"], R[2 * b + 1]["k_main"]], 0) for b in range(B)]).reshape(1, B, SEQ, 8, 128).astype(f)
    pv = np.stack([np.concatenate([R[2 * b]["v_main"], R[2 * b + 1]["v_main"]], 0) for b in range(B)]).reshape(1, B, SEQ, 8, 128).astype(f)
    pconv = np.stack([R[2 * b + 1]["conv_main"] for b in range(B)])[None].astype(f)
    pC = np.stack([R[2 * b + 1]["C_main"] for b in range(B)])[None].astype(f)
    pn = np.stack([R[2 * b + 1]["n_main"] for b in range(B)])[None].astype(f)
    pm = np.stack([R[2 * b + 1]["m_main"].reshape(4) for b in range(B)])[None].astype(f)
    sk = np.concatenate([R[c]["k_smp"].reshape(4, LS, 8, 128) for c in range(8)], 0)[None].astype(f)
    sv = np.concatenate([R[c]["v_smp"].reshape(4, LS, 8, 128) for c in range(8)], 0)[None].astype(f)
    sconv = np.concatenate([R[c]["conv_smp"].reshape(4, 3, D) for c in range(8)], 0)[None].astype(f)
    sC = np.concatenate([R[c]["C_smp"] for c in range(8)], 0)[None].astype(f)
    sn = np.concatenate([R[c]["n_smp"] for c in range(8)], 0)[None].astype(f)
    sm = np.concatenate([R[c]["m_smp"].reshape(4, 4) for c in range(8)], 0)[None].astype(f)
    return (yp, ys, pk, pv, pconv, pC, pn, pm, sk, sv, sconv, sC, sn, sm)
```
